# Optimizing a Trainium2 kernel written in Bass

```python
import jax, jax.numpy as jnp
from jax import lax
import numpy as np

D_MODEL = 1024
BATCH = 8
SEQ = 4096
DEPTH = 1
DEC_BATCH = 128
DEC_SEQ = 8
PAST_LEN = 16384
PAGE_SIZE = 128

HEAD_DIM = 64
SWA_HEADS = 8
SWA_KV_HEADS = 2
SWA_GROUP = SWA_HEADS // SWA_KV_HEADS
WINDOW = 128
RET_HEADS = 8
RET_DIM = 64
RET_CHUNK = 128
RET_THETA = 10000.0
MEM_LEN = 256
MEM_HEADS = 4
MEM_HEAD_DIM = 128
MEM_W = MEM_HEADS * MEM_HEAD_DIM
FFN_HIDDEN = ((8 * D_MODEL + 3 * 256 - 1) // (3 * 256)) * 256
ROPE_THETA = 10000.0
RMS_EPS = 1e-6
NEG_INF = -1e30
SWA_Q_W = SWA_HEADS * HEAD_DIM
SWA_KV_W = SWA_KV_HEADS * HEAD_DIM
RET_W = RET_HEADS * RET_DIM
IN_COLS = SWA_Q_W + 2 * SWA_KV_W + 4 * RET_W
MIX_WIDTH = SWA_Q_W + RET_W

kernel_name = 'hymba_swa_sink_retention_memxattn_step'


def rms_norm(x, g=None):
    xf = x.astype(jnp.float32)
    y = xf * lax.rsqrt(jnp.mean(xf * xf, axis=-1, keepdims=True) + RMS_EPS)
    if g is not None:
        y = y * g.astype(jnp.float32)
    return y.astype(x.dtype)


def rotary(x, pos):
    half = x.shape[-1] // 2
    inv = 1.0 / (ROPE_THETA ** (jnp.arange(half, dtype=jnp.float32) / half))
    ang = pos.astype(jnp.float32)[:, None] * inv[None, :]
    cos = jnp.cos(ang)[:, None, :]
    sin = jnp.sin(ang)[:, None, :]
    xf = x.astype(jnp.float32)
    x1, x2 = xf[..., :half], xf[..., half:]
    return jnp.concatenate([x1 * cos - x2 * sin, x2 * cos + x1 * sin], axis=-1).astype(x.dtype)


def retention_rotate(x, pos):
    half = x.shape[-1] // 2
    inv = RET_THETA ** (-jnp.linspace(0.0, 1.0, half, dtype=jnp.float32))
    ang = pos.astype(jnp.float32)[:, None] * inv[None, :]
    cos = jnp.cos(ang)[:, None, :]
    sin = jnp.sin(ang)[:, None, :]
    xf = x.astype(jnp.float32)
    xe, xo = xf[..., 0::2], xf[..., 1::2]
    out = jnp.stack([xe * cos - xo * sin, xo * cos + xe * sin], axis=-1)
    return out.reshape(x.shape).astype(x.dtype)


def window_mask(q_pos, k_pos):
    d = q_pos - k_pos
    return (k_pos >= 0) & (d >= 0) & (d < WINDOW)


def sink_attention(q, k, v, sinks, mask):
    s = jnp.einsum('...qhgd,...khd->...hgqk', q, k).astype(jnp.float32) * (HEAD_DIM ** -0.5)
    s = jnp.where(mask[..., None, None, :, :], s, NEG_INF)
    sink = sinks.astype(jnp.float32).reshape(SWA_KV_HEADS, SWA_GROUP, 1, 1)
    m = jnp.maximum(jnp.max(s, axis=-1, keepdims=True), sink)
    p = jnp.exp(s - m)
    denom = jnp.sum(p, axis=-1, keepdims=True) + jnp.exp(sink - m)
    w = (p / denom).astype(v.dtype)
    return jnp.einsum('...hgqk,...khd->...qhgd', w, v)


def swa_prompt(q, k, v, sinks):
    B, L, H, D = q.shape
    nb = L // WINDOW
    qb = q.reshape(B, nb, WINDOW, SWA_KV_HEADS, SWA_GROUP, D)
    kb = k.reshape(B, nb, WINDOW, SWA_KV_HEADS, D)
    vb = v.reshape(B, nb, WINDOW, SWA_KV_HEADS, D)
    prev = lambda t: jnp.concatenate([jnp.zeros_like(t[:, :1]), t[:, :-1]], axis=1)
    kk = jnp.concatenate([prev(kb), kb], axis=2)
    vv = jnp.concatenate([prev(vb), vb], axis=2)
    start = jnp.arange(nb, dtype=jnp.int32)[:, None] * WINDOW
    q_pos = start + jnp.arange(WINDOW, dtype=jnp.int32)[None, :]
    k_pos = start - WINDOW + jnp.arange(2 * WINDOW, dtype=jnp.int32)[None, :]
    mask = window_mask(q_pos[:, :, None], k_pos[:, None, :])
    o = sink_attention(qb, kk, vv, sinks, mask)
    return o.reshape(B, L, H, D), k[:, -WINDOW:], v[:, -WINDOW:]


def swa_sample(q, k, v, sinks, buf_k, buf_v):
    B, T, H, D = q.shape
    nbuf = buf_k.shape[1]
    kk = jnp.concatenate([buf_k.astype(k.dtype), k], axis=1)
    vv = jnp.concatenate([buf_v.astype(v.dtype), v], axis=1)
    q_pos = PAST_LEN + jnp.arange(T, dtype=jnp.int32)
    k_pos = PAST_LEN - nbuf + jnp.arange(nbuf + T, dtype=jnp.int32)
    mask = window_mask(q_pos[:, None], k_pos[None, :])
    o = sink_attention(q.reshape(B, T, SWA_KV_HEADS, SWA_GROUP, D), kk, vv, sinks, mask)
    return o.reshape(B, T, H, D), kk[:, -nbuf:], vv[:, -nbuf:]


def retention(q, k, v, s0):
    B, L, H, D = q.shape
    Dv = v.shape[-1]
    c = RET_CHUNK if L % RET_CHUNK == 0 else L
    n = L // c
    log_g = jnp.log(1.0 - jnp.exp2(-5.0 - jnp.arange(H, dtype=jnp.float32)))
    idx = jnp.arange(c, dtype=jnp.float32)
    diff = idx[:, None] - idx[None, :]
    dmat = jnp.where(diff >= 0, jnp.exp(jnp.maximum(diff, 0.0)[None] * log_g[:, None, None]), 0.0)
    q_decay = jnp.exp((idx + 1.0)[None, :] * log_g[:, None]).T[None, :, :, None]
    k_decay = jnp.exp((c - 1.0 - idx)[None, :] * log_g[:, None])
    chunk_decay = jnp.exp(c * log_g)[None, :, None, None]
    to_chunks = lambda t: t.astype(jnp.float32).reshape(B, n, c, H, t.shape[-1]).swapaxes(0, 1)

    def step(s, inp):
        qi, ki, vi = inp
        inner = jnp.einsum('bqhd,bkhd->bhqk', qi, ki) * dmat[None]
        o = jnp.einsum('bhqk,bkhe->bqhe', inner, vi) + jnp.einsum('bqhd,bhde->bqhe', qi, s) * q_decay
        s_new = s * chunk_decay + jnp.einsum('bkhd,bkhe,hk->bhde', ki, vi, k_decay)
        return s_new, o

    s, o = lax.scan(step, s0.astype(jnp.float32), (to_chunks(q), to_chunks(k), to_chunks(v)))
    return o.swapaxes(0, 1).reshape(B, L, H, Dv), s


def parallel_mixer(h, pos, swa_fn, s0, w_in, q_norm_a, k_norm_a, sinks, w_out):
    B, L, _ = h.shape
    proj = h @ w_in
    cuts = [SWA_Q_W, SWA_Q_W + SWA_KV_W, SWA_Q_W + 2 * SWA_KV_W]
    cuts = cuts + [cuts[-1] + RET_W, cuts[-1] + 2 * RET_W, cuts[-1] + 3 * RET_W]
    qa, ka, va, qr, kr, vr, g = jnp.split(proj, cuts, axis=-1)
    qa = rotary(rms_norm(qa.reshape(B, L, SWA_HEADS, HEAD_DIM), q_norm_a), pos)
    ka = rotary(rms_norm(ka.reshape(B, L, SWA_KV_HEADS, HEAD_DIM), k_norm_a), pos)
    va = va.reshape(B, L, SWA_KV_HEADS, HEAD_DIM)
    o_a, k_new, v_new = swa_fn(qa, ka, va, sinks)
    qr = retention_rotate(qr.reshape(B, L, RET_HEADS, RET_DIM), pos)
    kr = retention_rotate(kr.reshape(B, L, RET_HEADS, RET_DIM), pos) * (RET_DIM ** -0.5)
    vr = vr.reshape(B, L, RET_HEADS, RET_DIM)
    o_r, s_new = retention(qr, kr, vr, s0)
    o_r = rms_norm(o_r).astype(h.dtype).reshape(B, L, RET_W) * jax.nn.silu(g)
    y = jnp.concatenate([o_a.reshape(B, L, SWA_Q_W), o_r], axis=-1) @ w_out
    return y, k_new, v_new, s_new


def memory_kv(mem, norm_mem, w_mkv, k_norm_m):
    B, M, _ = mem.shape
    kv = rms_norm(mem, norm_mem) @ w_mkv
    k, v = jnp.split(kv, 2, axis=-1)
    k = rms_norm(k.reshape(B, M, MEM_HEADS, MEM_HEAD_DIM), k_norm_m)
    return k, v.reshape(B, M, MEM_HEADS, MEM_HEAD_DIM)


def memory_cross(h, mem_k, mem_v, w_mq, q_norm_m, w_mo):
    B, L, _ = h.shape
    q = rms_norm((h @ w_mq).reshape(B, L, MEM_HEADS, MEM_HEAD_DIM), q_norm_m)
    s = jnp.einsum('blhd,bmhd->bhlm', q, mem_k.astype(q.dtype)).astype(jnp.float32) * (MEM_HEAD_DIM ** -0.5)
    p = jax.nn.softmax(s, axis=-1).astype(h.dtype)
    o = jnp.einsum('bhlm,bmhd->blhd', p, mem_v.astype(h.dtype)).reshape(B, L, MEM_W)
    return o @ w_mo


def swiglu(h, w_gu, w_down):
    g, u = jnp.split(h @ w_gu, 2, axis=-1)
    return (jax.nn.silu(g) * u) @ w_down


def decoder_layer(x, pos, swa_fn, s0, mem_k, mem_v, norm_mix, w_in, q_norm_a, k_norm_a, sinks, w_out,
                  norm_cross, w_mq, q_norm_m, w_mo, norm_ffn, w_gu, w_down):
    y, k_new, v_new, s_new = parallel_mixer(rms_norm(x, norm_mix), pos, swa_fn, s0,
                                            w_in, q_norm_a, k_norm_a, sinks, w_out)
    x = x + y
    x = x + memory_cross(rms_norm(x, norm_cross), mem_k, mem_v, w_mq, q_norm_m, w_mo)
    x = x + swiglu(rms_norm(x, norm_ffn), w_gu, w_down)
    return x, k_new, v_new, s_new


def setup_inputs(seed: int = 0) -> dict:
    key = jax.random.key(seed)
    ks = jax.random.split(key, 24)
    f32 = jnp.float32
    nrm = lambda k, shape, scale: jax.random.normal(k, shape, f32) * scale
    gain = lambda k, n: 1.0 + 0.02 * jax.random.normal(k, (DEPTH, n), f32)
    buf = min(WINDOW, PAST_LEN)
    return {
        'x_prompt': nrm(ks[0], (BATCH, SEQ, D_MODEL), 1.0),
        'x_sample': nrm(ks[1], (DEC_BATCH, DEC_SEQ, D_MODEL), 1.0),
        'mem_prompt': nrm(ks[2], (BATCH, MEM_LEN, D_MODEL), 1.0),
        'cache_swa_k': nrm(ks[3], (DEPTH, DEC_BATCH, buf, SWA_KV_HEADS, HEAD_DIM), 1.0),
        'cache_swa_v': nrm(ks[4], (DEPTH, DEC_BATCH, buf, SWA_KV_HEADS, HEAD_DIM), 1.0),
        'state_ret': nrm(ks[5], (DEPTH, DEC_BATCH, RET_HEADS, RET_DIM, RET_DIM), 0.5),
        'cache_mem_k': nrm(ks[6], (DEPTH, DEC_BATCH, MEM_LEN, MEM_HEADS, MEM_HEAD_DIM), 1.0),
        'cache_mem_v': nrm(ks[7], (DEPTH, DEC_BATCH, MEM_LEN, MEM_HEADS, MEM_HEAD_DIM), 1.0),
        'norm_mix': gain(ks[8], D_MODEL),
        'w_in': nrm(ks[9], (DEPTH, D_MODEL, IN_COLS), D_MODEL ** -0.5),
        'q_norm_a': gain(ks[10], HEAD_DIM),
        'k_norm_a': gain(ks[11], HEAD_DIM),
        'sinks': nrm(ks[12], (DEPTH, SWA_HEADS), 1.0),
        'w_out': nrm(ks[13], (DEPTH, MIX_WIDTH, D_MODEL), MIX_WIDTH ** -0.5),
        'norm_cross': gain(ks[14], D_MODEL),
        'norm_mem': gain(ks[15], D_MODEL),
        'w_mq': nrm(ks[16], (DEPTH, D_MODEL, MEM_W), D_MODEL ** -0.5),
        'w_mkv': nrm(ks[17], (DEPTH, D_MODEL, 2 * MEM_W), D_MODEL ** -0.5),
        'q_norm_m': gain(ks[18], MEM_HEAD_DIM),
        'k_norm_m': gain(ks[19], MEM_HEAD_DIM),
        'w_mo': nrm(ks[20], (DEPTH, MEM_W, D_MODEL), MEM_W ** -0.5),
        'norm_ffn': gain(ks[21], D_MODEL),
        'w_gu': nrm(ks[22], (DEPTH, D_MODEL, 2 * FFN_HIDDEN), D_MODEL ** -0.5),
        'w_down': nrm(ks[23], (DEPTH, FFN_HIDDEN, D_MODEL), FFN_HIDDEN ** -0.5),
    }


def reference(x_prompt, x_sample, mem_prompt, cache_swa_k, cache_swa_v, state_ret, cache_mem_k, cache_mem_v,
              norm_mix, w_in, q_norm_a, k_norm_a, sinks, w_out, norm_cross, norm_mem, w_mq, w_mkv,
              q_norm_m, k_norm_m, w_mo, norm_ffn, w_gu, w_down):
    pos_p = jnp.arange(x_prompt.shape[1], dtype=jnp.int32)
    pos_s = PAST_LEN + jnp.arange(x_sample.shape[1], dtype=jnp.int32)
    xp, xs = x_prompt, x_sample
    pk, pv, ps, pmk, pmv, sk, sv, ss = [], [], [], [], [], [], [], []
    for l in range(DEPTH):
        lw = (norm_mix[l], w_in[l], q_norm_a[l], k_norm_a[l], sinks[l], w_out[l],
              norm_cross[l], w_mq[l], q_norm_m[l], w_mo[l], norm_ffn[l], w_gu[l], w_down[l])
        mk, mv = memory_kv(mem_prompt, norm_mem[l], w_mkv[l], k_norm_m[l])
        s0 = jnp.zeros((xp.shape[0], RET_HEADS, RET_DIM, RET_DIM), jnp.float32)
        xp, k_new, v_new, s_new = decoder_layer(xp, pos_p, swa_prompt, s0, mk, mv, *lw)
        pk.append(k_new); pv.append(v_new); ps.append(s_new); pmk.append(mk); pmv.append(mv)
        fn = lambda q, k, v, sk_, bk=cache_swa_k[l], bv=cache_swa_v[l]: swa_sample(q, k, v, sk_, bk, bv)
        xs, k_new, v_new, s_new = decoder_layer(xs, pos_s, fn, state_ret[l], cache_mem_k[l], cache_mem_v[l], *lw)
        sk.append(k_new); sv.append(v_new); ss.append(s_new)
    return (xp, xs, jnp.stack(pk), jnp.stack(pv), jnp.stack(ps), jnp.stack(pmk), jnp.stack(pmv),
            jnp.stack(sk), jnp.stack(sv), jnp.stack(ss))
```

```python
import contextlib
import numpy as np
import ml_dtypes
import concourse.bass as bass
import concourse.mybir as mybir
from concourse.bass_utils import run_bass_kernel_spmd

F32 = mybir.dt.float32
BF16 = mybir.dt.bfloat16
ALU = mybir.AluOpType
AF = mybir.ActivationFunctionType
AX = mybir.AxisListType

NCORES = 8
NT = 33
GRP = 2
GROUPS = [list(range(g, min(g + GRP, NT - 1))) for g in range(0, NT - 1, GRP)] + [[NT - 1]]
NG = len(GROUPS)
NJ = 22
EPS = 1e-6


class Op:
    __slots__ = ("idx", "stream", "fn", "deps", "chan", "is_dma", "sig", "sigval", "chanval", "cost", "nbytes")


class Prog:
    STREAMS = ("pe", "act", "dve", "pool", "sp")

    def __init__(self):
        self.ops = []
        self.tok = {}
        self.chan_n = {}
        self.chan_last = {}
        self.out_dmas = []
        self.cost = {"dve": 0.0, "pool": 0.0}

    DEFCOST = {"pe": 0.5, "act": 0.6, "dve": 0.5, "pool": 0.7, "sp": 0.1}

    def add(self, stream, fn, reads=(), writes=(), chan=None, is_out=False, cost=None, nbytes=0):
        op = Op()
        op.cost = self.DEFCOST[stream] if cost is None else cost
        op.nbytes = nbytes
        op.idx = len(self.ops)
        op.stream = stream
        op.fn = fn
        op.chan = chan
        op.is_dma = chan is not None
        op.sig = False
        op.sigval = 0
        op.chanval = 0
        deps = {}
        for t in reads:
            st = self.tok.setdefault(t, [None, []])
            if st[0] is not None:
                deps[st[0]] = "raw"
            if isinstance(t, tuple) and t[0] == "bk":
                for r in st[1]:
                    if self.ops[r].stream != stream:
                        deps.setdefault(r, "rr")
        for t in writes:
            st = self.tok.setdefault(t, [None, []])
            if st[0] is not None:
                deps.setdefault(st[0], "waw")
            for r in st[1]:
                deps.setdefault(r, "war")
        for t in reads:
            self.tok[t][1].append(op.idx)
        for t in writes:
            self.tok[t] = [op.idx, []]
        deps.pop(op.idx, None)
        op.deps = deps
        if op.is_dma:
            prev = self.chan_last.get(chan)
            if prev is not None:
                deps.setdefault(prev, "chan")
            self.chan_last[chan] = op.idx
            self.chan_n[chan] = self.chan_n.get(chan, 0) + 1
            op.chanval = 16 * self.chan_n[chan]
            if is_out:
                self.out_dmas.append(op.idx)
        self.ops.append(op)
        return op

    def ew(self, fn, reads=(), writes=(), cost=1.0, only=None):
        if only is None:
            eng = "dve" if self.cost["dve"] <= self.cost["pool"] + 2.0 * cost else "pool"
        else:
            eng = only
        self.cost[eng] += cost * (2.0 if eng == "pool" else 1.0)
        return self.add(eng, fn, reads, writes, cost=(0.15 + 1.0 * cost) if eng == "pool" else (0.1 + 0.6 * cost))

    do_schedule = True
    LAT = 0.45
    DMA_FIXED = 2.0
    DMA_BW = 300e3

    def schedule(self):
        import heapq
        ops = self.ops
        n = len(ops)
        succ = [[] for _ in range(n)]
        indeg = [0] * n
        for o in ops:
            for d in o.deps:
                succ[d].append(o.idx)
                indeg[o.idx] += 1
        dur = [(o.cost if not o.is_dma else self.DMA_FIXED + o.nbytes / self.DMA_BW) for o in ops]
        prio = [0.0] * n
        for i in range(n - 1, -1, -1):
            m = 0.0
            for s_ in succ[i]:
                if prio[s_] > m:
                    m = prio[s_]
            prio[i] = dur[i] + m
        eng_free = {s_: 0.0 for s_ in self.STREAMS}
        dma_free = 0.0
        finish = [0.0] * n
        ready_t = [0.0] * n
        avail = [i for i in range(n) if indeg[i] == 0]
        order = []
        while avail:
            best, bkey = None, None
            for i in avail:
                o = ops[i]
                st_ = max(eng_free[o.stream], ready_t[i])
                key = (int(st_ / 0.35), -prio[i], i)
                if bkey is None or key < bkey:
                    best, bkey = i, key
            avail.remove(best)
            o = ops[best]
            st_ = max(eng_free[o.stream], ready_t[best])
            if o.is_dma:
                issue = 1.0 if o.stream == "pool" else 0.08
                eng_free[o.stream] = st_ + issue
                t0 = max(st_ + issue, dma_free)
                dma_free = t0 + o.nbytes / self.DMA_BW
                finish[best] = dma_free + self.DMA_FIXED
            else:
                eng_free[o.stream] = st_ + o.cost
                finish[best] = st_ + o.cost
            order.append(best)
            for s_ in succ[best]:
                r = finish[best] + self.LAT
                if r > ready_t[s_]:
                    ready_t[s_] = r
                indeg[s_] -= 1
                if indeg[s_] == 0:
                    avail.append(s_)
        assert len(order) == n
        self.model_time = max(finish)
        return order

    def _needs_wait(self, c, p, kind):
        if p.is_dma:
            return True
        if c.stream == p.stream:
            if c.is_dma:
                return True
            if c.stream == "pe":
                return False
            return True
        return True

    def emit(self, nc, stack):
        ops = self.ops
        for c in ops:
            for d, kind in c.deps.items():
                if self._needs_wait(c, ops[d], kind):
                    ops[d].sig = True
        cnt = {s: 0 for s in self.STREAMS}
        order = self.schedule() if self.do_schedule else list(range(len(ops)))
        self._order = order
        for o in (ops[i] for i in order):
            if not o.is_dma and o.sig:
                cnt[o.stream] += 1
                o.sigval = cnt[o.stream]
        esem = {s: stack.enter_context(nc.semaphore("s_" + s)) for s in self.STREAMS if s != "sp"}
        csem = {c: stack.enter_context(nc.semaphore("c_" + c)) for c in self.chan_n}
        block = stack.enter_context(nc.Block())
        streams = {s: [ops[i] for i in order if ops[i].stream == s] for s in self.STREAMS}

        def run_stream(sname, eng):
            waited = {}
            for o in streams[sname]:
                for d, kind in o.deps.items():
                    p = ops[d]
                    if not self._needs_wait(o, p, kind):
                        continue
                    if p.is_dma:
                        sem, val, key = csem[p.chan], p.chanval, "c_" + p.chan
                    else:
                        sem, val, key = esem[p.stream], p.sigval, "e_" + p.stream
                    if waited.get(key, 0) >= val:
                        continue
                    waited[key] = val
                    eng.wait_ge(sem, val)
                ins = o.fn(eng)
                if o.is_dma:
                    ins.then_inc(csem[o.chan], 16)
                elif o.sig:
                    ins.then_inc(esem[o.stream], 1)
            if sname == "sp":
                done = set()
                for d in self.out_dmas:
                    p = ops[d]
                    if p.chan in done:
                        continue
                    done.add(p.chan)
                    eng.wait_ge(csem[p.chan], 16 * self.chan_n[p.chan])

        @block.tensor
        def _(e):
            run_stream("pe", e)

        @block.scalar
        def _(e):
            run_stream("act", e)

        @block.vector
        def _(e):
            run_stream("dve", e)

        @block.gpsimd
        def _(e):
            run_stream("pool", e)

        @block.sync
        def _(e):
            run_stream("sp", e)


def host_consts():
    bf = ml_dtypes.bfloat16
    k = np.arange(128)[:, None]
    q = np.arange(128)[None, :]
    masks = np.zeros((128, 4, 128), np.float32)
    masks[:, 0, :] = (k <= q)
    masks[:, 1, :] = (k > q)
    masks[:, 2, :] = (k // 8 == q // 8) & (k % 8 <= q % 8)
    masks[:, 3, :] = (k > q % 8)
    colmask = np.zeros((128, 16, 128), np.float32)
    for b in range(16):
        colmask[:, b, :] = (q // 8 == b)
    rowmask = np.zeros((128, 16), np.float32)
    for b in range(16):
        rowmask[:, b] = (np.arange(128) // 8 == b)
    f32 = np.float32
    tabs = np.zeros((NT, 128, 4, 64), np.float32)
    inv_a = (f32(1.0) / (f32(10000.0) ** (np.arange(32, dtype=f32) / f32(32)))).astype(f32)
    inv_r = (f32(10000.0) ** (-np.linspace(0.0, 1.0, 32, dtype=f32))).astype(f32)
    p = np.arange(128)
    for t in range(NT):
        pos = (t * 128 + p) if t < 32 else (16384 + p % 8)
        pos = pos.astype(f32)
        ang = (pos[:, None] * inv_a[None, :]).astype(f32).astype(np.float64)
        c, s = np.cos(ang), np.sin(ang)
        tabs[t, :, 0, :] = np.concatenate([c, c], -1)
        tabs[t, :, 1, :] = np.concatenate([-s, s], -1)
        ang = (pos[:, None] * inv_r[None, :]).astype(f32).astype(np.float64)
        c, s = np.cos(ang), np.sin(ang)
        tabs[t, :, 2, :] = np.stack([c, c], -1).reshape(128, 64)
        tabs[t, :, 3, :] = np.stack([-s, s], -1).reshape(128, 64)
    log_g = np.log(1.0 - np.exp2(-5.0 - np.arange(8, dtype=np.float64)))
    dec = np.zeros((128, 2, 2, 8), np.float32)
    for v, pp in enumerate([p, p % 8]):
        dec[:, v, 0, :] = np.exp((pp[:, None] + 1.0) * log_g[None, :])
        dec[:, v, 1, :] = np.exp(-(pp[:, None] + 1.0) * log_g[None, :]) / 8.0
    cdec = np.zeros((128, 2, 4), np.float32)
    for c_ in range(4):
        h = 2 * c_ + p // 64
        cdec[:, 0, c_] = np.exp(128.0 * log_g[h])
        cdec[:, 1, c_] = np.exp(8.0 * log_g[h])
    return {
        "c_ident": np.eye(128).astype(bf),
        "c_masks": masks.astype(bf),
        "c_colmask": colmask.astype(bf),
        "c_rowmask": rowmask,
        "c_tabs": tabs,
        "c_dec": dec,
        "c_cdec": cdec,
    }


class _Stop(Exception):
    pass


def build_program(stage=99, skip=()):
    nc = bass.Bass("TRN2", target_bir_lowering=False)
    P = Prog()

    def ckpt(k):
        if stage <= k:
            raise _Stop()
    din = lambda n, s, d=F32: nc.dram_tensor(n, list(s), d, kind="ExternalInput").ap()
    dout = lambda n, s: nc.dram_tensor(n, list(s), F32, kind="ExternalOutput").ap()
    x_p = din("x_p", [4096, 1024]); x_s = din("x_s", [128, 1024]); mem_p = din("mem_p", [256, 1024])
    c_k = din("c_k", [16, 128, 128]); c_v = din("c_v", [16, 128, 128]); s_ret = din("s_ret", [16, 8, 64, 64])
    c_mk = din("c_mk", [16, 256, 512]); c_mv = din("c_mv", [16, 256, 512])
    w_in = din("w_in", [1024, 2816]); w_out = din("w_out", [1024, 1024]); w_mq = din("w_mq", [1024, 512])
    w_mkv = din("w_mkv", [1024, 1024]); w_mo = din("w_mo", [512, 1024]); w_gu = din("w_gu", [1024, 5632])
    w_down = din("w_down", [2816, 1024])
    norm_mix = din("norm_mix", [1, 1024]); norm_cross = din("norm_cross", [1, 1024])
    norm_mem = din("norm_mem", [1, 1024]); norm_ffn = din("norm_ffn", [1, 1024])
    q_norm_a = din("q_norm_a", [1, 64]); k_norm_a = din("k_norm_a", [1, 64]); sinks = din("sinks", [1, 8])
    q_norm_m = din("q_norm_m", [1, 128]); k_norm_m = din("k_norm_m", [1, 128])
    c_ident = din("c_ident", [128, 128], BF16); c_masks = din("c_masks", [128, 4, 128], BF16)
    c_colmask = din("c_colmask", [128, 16, 128], BF16); c_rowmask = din("c_rowmask", [128, 16])
    c_tabs = din("c_tabs", [NT, 128, 4, 64]); c_dec = din("c_dec", [128, 2, 2, 8]); c_cdec = din("c_cdec", [128, 2, 4])
    y_p = dout("y_p", [4096, 1024]); y_s = dout("y_s", [128, 1024])
    o_swk_p = dout("o_swk_p", [128, 128]); o_swv_p = dout("o_swv_p", [128, 128]); o_ret_p = dout("o_ret_p", [8, 64, 64])
    o_mk_p = dout("o_mk_p", [256, 512]); o_mv_p = dout("o_mv_p", [256, 512])
    o_swk_s = dout("o_swk_s", [16, 128, 128]); o_swv_s = dout("o_swv_s", [16, 128, 128]); o_ret_s = dout("o_ret_s", [16, 8, 64, 64])
    wgu_d = nc.dram_tensor("wgu_d", [NJ, 128, 2048], BF16, kind="Internal").ap()
    wd_d = nc.dram_tensor("wd_d", [NJ, 128, 1024], BF16, kind="Internal").ap()

    def bc_row(ap, n):
        return bass.AP(ap.tensor, 0, [[0, 128], [1, n]])

    with contextlib.ExitStack() as st:
        st.enter_context(nc.allow_non_contiguous_dma(reason="small strided constant / cache-row loads"))
        sb = lambda n, s, d=F32: st.enter_context(nc.sbuf_tensor(n, list(s), d))
        W_in = sb("W_in", [128, 8, 2816], BF16); W_out = sb("W_out", [128, 8, 1024], BF16)
        W_mq = sb("W_mq", [128, 8, 512], BF16); W_mo = sb("W_mo", [128, 4, 1024], BF16)
        ident = sb("ident", [128, 128], BF16); masks = sb("masks", [128, 4, 128], BF16)
        ident32 = sb("ident32", [128, 128], F32)
        colmask = sb("colmask", [128, 16, 128], BF16); rowmask = sb("rowmask", [128, 16])
        dec = sb("dec", [128, 2, 2, 8]); cdec = sb("cdec", [128, 2, 4]); nh = sb("nh", [128, 8])
        gcols = sb("gcols", [128, 4, 8])
        gqa = sb("gqa", [128, 2, 64]); gka = sb("gka", [128, 2, 64])
        gqm = sb("gqm", [128, 128]); gkm = sb("gkm", [128, 128]); esink = sb("esink", [128, 8])
        xres = [sb("xres%d" % i, [128, 1024]) for i in range(2 * GRP)]

        class _Set:
            def __init__(self, n):
                self.n = n
                self.xn = sb("xn%d" % n, [128, 1024], BF16)
                self.xnT = sb("xnT%d" % n, [128, 8, 128], BF16)
                self.ss = sb("ss%d" % n, [128, 1]); self.rs = sb("rs%d" % n, [128, 1])
                self.tmpA = sb("tmpA%d" % n, [128, 512]); self.tmpB = sb("tmpB%d" % n, [128, 512]); self.tmpC = sb("tmpC%d" % n, [128, 512])
                self.st8 = sb("st8%d" % n, [128, 8]); self.rh8 = sb("rh8%d" % n, [128, 8]); self.den8 = sb("den8%d" % n, [128, 8])
                self.XN = [self.T("xnT", 0), self.T("xnT", 1)]

            def T(self, *tok):
                return ("S%d" % self.n,) + tok
        SETS = [_Set(0), _Set(1)]
        proj = sb("proj", [128, 2816])
        tab = [sb("tab%d" % i, [128, 4, 64]) for i in range(2)]
        tg = sb("tg", [128, 4, 64])
        qa_bf = sb("qa_bf", [128, 512], BF16); ka_f = sb("ka_f", [128, 128]); ka_bf = sb("ka_bf", [128, 128], BF16)
        qaT = sb("qaT", [128, 512], BF16)
        kaT = [sb("kaT%d" % i, [128, 128], BF16) for i in range(2)]
        Vext = [sb("Vext%d" % i, [128, 2, 65], BF16) for i in range(2)]
        qr_bf = sb("qr_bf", [128, 512], BF16); kr_bf = sb("kr_bf", [128, 512], BF16); vr_bf = sb("vr_bf", [128, 512], BF16)
        qkT = sb("qkT", [128, 8, 128], BF16)
        innT = sb("innT", [128, 8, 128], BF16)
        S_f = sb("S_f", [128, 4, 64]); S_bd = [sb("S_bd%d" % i, [128, 4, 128], BF16) for i in range(2)]
        kmT = sb("kmT", [128, 4, 256], BF16); Vm = sb("Vm", [128, 2, 4, 129], BF16)
        xn3T = sb("xn3T", [128, 8, GRP * 128], BF16)
        wgu = [sb("wgu%d" % i, [128, 8, 256], BF16) for i in range(3)]
        wdb = [sb("wdb%d" % i, [128, 1024], BF16) for i in range(2)]
        sgb = [sb("sgb%d" % i, [128, GRP * 128]) for i in range(2)]
        hT = [sb("hT%d" % i, [128, GRP * 128], BF16) for i in range(2)]
        stg = [sb("stg%d" % i, [128, 1024]) for i in range(2)]
        stg_bf = sb("stg_bf", [128, 1024], BF16)
        wdb.append(stg_bf)
        WDT = [("wdb", 0), ("wdb", 1), "stg_bf"]
        WGV = [w_[:] for w_ in wgu] + [stg[i][:].bitcast(BF16).rearrange("p (k c) -> p k c", k=8) for i in range(2)]
        WGT = [("wgu", 0), ("wgu", 1), ("wgu", 2), ("stg", 0), ("stg", 1)]
        NWG = 5
        kcT = sb("kcT", [128, 16, 128], BF16); Vc = sb("Vc", [128, 16, 2, 65], BF16)
        kmTb = sb("kmTb", [128, 1024], BF16); Vmb = sb("Vmb", [128, 2, 4, 129], BF16)
        L0 = SETS[0]
        L0.pT = [sb("pT%d" % i, [128, 512], BF16)[:] for i in range(2)]
        L0.mix_bf = sb("mix_bf", [128, 1024], BF16)[:]
        L0.qm_bf = sb("qm_bf", [128, 512], BF16)[:]; L0.qmT = sb("qmT", [128, 512], BF16)[:]
        L0.pmT = sb("pmT", [128, 1024], BF16)[:]; L0.om_bf = sb("om_bf", [128, 512], BF16)[:]; L0.omT = sb("omT", [128, 512], BF16)[:]
        L1 = SETS[1]
        kc_flat = kcT[:].rearrange("p b r -> p (b r)")
        vc_flat = Vc[:].rearrange("p b h d -> p (b h d)")
        L1.mix_bf = kc_flat[:, 0:1024]; L1.pmT = kc_flat[:, 1024:2048]
        L1.pT = [vc_flat[:, 0:512], vc_flat[:, 512:1024]]
        L1.qm_bf = vc_flat[:, 1024:1536]; L1.qmT = vc_flat[:, 1536:2048]
        L1.om_bf = kmTb[:, 0:512]; L1.omT = kmTb[:, 512:1024]
        LATE_TOKS = []
        ps = st.enter_context(nc.psum_tensor("ps", [128, 4096], F32))

        def bank(b, n=512):
            return ps[:, b * 512:b * 512 + n]

        def bankbf(b, n=1024):
            return ps[:, b * 512:(b + 1) * 512].bitcast(BF16)[:, 0:n]

        BK = lambda b: ("bk", b)

        class _Banks:
            def __init__(self, banks):
                self.free = list(banks)

            def get(self):
                assert self.free, "PSUM banks exhausted in program order"
                return self.free.pop(0)

            def put(self, *bs):
                for b in bs:
                    assert b not in self.free
                    self.free.append(b)
        BAS = [_Banks(range(0, 4)), _Banks(range(4, 8))]

        class _BAProxy:
            cur = 0

            def get(self):
                return BAS[self.cur].get()

            def put(self, *bs):
                for b in bs:
                    BAS[0 if b < 4 else 1].put(b)
        BA = _BAProxy()
        cnt = {"dma": 0}

        def dma(stream, out, in_, reads, writes, chan, is_out=False):
            if "wcast" in skip and stream == "pool":
                return None
            nb = out.size() * (2 if out.dtype == BF16 else 4)
            return P.add(stream, lambda e: e.dma_start(out=out, in_=in_), reads, writes, chan=chan, is_out=is_out, nbytes=nb)

        dma("sp", ident[:], c_ident, [], ["ident"], "k0")
        dma("sp", masks[:], c_masks, [], ["masks"], "k1")
        P.add("act", lambda e: e.activation(out=ident32[:], in_=ident[:], func=AF.Copy), ["ident"], ["ident32"])
        dma("sp", colmask[:], c_colmask, [], ["colmask"], "k2")
        dma("sp", rowmask[:], c_rowmask, [], ["rowmask"], "k3")
        dma("sp", dec[:], c_dec, [], ["dec"], "k4")
        dma("sp", cdec[:], c_cdec, [], ["cdec"], "k5")
        for i, nrm in enumerate([] if "gcols" in skip else [norm_mix, norm_cross, norm_ffn, norm_mem]):
            dma("sp", gcols[:, i, :], nrm[0].rearrange("(k p) -> p k", p=128), [], [("gcols", i)], "k6")
        for t_, src in (() if "bcast" in skip else ((gqa, q_norm_a), (gka, k_norm_a))):
            nm = "gqa" if t_ is gqa else "gka"
            dma("sp", t_[:, 0, :], bc_row(src, 64), [], [nm + "0"], "k7")
            dma("sp", t_[:, 1, 0:32], bass.AP(src.tensor, 32, [[0, 128], [1, 32]]), [], [nm + "1"], "k8")
            dma("sp", t_[:, 1, 32:64], bass.AP(src.tensor, 0, [[0, 128], [1, 32]]), [], [nm + "2"], "k9")
        GQA = ["gqa0", "gqa1", "gqa2"]; GKA = ["gka0", "gka1", "gka2"]
        if "bcast2" not in skip:
            dma("sp", gqm[:], bc_row(q_norm_m, 128), [], ["gqm"], "k10")
            dma("sp", gkm[:], bc_row(k_norm_m, 128), [], ["gkm"], "k11")
            dma("sp", esink[:], bc_row(sinks, 8), [], ["esink"], "k12")
            P.add("act", lambda e: e.activation(out=esink[:], in_=esink[:], func=AF.Exp), ["esink"], ["esink"])
        P.add("pool", lambda e: e.memset(nh[:], -0.5), [], ["nh"])
        P.add("pool", lambda e: e.memset(S_f[:], 0.0), [], ["S_f"])
        for i in range(2):
            P.add("pool", lambda e, i=i: e.memset(S_bd[i][:], 0.0), [], [("S_bd", i)])
            P.add("pool", lambda e, i=i: e.memset(Vext[i][:], 1.0), [], [("Vext", i)])
        P.add("pool", lambda e: e.memset(Vm[:], 1.0), [], ["Vm"])
        P.add("pool", lambda e: e.memset(Vmb[:], 1.0), [], ["Vmb"])
        WIN = []
        for kvh in range(2):
            for g_ in range(4):
                hh = kvh * 4 + g_
                src = w_in[:, hh * 64:(hh + 1) * 64].rearrange("(k p) d -> p k d", p=128)
                dst = W_in[:, :, g_ * 128 + kvh * 64: g_ * 128 + (kvh + 1) * 64]
                dma("pool", dst, src, [], [("W_in", kvh)], "w%d" % kvh)
            WIN.append(("W_in", kvh))
        for i in range(4):
            src = w_in[:, 768 + i * 512:768 + (i + 1) * 512].rearrange("(k p) n -> p k n", p=128)
            dma("pool", W_in[:, :, 512 + i * 512:1024 + i * 512], src, [], [("W_in", 2 + i)], "w%d" % (2 + i)); WIN.append(("W_in", 2 + i))
        dma("pool", W_in[:, :, 2560:2816], w_in[:, 512:768].rearrange("(k p) n -> p k n", p=128), [], [("W_in", 6)], "w6"); WIN.append(("W_in", 6))
        dma("pool", W_out[:], w_out.rearrange("(k p) n -> p k n", p=128), [], ["W_out"], "w7")
        dma("pool", W_mq[:], w_mq.rearrange("(k p) n -> p k n", p=128), [], ["W_mq"], "w8")
        dma("pool", W_mo[:], w_mo.rearrange("(k p) n -> p k n", p=128), [], ["W_mo"], "w9")

        def rstd_chain(S, src_tok_list, st_ap, rh_ap, n, scale, rd, wr):
            P.add("pool", lambda e: e.tensor_scalar(out=rh_ap, in0=st_ap, scalar1=scale, scalar2=EPS, op0=ALU.mult, op1=ALU.add), rd, wr, cost=0.25)
            P.add("pool", lambda e: e.tensor_tensor(out=rh_ap, in0=rh_ap, in1=nh[:, 0:n], op=ALU.pow), wr + ["nh"], wr, cost=0.3)

        def norm_T(S, src, src_tok, gi, dstT, dst_tok, defer=False):
            P.add("act", lambda e: e.activation(out=S.xn[:], in_=src, func=AF.Square, accum_out=S.ss[:]), [src_tok], [S.T("xn"), S.T("ss")], cost=1.1)
            rstd_chain(S, None, S.ss[:], S.rs[:], 1, 1.0 / 1024, [S.T("ss")], [S.T("rs")])
            if defer:
                P.add("act", lambda e: e.activation(out=S.xn[:], in_=src, func=AF.Copy), [src_tok], [S.T("xn")], cost=1.1)
            else:
                P.add("act", lambda e: e.activation(out=S.xn[:], in_=src, func=AF.Copy, scale=S.rs[:]), [src_tok, S.T("rs")], [S.T("xn")], cost=1.1)

            ba, bd = BA.get(), BA.get()

            def tr(e):
                for k in range(8):
                    bb = ba if k < 4 else bd
                    ins = e.transpose(out=bankbf(bb)[:, (k % 4) * 128:(k % 4 + 1) * 128], in_=S.xn[:, k * 128:(k + 1) * 128], identity=ident[:])
                return ins
            P.add("pe", tr, [S.T("xn"), "ident"], [BK(ba), BK(bd)], cost=0.6)

            def ev_a(e):
                for k in range(0, 4):
                    ins = e.activation(out=dstT[:, k, :], in_=bankbf(ba)[:, k * 128:(k + 1) * 128], func=AF.Copy, scale=gcols[:, gi, k:k + 1])
                return ins

            def ev_d(e):
                for k in range(4, 8):
                    ins = e.tensor_scalar(out=dstT[:, k, :], in0=bankbf(bd)[:, (k - 4) * 128:(k - 3) * 128], scalar1=gcols[:, gi, k:k + 1], scalar2=None, op0=ALU.mult)
                return ins
            P.add("act", ev_a, [BK(ba), ("gcols", gi)], [dst_tok + (0,)], cost=0.9)
            P.add("dve", ev_d, [BK(bd), ("gcols", gi)], [dst_tok + (1,)], cost=0.7)
            BA.put(ba, bd)

        def head_rstd(S, src3, H, D, rd, scratch, scratch_tok):
            sq = scratch[:, 0:H * D].rearrange("p (h d) -> p h d", d=D)
            P.ew(lambda e: e.tensor_tensor(out=sq, in0=src3, in1=src3, op=ALU.mult), rd, [scratch_tok], cost=H * D / 512.0)
            P.add("dve", lambda e: e.tensor_reduce(out=S.st8[:, 0:H], in_=sq, axis=AX.X, op=ALU.add), [scratch_tok], [S.T("st8")])
            rstd_chain(S, None, S.st8[:, 0:H], S.rh8[:, 0:H], H, 1.0 / D, [S.T("st8")], [S.T("rh8")])

        def bc_h(ap2, H):
            return ap2.unsqueeze(1).to_broadcast([128, H, ap2.shape[1]])

        def bc_d(ap2, D):
            return ap2.unsqueeze(2).to_broadcast([128, ap2.shape[1], D])

        def rot_half(S, src3, H, CT, ST, rd):
            A = S.tmpA[:, 0:H * 64].rearrange("p (h d) -> p h d", d=64)
            B = S.tmpB[:, 0:H * 64].rearrange("p (h d) -> p h d", d=64)
            c = H * 64 / 512.0
            P.ew(lambda e: e.tensor_tensor(out=A, in0=src3, in1=bc_h(CT, H), op=ALU.mult), rd, [S.T("tmpA")], cost=c)
            P.ew(lambda e: e.tensor_tensor(out=B[:, :, 0:32], in0=src3[:, :, 32:64], in1=bc_h(ST[:, 0:32], H), op=ALU.mult), rd, [S.T("tmpB", 0)], cost=c / 2)
            P.ew(lambda e: e.tensor_tensor(out=B[:, :, 32:64], in0=src3[:, :, 0:32], in1=bc_h(ST[:, 32:64], H), op=ALU.mult), rd, [S.T("tmpB", 1)], cost=c / 2)
            P.ew(lambda e: e.tensor_tensor(out=A, in0=A, in1=B, op=ALU.add), [S.T("tmpA"), S.T("tmpB", 0), S.T("tmpB", 1)], [S.T("tmpA")], cost=c)
            return A

        def rot_pair(S, src3, CF, SF, rd):
            A = S.tmpA[:].rearrange("p (h d) -> p h d", d=64)
            B4 = S.tmpB[:].rearrange("p (h i two) -> p h i two", h=8, two=2)
            s4 = src3.rearrange("p h (i two) -> p h i two", two=2)
            SF3 = SF.rearrange("p (i two) -> p i two", two=2)
            P.ew(lambda e: e.tensor_tensor(out=A, in0=src3, in1=bc_h(CF, 8), op=ALU.mult), rd, [S.T("tmpA")], cost=1)
            P.ew(lambda e: e.tensor_tensor(out=B4[:, :, :, 0], in0=s4[:, :, :, 1], in1=bc_h(SF3[:, :, 0], 8), op=ALU.mult), rd, [S.T("tmpB", 0)], cost=0.5)
            P.ew(lambda e: e.tensor_tensor(out=B4[:, :, :, 1], in0=s4[:, :, :, 0], in1=bc_h(SF3[:, :, 1], 8), op=ALU.mult), rd, [S.T("tmpB", 1)], cost=0.5)
            P.ew(lambda e: e.tensor_tensor(out=S.tmpA[:], in0=S.tmpA[:], in1=S.tmpB[:], op=ALU.add), [S.T("tmpA"), S.T("tmpB", 0), S.T("tmpB", 1)], [S.T("tmpA")], cost=1)
            return A

        def transposes(pairs, rd, dst, dst_tok, n):
            b0 = BA.get()

            def tr(e):
                for i, src in enumerate(pairs):
                    ins = e.transpose(out=bankbf(b0)[:, i * 128:(i + 1) * 128], in_=src, identity=ident[:])
                return ins
            P.add("pe", tr, list(rd) + ["ident"], [BK(b0)], cost=0.08 * n)
            wr = dst_tok if isinstance(dst_tok, list) else [dst_tok]
            P.add("act", lambda e: e.activation(out=dst, in_=bankbf(b0)[:, 0:n * 128], func=AF.Copy), [BK(b0)], wr, cost=0.25 + n * 0.1)
            BA.put(b0)

        def memkv():
            S = SETS[0]
            memT = [xn3T[:, :, i * 128:(i + 1) * 128] for i in range(2)]
            for mt in range(2):
                dma("sp", stg[mt][:], mem_p[mt * 128:(mt + 1) * 128, :], [], [("stg", mt)], "sg%d" % mt)
                norm_T(S, stg[mt][:], ("stg", mt), 3, memT[mt], ("xn3T", mt))
            ckpt(1.1)
            for c in range(4):
                s = c % 3
                dma("pool", wgu[s][:], w_mkv[:, c * 256:(c + 1) * 256].rearrange("(k p) n -> p k n", p=128), [], [("wgu", s)], "pg%d" % s)
                for mt in range(2):
                    bk = BA.get()

                    def mmk(e, s=s, mt=mt, bk=bk):
                        for k in range(8):
                            ins = e.matmul(bank(bk, 256), lhsT=memT[mt][:, k, :], rhs=wgu[s][:, k, :], start=(k == 0), stop=(k == 7))
                        return ins
                    P.add("pe", mmk, [("xn3T", mt, 0), ("xn3T", mt, 1), ("wgu", s)], [BK(bk)])
                    P.add("act", lambda e, mt=mt, c=c, bk=bk: e.activation(out=proj[:, mt * 1024 + c * 256: mt * 1024 + (c + 1) * 256], in_=bank(bk, 256), func=AF.Copy),
                          [BK(bk)], [("proj", mt * 2 + c // 2)])
                    BA.put(bk)
            ckpt(1.2)
            for mt in range(2):
                kv = proj[:, mt * 1024:(mt + 1) * 1024]
                k3 = kv[:, 0:512].rearrange("p (h d) -> p h d", d=128)
                rdk = [("proj", mt * 2)]
                head_rstd(S, k3, 4, 128, rdk, S.tmpA, S.T("tmpA"))
                A3 = S.tmpA[:].rearrange("p (h d) -> p h d", d=128)
                P.ew(lambda e, k3=k3, A3=A3: e.tensor_tensor(out=A3, in0=k3, in1=bc_d(S.rh8[:, 0:4], 128), op=ALU.mult), rdk + [S.T("rh8"), S.T("tmpA")], [S.T("tmpA")])
                C3 = S.tmpC[:].rearrange("p (h d) -> p h d", d=128)
                P.ew(lambda e, A3=A3, C3=C3: e.tensor_tensor(out=C3, in0=A3, in1=bc_h(gkm[:], 4), op=ALU.mult), [S.T("tmpA"), "gkm"], [S.T("tmpC")])
                ckpt(1.3)
                dma("sp", o_mk_p[mt * 128:(mt + 1) * 128, :], S.tmpC[:], [S.T("tmpC")], [], "omk", is_out=True)
                dma("sp", o_mv_p[mt * 128:(mt + 1) * 128, :], kv[:, 512:1024], [("proj", mt * 2 + 1)], [], "omv", is_out=True)
                P.ew(lambda e: e.tensor_copy(out=stg_bf[:, 0:512], in_=S.tmpC[:]), [S.T("tmpC")], ["stg_bf"])
                P.ew(lambda e, mt=mt, kv=kv: e.tensor_copy(out=Vm[:, mt, :, 0:128], in_=kv[:, 512:1024].rearrange("p (h d) -> p h d", d=128)),
                     [("proj", mt * 2 + 1), "Vm"], ["Vm"])

                ckpt(1.4)

                b0 = BA.get()

                def trk(e, b0=b0):
                    for h in range(4):
                        ins = e.transpose(out=bankbf(b0)[:, h * 128:(h + 1) * 128], in_=stg_bf[:, h * 128:(h + 1) * 128], identity=ident[:])
                    return ins
                P.add("pe", trk, ["stg_bf", "ident"], [BK(b0)])
                P.add("act", lambda e, mt=mt, b0=b0: e.activation(out=kmT[:, :, mt * 128:(mt + 1) * 128], in_=bankbf(b0)[:, 0:512].rearrange("p (h m) -> p h m", h=4), func=AF.Copy),
                      [BK(b0), "kmT"], ["kmT"])
                BA.put(b0)

        if stage > 1:
            try:
                memkv()
            except _Stop:
                pass
        if stage >= 3:
            for j in range(NJ):
                s = j % 3
                for two in range(2):
                    src = w_gu[:, two * 2816 + j * 128: two * 2816 + (j + 1) * 128].rearrange("(k p) n -> p k n", p=128)
                    dma("pool", wgu[s][:, :, two * 128:(two + 1) * 128], src, [], [("wgu", s)], "pg%d" % s)
                dma("sp", wgu_d[j], wgu[s][:].rearrange("p k c -> p (k c)"), [("wgu", s)], [("wgud", j)], "cs%d" % s)
                s_d = j % 2
                dma("pool", wdb[s_d][:], w_down[j * 128:(j + 1) * 128, :], [], [WDT[s_d]], "pd%d" % s_d)
                dma("sp", wd_d[j], wdb[s_d][:], [WDT[s_d]], [("wdd", j)], "ct%d" % s_d)

        def ffn_load(j):
            s = j % NWG
            dma("sp", WGV[s].rearrange("p k c -> p (k c)"), wgu_d[j], [("wgud", j)], [WGT[s]], "fg%d" % s)

        def ffn_load_d(j):
            s = j % 3
            dma("sp", wdb[s][:], wd_d[j], [("wdd", j)], [WDT[s]], "fd%d" % s)

        mult, add = ALU.mult, ALU.add

        def tt(out, in0, in1, op):
            return lambda e: e.tensor_tensor(out=out, in0=in0, in1=in1, op=op)

        def cpy(out, in_):
            return lambda e: e.tensor_copy(out=out, in_=in_)

        def v3(ap, d):
            return ap.rearrange("p (h d) -> p h d", d=d)

        def sample_preload():
            S1_ = SETS[1]
            fence_reads = [S1_.T("mix", 0), S1_.T("mix", 1), S1_.T("mix", 2), S1_.T("pmT", 0), S1_.T("pmT", 1), S1_.T("pT", 0), S1_.T("pT", 1),
                           S1_.T("qm_bf"), S1_.T("qmT"), S1_.T("om_bf", 0), S1_.T("om_bf", 1), S1_.T("omT")]
            P.add("pool", lambda e: e.memset(Vc[:], 1.0), fence_reads, ["Vc", "kcT", "kmTb"])
            for hf in range(2):
                b0 = hf * 8
                dma("sp", v3(stg[0][:], 128), c_k[b0:b0 + 8].rearrange("b r f -> r b f"), [], [("stg", 0)], "sg0")
                dma("sp", v3(stg[1][:], 128), c_v[b0:b0 + 8].rearrange("b r f -> r b f"), [], [("stg", 1)], "sg1")
                P.ew(cpy(stg_bf[:], stg[0][:]), [("stg", 0)], ["stg_bf"], cost=2)
                P.ew(cpy(Vc[:, b0:b0 + 8, :, 0:64], stg[1][:].rearrange("p (b h d) -> p b h d", b=8, h=2)), [("stg", 1), "Vc"], ["Vc"], cost=2)

                bq = BA.get()

                def trc(e, bq=bq):
                    for b in range(8):
                        ins = e.transpose(out=bankbf(bq)[:, b * 128:(b + 1) * 128], in_=stg_bf[:, b * 128:(b + 1) * 128], identity=ident[:])
                    return ins
                P.add("pe", trc, ["stg_bf", "ident"], [BK(bq)])
                P.add("act", lambda e, b0=b0, bq=bq: e.activation(out=kcT[:, b0:b0 + 8, :].rearrange("p b r -> p (b r)"), in_=bankbf(bq), func=AF.Copy), [BK(bq), "kcT"], ["kcT"])
                BA.put(bq)
            dma("sp", o_swk_s[:, 0:120, :], c_k[:, 8:128, :], [], [], "osk", is_out=True)
            dma("sp", o_swv_s[:, 0:120, :], c_v[:, 8:128, :], [], [], "osv", is_out=True)

        def tile_mixer(T, i, xi):
            samp = (T == NT - 1)
            S = SETS[i % 2]
            BA.cur = i % 2
            X = xres[xi]; XT = ("xres", xi)
            sl = T % 2
            tb = tab[T % 2]; TB = ("tab", T % 2)
            dv = 1 if samp else 0
            dma("sp", tb[:], c_tabs[T], [], [TB], "tb%d" % (T % 2))
            norm_T(S, X[:], XT, 0, S.xnT, S.T("xnT"), defer=True)
            groups = [(0, 512), (512, 1024), (1024, 1536), (1536, 2048), (2048, 2560), (2560, 2816)]
            if T == 0:
                S1 = SETS[1]
                xn32 = proj[:, 0:1024]
                P.add("act", lambda e: e.activation(out=xn32, in_=X[:], func=AF.Copy, scale=S.rs[:]), [XT, S.T("rs")], [("proj", 0), ("proj", 1)], cost=1.1)
                b32a, b32b = BA.get(), BA.get()

                def tr32(e):
                    for k in range(8):
                        bb = b32a if k < 4 else b32b
                        ins = e.transpose(out=bank(bb)[:, (k % 4) * 128:(k % 4 + 1) * 128], in_=xn32[:, k * 128:(k + 1) * 128], identity=ident32[:])
                    return ins
                P.add("pe", tr32, [("proj", 0), ("proj", 1), "ident32"], [BK(b32a), BK(b32b)], cost=2.0)

                def ev32a(e):
                    for k in range(4):
                        ins = e.activation(out=S1.tmpA[:, k * 128:(k + 1) * 128], in_=bank(b32a)[:, k * 128:(k + 1) * 128], func=AF.Copy, scale=gcols[:, 0, k:k + 1])
                    return ins

                def ev32b(e):
                    for k in range(4, 8):
                        ins = e.tensor_scalar(out=S1.tmpB[:, (k - 4) * 128:(k - 3) * 128], in0=bank(b32b)[:, (k - 4) * 128:(k - 3) * 128], scalar1=gcols[:, 0, k:k + 1], scalar2=None, op0=ALU.mult)
                    return ins
                P.add("act", ev32a, [BK(b32a), ("gcols", 0)], [S1.T("tmpA")], cost=0.9)
                P.add("dve", ev32b, [BK(b32b), ("gcols", 0)], [S1.T("tmpB", 0), S1.T("tmpB", 1)], cost=0.7)
                BA.put(b32a, b32b)
                for c in range(4):
                    c0 = 768 + c * 256
                    for kh in range(2):
                        dma("sp", stg[kh][:].rearrange("p (k n) -> p k n", k=4), w_in[kh * 512:(kh + 1) * 512, c0:c0 + 256].rearrange("(k p) n -> p k n", p=128),
                            [], [("stg", kh)], "sg%d" % kh)
                    bk = BA.get()

                    def mm32(e, bk=bk):
                        for k in range(8):
                            xt_ = S1.tmpA if k < 4 else S1.tmpB
                            ins = e.matmul(bank(bk, 256), lhsT=xt_[:, (k % 4) * 128:(k % 4 + 1) * 128], rhs=stg[k // 4][:, (k % 4) * 256:(k % 4 + 1) * 256],
                                           start=(k == 0), stop=(k == 7))
                        return ins
                    P.add("pe", mm32, [S1.T("tmpA"), S1.T("tmpB", 0), S1.T("tmpB", 1), ("stg", 0), ("stg", 1)], [BK(bk)], cost=3.6)
                    P.add("act", lambda e, c=c, bk=bk: e.activation(out=proj[:, 512 + c * 256:768 + c * 256], in_=bank(bk, 256), func=AF.Copy), [BK(bk)], [("proj", 1 + c // 2)], cost=0.45)
                    BA.put(bk)
            for gi, (c0, c1) in enumerate(groups):
                if T == 0 and gi in (1, 2):
                    continue
                bk = BA.get()

                def mmp(e, c0=c0, c1=c1, bk=bk):
                    for k in range(8):
                        ins = e.matmul(bank(bk, c1 - c0), lhsT=S.xnT[:, k, :], rhs=W_in[:, k, c0:c1], start=(k == 0), stop=(k == 7))
                    return ins
                P.add("pe", mmp, S.XN + WIN, [BK(bk)], cost=0.03 + 8 * (c1 - c0) / 2400.0)
                P.add("act", lambda e, c0=c0, c1=c1, bk=bk: e.activation(out=proj[:, c0:c1], in_=bank(bk, c1 - c0), func=AF.Copy, scale=S.rs[:]), [BK(bk), S.T("rs")], [("proj", gi)], cost=0.75)
                BA.put(bk)
            ckpt(4)
            P.ew(tt(tg[:, 0, :], tb[:, 0, :], gqa[:, 0, :], mult), [TB] + GQA, [("tg", 0)], cost=0.15)
            P.ew(tt(tg[:, 1, :], tb[:, 1, :], gqa[:, 1, :], mult), [TB] + GQA, [("tg", 1)], cost=0.15)
            P.ew(tt(tg[:, 2, :], tb[:, 0, :], gka[:, 0, :], mult), [TB] + GKA, [("tg", 2)], cost=0.15)
            P.ew(tt(tg[:, 3, :], tb[:, 1, :], gka[:, 1, :], mult), [TB] + GKA, [("tg", 3)], cost=0.15)
            q3 = v3(proj[:, 0:512], 64)
            head_rstd(S, q3, 8, 64, [("proj", 0)], S.tmpC, S.T("tmpC"))
            A = rot_half(S, q3, 8, tg[:, 0, :], tg[:, 1, :], [("proj", 0), ("tg", 0), ("tg", 1)])
            P.ew(tt(v3(qa_bf[:], 64), A, bc_d(S.rh8[:, 0:8], 64), mult), [S.T("tmpA"), S.T("rh8")], ["qa_bf"])
            k3 = v3(proj[:, 2560:2688], 64)
            head_rstd(S, k3, 2, 64, [("proj", 5)], S.tmpC, S.T("tmpC"))
            A = rot_half(S, k3, 2, tg[:, 2, :], tg[:, 3, :], [("proj", 5), ("tg", 2), ("tg", 3)])
            P.ew(tt(v3(ka_f[:], 64), A, bc_d(S.rh8[:, 0:2], 64), mult), [S.T("tmpA"), S.T("rh8")], ["ka_f"], cost=0.25)
            P.ew(cpy(ka_bf[:], ka_f[:]), ["ka_f"], ["ka_bf"], cost=0.25)
            P.ew(cpy(Vext[sl][:, :, 0:64], v3(proj[:, 2688:2816], 64)), [("proj", 5), ("Vext", sl)], [("Vext", sl)], cost=0.25)

            bq, bk_ = BA.get(), BA.get()

            def tra(e):
                for c in range(4):
                    ins = e.transpose(out=bankbf(bq)[:, c * 128:(c + 1) * 128], in_=qa_bf[:, c * 128:(c + 1) * 128], identity=ident[:])
                return ins
            P.add("pe", tra, ["qa_bf", "ident"], [BK(bq)], cost=0.35)
            P.add("pe", lambda e: e.transpose(out=bankbf(bk_)[:, 0:128], in_=ka_bf[:], identity=ident[:]), ["ka_bf", "ident"], [BK(bk_)], cost=0.1)
            P.add("act", lambda e: e.activation(out=qaT[:], in_=bankbf(bq)[:, 0:512], func=AF.Copy), [BK(bq)], ["qaT"], cost=0.65)
            P.add("dve", cpy(kaT[sl][:], bankbf(bk_)[:, 0:128]), [BK(bk_)], [("kaT", sl)], cost=0.25)
            BA.put(bq, bk_)
            if T == NT - 2:
                dma("sp", o_swk_p, ka_f[:], ["ka_f"], [], "okp", is_out=True)
                dma("sp", o_swv_p, proj[:, 2688:2816], [("proj", 5)], [], "ovp", is_out=True)
            if samp:
                for b in range(16):
                    dma("sp", o_swk_s[b, 120:128, :], ka_f[b * 8:(b + 1) * 8, :], ["ka_f"], [], "osk%d" % (b % 4), is_out=True)
                    dma("sp", o_swv_s[b, 120:128, :], proj[b * 8:(b + 1) * 8, 2688:2816], [("proj", 5)], [], "osv%d" % (b % 4), is_out=True)
            blocks = []
            if samp:
                for b in range(16):
                    blocks.append((kcT[:, b, :], "kcT", Vc[:, b, :, :], "Vc", (colmask[:, b, :], masks[:, 3, :])))
                blocks.append((kaT[sl][:], ("kaT", sl), Vext[sl][:], ("Vext", sl), masks[:, 2, :]))
            else:
                if T > 0:
                    blocks.append((kaT[1 - sl][:], ("kaT", 1 - sl), Vext[1 - sl][:], ("Vext", 1 - sl), masks[:, 1, :]))
                blocks.append((kaT[sl][:], ("kaT", sl), Vext[sl][:], ("Vext", sl), masks[:, 0, :]))
            n = 0
            for kvh in range(2):
                ob = BA.get()
                for bi, (kT, kTt, Vx, Vt, mk) in enumerate(blocks):
                    sbk = BA.get()
                    pi = n % 2
                    n += 1
                    P.add("pe", lambda e, kT=kT, kvh=kvh, sbk=sbk: e.matmul(bank(sbk), lhsT=kT[kvh * 64:(kvh + 1) * 64, :], rhs=qaT[kvh * 64:(kvh + 1) * 64, :], start=True, stop=True),
                          [kTt, "qaT"], [BK(sbk)])
                    P.add("act", lambda e, sbk=sbk, pi=pi: e.activation(out=S.pT[pi][:], in_=bank(sbk), func=AF.Exp, scale=0.125), [BK(sbk)], [S.T("pT", pi)], cost=0.65)
                    BA.put(sbk)
                    for mk1 in (mk if isinstance(mk, tuple) else (mk,)):
                        P.ew(tt(v3(S.pT[pi][:], 128), v3(S.pT[pi][:], 128), bc_h(mk1, 4), mult), [S.T("pT", pi), "masks", "colmask"], [S.T("pT", pi)])

                    def pv(e, pi=pi, Vx=Vx, kvh=kvh, ob=ob, bi=bi, nb=len(blocks)):
                        for g in range(4):
                            ins = e.matmul(bank(ob)[:, g * 65:(g + 1) * 65], lhsT=S.pT[pi][:, g * 128:(g + 1) * 128], rhs=Vx[:, kvh, :],
                                           start=(bi == 0 and g == 0), stop=(bi == nb - 1 and g == 3), skip_group_check=True)
                        return ins
                    P.add("pe", pv, [S.T("pT", pi), Vt], [BK(ob)])
                o3 = bank(ob)[:, 0:260].rearrange("p (g d) -> p g d", d=65)
                P.add("dve", tt(S.den8[:, 0:4], o3[:, :, 64], esink[:, kvh * 4:(kvh + 1) * 4], add), [BK(ob), "esink"], [S.T("den8")])
                P.add("dve", lambda e: e.reciprocal(out=S.den8[:, 0:4], in_=S.den8[:, 0:4]), [S.T("den8")], [S.T("den8")])
                P.add("dve", tt(v3(S.mix_bf[:, kvh * 256:(kvh + 1) * 256], 64), o3[:, :, 0:64], bc_d(S.den8[:, 0:4], 64), mult), [BK(ob), S.T("den8")], [S.T("mix", kvh)])
                BA.put(ob)
            ckpt(5)
            q_lo, k_lo = S.qm_bf, S.om_bf
            QLO, KLO = [S.T("qm_bf")], [S.T("om_bf", 0), S.T("om_bf", 1)]
            hilo = (T == 0)
            A = rot_pair(S, v3(proj[:, 512:1024], 64), tb[:, 2, :], tb[:, 3, :], [("proj", 1), TB])
            if hilo:
                P.ew(tt(A, A, bc_d(dec[:, dv, 0, :], 64), mult), [S.T("tmpA"), "dec"], [S.T("tmpA")])
                P.ew(cpy(qr_bf[:], S.tmpA[:]), [S.T("tmpA")], ["qr_bf"])
                P.ew(tt(q_lo[:], S.tmpA[:], qr_bf[:], ALU.subtract), [S.T("tmpA"), "qr_bf"], QLO)
            else:
                P.ew(tt(v3(qr_bf[:], 64), A, bc_d(dec[:, dv, 0, :], 64), mult), [S.T("tmpA"), "dec"], ["qr_bf"])
            A = rot_pair(S, v3(proj[:, 1024:1536], 64), tb[:, 2, :], tb[:, 3, :], [("proj", 2), TB])
            if hilo:
                P.ew(tt(A, A, bc_d(dec[:, dv, 1, :], 64), mult), [S.T("tmpA"), "dec"], [S.T("tmpA")])
                P.ew(cpy(kr_bf[:], S.tmpA[:]), [S.T("tmpA")], ["kr_bf"])
                P.ew(tt(k_lo[:], S.tmpA[:], kr_bf[:], ALU.subtract), [S.T("tmpA"), "kr_bf"], KLO)
            else:
                P.ew(tt(v3(kr_bf[:], 64), A, bc_d(dec[:, dv, 1, :], 64), mult), [S.T("tmpA"), "dec"], ["kr_bf"])
            P.ew(cpy(vr_bf[:], proj[:, 1536:2048]), [("proj", 3)], ["vr_bf"])
            transposes([qr_bf[:, c * 128:(c + 1) * 128] for c in range(4)] + [kr_bf[:, c * 128:(c + 1) * 128] for c in range(4)],
                       ["qr_bf", "kr_bf"], qkT[:].rearrange("p a t -> p (a t)"), "qkT", 8)
            if hilo:
                transposes([q_lo[:, c * 128:(c + 1) * 128] for c in range(4)] + [k_lo[:, c * 128:(c + 1) * 128] for c in range(4)],
                           QLO + KLO, innT[:].rearrange("p a t -> p (a t)"), [("innT", 0), ("innT", 1)], 8)

            bi0, bi1 = BA.get(), BA.get()
            bis = (bi0, bi1)

            def inn(e):
                for c in range(4):
                    for hl in range(2):
                        pr = slice(hl * 64, (hl + 1) * 64)
                        o_ = bank(bis[hl])[:, c * 128:(c + 1) * 128]
                        ins = e.matmul(o_, lhsT=qkT[pr, 4 + c, :], rhs=qkT[pr, c, :], start=(c == 0), stop=(c == 3 and not hilo), skip_group_check=True)
                        if hilo:
                            e.matmul(o_, lhsT=qkT[pr, 4 + c, :], rhs=innT[pr, c, :], start=False, stop=False, skip_group_check=True)
                            ins = e.matmul(o_, lhsT=innT[pr, 4 + c, :], rhs=qkT[pr, c, :], start=False, stop=(c == 3), skip_group_check=True)
                return ins
            P.add("pe", inn, ["qkT"] + ([("innT", 0), ("innT", 1)] if hilo else []), [BK(bi0), BK(bi1)], cost=1.6 if hilo else 0.6)
            mk = masks[:, 2 if samp else 0, :]
            for hl in range(2):
                P.add("dve", tt(innT[:, hl * 4:(hl + 1) * 4, :], v3(bank(bis[hl]), 128), bc_h(mk, 4), mult), [BK(bis[hl]), "masks"], [("innT", hl)], cost=0.65)
            BA.put(bi0, bi1)
            b7 = BA.get()

            def orm(e):
                for h in range(8):
                    c, hl = h // 2, h % 2
                    ins = e.matmul(bank(b7)[:, h * 64:(h + 1) * 64], lhsT=innT[:, hl * 4 + c, :], rhs=vr_bf[:, h * 64:(h + 1) * 64],
                                   start=(h == 0), stop=False, skip_group_check=True)
                return ins
            P.add("pe", orm, [("innT", 0), ("innT", 1), "vr_bf"], [BK(b7)], cost=0.5)
            if not samp:
                cur = T % 2

                def crs(e):
                    for c in range(4):
                        ins = e.matmul(bank(b7)[:, c * 128:(c + 1) * 128], lhsT=qkT[:, c, :], rhs=S_bd[cur][:, c, :], start=False, stop=(c == 3), skip_group_check=True)
                    return ins
                P.add("pe", crs, ["qkT", ("S_bd", cur)], [BK(b7)], cost=0.3)
                bkv = BA.get()
                def kvm(e):
                    for c in range(4):
                        ins = e.matmul(bank(bkv)[:, c * 128:(c + 1) * 128], lhsT=kr_bf[:, c * 128:(c + 1) * 128], rhs=vr_bf[:, c * 128:(c + 1) * 128],
                                       start=(c == 0), stop=(c == 3), skip_group_check=True)
                    return ins
                P.add("pe", kvm, ["kr_bf", "vr_bf"], [BK(bkv)], cost=0.3)
                kv4 = v3(bank(bkv), 128)
                P.add("dve", tt(S_f[0:64], kv4[0:64, :, 0:64], S_f[0:64], add), [BK(bkv), "S_f"], ["S_f"], cost=0.3)
                P.add("dve", tt(S_f[64:128], kv4[64:128, :, 64:128], S_f[64:128], add), [BK(bkv), "S_f"], ["S_f"], cost=0.3)
                BA.put(bkv)
                P.ew(tt(S_f[:], S_f[:], bc_d(cdec[:, 0, :], 64), mult), ["S_f", "cdec"], ["S_f"], cost=0.5)
                nx = 1 - cur
                P.ew(cpy(S_bd[nx][0:64, :, 0:64], S_f[0:64]), ["S_f", ("S_bd", nx)], [("S_bd", nx)], cost=0.25)
                P.ew(cpy(S_bd[nx][64:128, :, 64:128], S_f[64:128]), ["S_f", ("S_bd", nx)], [("S_bd", nx)], cost=0.25)
                if T == NT - 2:
                    dma("sp", o_ret_p.rearrange("(c hl) d e -> (hl d) c e", hl=2), S_f[:], ["S_f"], [], "orp", is_out=True)
            else:
                for b in range(16):
                    s2 = b % 2
                    S0 = stg[s2][:, 0:256].rearrange("p (c e) -> p c e", e=64)
                    dma("sp", S0, s_ret[b].rearrange("(c hl) d e -> (hl d) c e", hl=2), [], [("stg", s2)], "sg%d" % s2)
                    P.ew(cpy(S_bd[s2][0:64, :, 0:64], S0[0:64]), [("stg", s2), ("S_bd", s2)], [("S_bd", s2)], cost=0.25)
                    P.ew(cpy(S_bd[s2][64:128, :, 64:128], S0[64:128]), [("stg", s2), ("S_bd", s2)], [("S_bd", s2)], cost=0.25)
                    padb = (qa_bf, S.qm_bf)
                    padt = ("qa_bf", S.T("qm_bf"))[s2]
                    P.ew(tt(v3(padb[s2][:], 128), qkT[:, 0:4, :], bc_h(colmask[:, b, :], 4), mult), ["qkT", "colmask"], [padt])

                    def crs(e, b=b, s2=s2, padb=padb):
                        for c in range(4):
                            ins = e.matmul(bank(b7)[:, c * 128:(c + 1) * 128], lhsT=padb[s2][:, c * 128:(c + 1) * 128], rhs=S_bd[s2][:, c, :],
                                           start=False, stop=(b == 15 and c == 3), skip_group_check=True)
                        return ins
                    P.add("pe", crs, [padt, ("S_bd", s2)], [BK(b7)])
                    P.ew(lambda e, b=b, s2=s2: e.tensor_scalar(out=S.pT[s2][:], in0=kr_bf[:], scalar1=rowmask[:, b:b + 1], scalar2=None, op0=mult),
                         ["kr_bf", "rowmask"], [S.T("pT", s2)])
                    kb = BA.get()

                    def kvm(e, s2=s2, kb=kb):
                        for c in range(4):
                            ins = e.matmul(bank(kb)[:, c * 128:(c + 1) * 128], lhsT=S.pT[s2][:, c * 128:(c + 1) * 128], rhs=vr_bf[:, c * 128:(c + 1) * 128],
                                           start=(c == 0), stop=(c == 3), skip_group_check=True)
                        return ins
                    P.add("pe", kvm, [S.T("pT", s2), "vr_bf"], [BK(kb)])
                    kv4 = v3(bank(kb), 128)
                    So = S.tmpC[:, s2 * 256:(s2 + 1) * 256].rearrange("p (c e) -> p c e", e=64)
                    SoT = S.T("tmpC2", s2)
                    P.add("dve", tt(So[0:64], kv4[0:64, :, 0:64], S0[0:64], add), [BK(kb), ("stg", s2), S.T("tmpC")], [SoT])
                    P.add("dve", tt(So[64:128], kv4[64:128, :, 64:128], S0[64:128], add), [BK(kb), ("stg", s2), SoT], [SoT])
                    BA.put(kb)
                    P.ew(tt(So, So, bc_d(cdec[:, 1, :], 64), mult), [SoT, "cdec"], [SoT], cost=0.5)
                    dma("sp", o_ret_s[b].rearrange("(c hl) d e -> (hl d) c e", hl=2), So, [SoT], [], "ors%d" % s2, is_out=True)
            P.add("act", lambda e: e.activation(out=S.tmpA[:], in_=bank(b7), func=AF.Square), [BK(b7)], [S.T("tmpA")], cost=0.65)
            P.add("dve", lambda e: e.tensor_reduce(out=S.st8[:], in_=v3(S.tmpA[:], 64), axis=AX.X, op=add), [S.T("tmpA")], [S.T("st8")])
            rstd_chain(S, None, S.st8[:], S.rh8[:], 8, 1.0 / 64, [S.T("st8")], [S.T("rh8")])
            P.add("dve", tt(v3(S.tmpB[:], 64), v3(bank(b7), 64), bc_d(S.rh8[:], 64), mult), [BK(b7), S.T("rh8")], [S.T("tmpB", 0), S.T("tmpB", 1)], cost=0.65)
            BA.put(b7)
            P.add("act", lambda e: e.activation(out=S.tmpC[:], in_=proj[:, 2048:2560], func=AF.Silu), [("proj", 4)], [S.T("tmpC"), S.T("tmpC2", 0), S.T("tmpC2", 1)])
            P.ew(tt(S.mix_bf[:, 512:1024], S.tmpB[:], S.tmpC[:], mult), [S.T("tmpB", 0), S.T("tmpB", 1), S.T("tmpC")], [S.T("mix", 2)])
            ckpt(6)
            transposes([S.mix_bf[:, k * 128:(k + 1) * 128] for k in range(8)], [S.T("mix", 0), S.T("mix", 1), S.T("mix", 2)], S.xnT[:].rearrange("p k t -> p (k t)"), S.XN, 8)

            for hf in range(2):
                by = BA.get()

                def mmo(e, hf=hf, by=by):
                    for k in range(8):
                        ins = e.matmul(bank(by), lhsT=S.xnT[:, k, :], rhs=W_out[:, k, hf * 512:(hf + 1) * 512], start=(k == 0), stop=(k == 7))
                    return ins
                P.add("pe", mmo, S.XN + ["W_out"], [BK(by)], cost=1.75)
                P.add("dve", tt(X[:, hf * 512:(hf + 1) * 512], X[:, hf * 512:(hf + 1) * 512], bank(by), add), [XT, BK(by)], [XT], cost=0.65)
                BA.put(by)

        def tile_cross(T, i, xi):
            samp = (T == NT - 1)
            S = SETS[i % 2]
            X = xres[xi]; XT = ("xres", xi)
            norm_T(S, X[:], XT, 1, S.xnT, S.T("xnT"), defer=True)

            bq_ = BA.get()

            def mmq(e):
                for k in range(8):
                    ins = e.matmul(bank(bq_), lhsT=S.xnT[:, k, :], rhs=W_mq[:, k, :], start=(k == 0), stop=(k == 7))
                return ins
            P.add("pe", mmq, S.XN + ["W_mq"], [BK(bq_)], cost=1.75)
            P.add("act", lambda e: e.activation(out=S.tmpA[:], in_=bank(bq_), func=AF.Copy, scale=S.rs[:]), [BK(bq_), S.T("rs")], [S.T("tmpA")], cost=0.75)
            BA.put(bq_)
            A3 = v3(S.tmpA[:], 128)
            head_rstd(S, A3, 4, 128, [S.T("tmpA")], S.tmpC, S.T("tmpC"))
            P.ew(tt(A3, A3, bc_d(S.rh8[:, 0:4], 128), mult), [S.T("tmpA"), S.T("rh8")], [S.T("tmpA")])
            P.ew(tt(v3(S.qm_bf[:], 128), A3, bc_h(gqm[:], 4), mult), [S.T("tmpA"), "gqm"], [S.T("qm_bf")])
            transposes([S.qm_bf[:, h * 128:(h + 1) * 128] for h in range(4)], [S.T("qm_bf")], S.qmT[:], S.T("qmT"), 4)
            SC = 128.0 ** -0.5
            nb = 16 if samp else 1
            bo = (BA.get(), BA.get())
            for b in range(nb):
                if samp:
                    s2 = b % 2
                    kb = stg_bf[:] if s2 == 0 else stg[0][:].bitcast(BF16)[:, 0:1024]
                    kb_tok = "stg_bf" if s2 == 0 else ("stg", 0)
                    vb = Vmb[:] if s2 == 0 else stg[1][:].bitcast(BF16)[:, 0:1032].rearrange("p (mc h d) -> p mc h d", mc=2, h=4)
                    vb_tok = "Vmb" if s2 == 0 else ("stg", 1)
                    if b == 1:
                        P.add("pool", lambda e, vb=vb: e.memset(vb, 1.0), [], [vb_tok])
                    dma("pool", kb.rearrange("p (mc f) -> p mc f", mc=2), c_mk[b].rearrange("(mc m) f -> m mc f", m=128), [], [kb_tok], "pk%d" % s2)
                    for mc in range(2):
                        dma("pool", vb[:, mc, :, 0:128], c_mv[b, mc * 128:(mc + 1) * 128, :].rearrange("m (h d) -> m h d", h=4), [vb_tok], [vb_tok], "pv%d" % s2)
                    bt = BAS[1].get()

                    def trk(e, bt=bt, kb=kb):
                        for h in range(4):
                            for mc in range(2):
                                s8 = h * 2 + mc
                                ins = e.transpose(out=bankbf(bt)[:, s8 * 128:(s8 + 1) * 128], in_=kb[:, mc * 512 + h * 128: mc * 512 + (h + 1) * 128], identity=ident[:])
                        return ins
                    P.add("pe", trk, [kb_tok, "ident"], [BK(bt)], cost=0.7)
                    kmb = kmTb[:] if s2 == 0 else qkT[:].rearrange("p a t -> p (a t)")
                    kmb_tok = "kmTb" if s2 == 0 else "qkT"
                    P.add("act", lambda e, bt=bt, kmb=kmb: e.activation(out=kmb, in_=bankbf(bt), func=AF.Copy), [BK(bt)], [kmb_tok], cost=1.1)
                    BA.put(bt)
                    kt_tok, v_tok = kmb_tok, vb_tok
                    kt_of = lambda h, mc, kmb=kmb: kmb[:, (h * 2 + mc) * 128:(h * 2 + mc + 1) * 128]
                    v_of = lambda h, mc, vb=vb: vb[:, mc, h, :]
                else:
                    kt_tok, v_tok = "kmT", "Vm"
                    kt_of = lambda h, mc: kmT[:, h, mc * 128:(mc + 1) * 128]
                    v_of = lambda h, mc: Vm[:, mc, h, :]

                bs = (BAS[1].get(), BAS[1].get()) if samp else (BA.get(), BA.get())
                if samp and b % 2 == 1:
                    pm = (S.pT[0][:], S.pT[1][:]); pm_tok = [S.T("pT", 0), S.T("pT", 1)]
                else:
                    pm = (S.pmT[:, 0:512], S.pmT[:, 512:1024]); pm_tok = [S.T("pmT", 0), S.T("pmT", 1)]

                def scm(e, kt_of=kt_of, bs=bs):
                    for h in range(4):
                        for mc in range(2):
                            s8 = h * 2 + mc
                            ins = e.matmul(bank(bs[s8 // 4])[:, (s8 % 4) * 128:(s8 % 4 + 1) * 128], lhsT=kt_of(h, mc), rhs=S.qmT[:, h * 128:(h + 1) * 128],
                                           start=(s8 % 4 == 0), stop=(s8 % 4 == 3), skip_group_check=True)
                    return ins
                P.add("pe", scm, [kt_tok, S.T("qmT")], [BK(bs[0]), BK(bs[1])], cost=0.7)
                for hb in range(2):
                    P.add("act", lambda e, hb=hb, bs=bs, pm=pm: e.activation(out=pm[hb], in_=bank(bs[hb]), func=AF.Exp, scale=SC), [BK(bs[hb])], [pm_tok[hb]], cost=0.65)
                BA.put(*bs)
                if samp:
                    for hb in range(2):
                        P.ew(tt(v3(pm[hb], 128), v3(pm[hb], 128), bc_h(colmask[:, b, :], 4), mult), [pm_tok[hb], "colmask"], [pm_tok[hb]], cost=1)

                def pvm(e, b=b, v_of=v_of, pm=pm):
                    for h in range(4):
                        for mc in range(2):
                            s8 = h * 2 + mc
                            ins = e.matmul(bank(bo[h // 2])[:, (h % 2) * 256:(h % 2) * 256 + 129], lhsT=pm[s8 // 4][:, (s8 % 4) * 128:(s8 % 4 + 1) * 128], rhs=v_of(h, mc),
                                           start=(b == 0 and h % 2 == 0 and mc == 0), stop=(b == nb - 1 and h % 2 == 1 and mc == 1), skip_group_check=True)
                    return ins
                P.add("pe", pvm, pm_tok + [v_tok], [BK(bo[0]), BK(bo[1])], cost=0.8)
            for hh in range(2):
                om2 = bank(bo[hh]).rearrange("p (h d) -> p h d", d=256)
                P.add("dve", lambda e, hh=hh, om2=om2: e.reciprocal(out=S.den8[:, 4 + 2 * hh:6 + 2 * hh], in_=om2[:, :, 128]), [BK(bo[hh])], [S.T("den8b", hh)], cost=0.15)
                P.add("dve", tt(v3(S.om_bf[:, hh * 256:(hh + 1) * 256], 128), om2[:, :, 0:128], bc_d(S.den8[:, 4 + 2 * hh:6 + 2 * hh], 128), mult),
                      [BK(bo[hh]), S.T("den8b", hh)], [S.T("om_bf", hh)], cost=0.4)
            BA.put(*bo)
            transposes([S.om_bf[:, h * 128:(h + 1) * 128] for h in range(4)], [S.T("om_bf", 0), S.T("om_bf", 1)], S.omT[:], S.T("omT"), 4)
            for hf in range(2):
                by = BA.get()

                def mmo(e, hf=hf, by=by):
                    for c in range(4):
                        ins = e.matmul(bank(by), lhsT=S.omT[:, c * 128:(c + 1) * 128], rhs=W_mo[:, c, hf * 512:(hf + 1) * 512], start=(c == 0), stop=(c == 3))
                    return ins
                P.add("pe", mmo, [S.T("omT"), "W_mo"], [BK(by)], cost=0.9)
                P.add("dve", tt(X[:, hf * 512:(hf + 1) * 512], X[:, hf * 512:(hf + 1) * 512], bank(by), add), [XT, BK(by)], [XT], cost=0.65)
                BA.put(by)
            norm_T(S, X[:], XT, 2, xn3T[:, :, i * 128:(i + 1) * 128], ("xn3T", i))

        def ffn_group(g):
            tiles = GROUPS[g]
            nt_ = len(tiles)
            xo = (g % 2) * GRP
            XR = [("xn3T", i, h) for i in range(nt_) for h in range(2)]
            NTOK = nt_ * 128
            if (NT - 1) in tiles:
                ffn_load_d(2)
            if (NT - 1) in tiles or 0 in tiles:
                for j_ in range(3, NWG):
                    ffn_load(j_)
            yb = []
            for i in range(nt_):
                yb += [BAS[i % 2].get(), BAS[i % 2].get()]
            for j in range(NJ):
                s = j % NWG
                hs = j % 2
                bg, bu = BAS[0].get(), BAS[1].get()

                def gmm(e, s=s, bg=bg):
                    for k in range(8):
                        ins = e.matmul(bank(bg, NTOK), lhsT=WGV[s][:, k, 0:128], rhs=xn3T[:, k, 0:NTOK], start=(k == 0), stop=(k == 7))
                    return ins

                def umm(e, s=s, bu=bu):
                    for k in range(8):
                        ins = e.matmul(bank(bu, NTOK), lhsT=WGV[s][:, k, 128:256], rhs=xn3T[:, k, 0:NTOK], start=(k == 0), stop=(k == 7))
                    return ins
                P.add("pe", gmm, XR + [WGT[s]], [BK(bg)], cost=0.05 + 8 * max(NTOK, 128) / 2400.0)
                P.add("pe", umm, XR + [WGT[s]], [BK(bu)], cost=0.05 + 8 * max(NTOK, 128) / 2400.0)
                P.add("act", lambda e, hs=hs, bg=bg: e.activation(out=sgb[hs][:, 0:NTOK], in_=bank(bg, NTOK), func=AF.Silu), [BK(bg)], [("sgb", hs)], cost=0.45)
                P.add("dve", tt(hT[hs][:, 0:NTOK], sgb[hs][:, 0:NTOK], bank(bu, NTOK), mult), [("sgb", hs), BK(bu)], [("hT", hs)], cost=0.45)
                BA.put(bg, bu)

                def dn(e, s=s, hs=hs, j=j):
                    for i in range(nt_):
                        for hf in range(2):
                            ins = e.matmul(bank(yb[2 * i + hf]), lhsT=hT[hs][:, i * 128:(i + 1) * 128], rhs=wdb[j % 3][:, hf * 512:(hf + 1) * 512],
                                           start=(j == 0), stop=(j == NJ - 1))
                    return ins
                P.add("pe", dn, [("hT", hs), WDT[j % 3]], [BK(b_) for b_ in yb], cost=0.05 + 2 * nt_ * 0.22)
                if j + NWG < NJ:
                    ffn_load(j + NWG)
                if j + 3 < NJ:
                    ffn_load_d(j + 3)
            for i, T in enumerate(tiles):
                xi = xo + i
                for hf in range(2):
                    P.add("dve", tt(xres[xi][:, hf * 512:(hf + 1) * 512], xres[xi][:, hf * 512:(hf + 1) * 512], bank(yb[2 * i + hf]), add),
                          [("xres", xi), BK(yb[2 * i + hf])], [("xres", xi)], cost=0.65)
                dst = y_s if T == NT - 1 else y_p[T * 128:(T + 1) * 128, :]
                dma("sp", dst, xres[xi][:], [("xres", xi)], [], "yo%d" % xi, is_out=True)
            BA.put(*yb)

        def main_loop():
          ckpt(3)
          for g in range(NG):
              xo = (g % 2) * GRP
              for i, T in enumerate(GROUPS[g]):
                  src = x_s if T == NT - 1 else x_p[T * 128:(T + 1) * 128, :]
                  dma("sp", xres[xo + i][:], src, [], [("xres", xo + i)], "xl%d" % (xo + i))
              has_samp = (NT - 1) in GROUPS[g]
              uses_stg = has_samp or (0 in GROUPS[g])
              for j in range(3 if uses_stg else NWG):
                  ffn_load(j)
              for j in range(2 if has_samp else 3):
                  ffn_load_d(j)
              for i, T in enumerate(GROUPS[g]):
                  if T == NT - 1:
                      sample_preload()
                  tile_mixer(T, i, xo + i)
                  ckpt(7)
                  tile_cross(T, i, xo + i)
                  ckpt(8)
              ckpt(9)
              ffn_group(g)
              ckpt(10 + g)
        try:
            main_loop()
        except _Stop:
            pass
        P.emit(nc, st)
    return nc


_CACHE = {}


def kernel(x_prompt, x_sample, mem_prompt, cache_swa_k, cache_swa_v, state_ret, cache_mem_k, cache_mem_v,
           norm_mix, w_in, q_norm_a, k_norm_a, sinks, w_out, norm_cross, norm_mem, w_mq, w_mkv,
           q_norm_m, k_norm_m, w_mo, norm_ffn, w_gu, w_down):
    f = lambda a: np.ascontiguousarray(np.asarray(a, dtype=np.float32))
    if "nc" not in _CACHE:
        _CACHE["nc"] = build_program()
        _CACHE["consts"] = host_consts()
    nc = _CACHE["nc"]
    consts = _CACHE["consts"]
    shared = {
        "w_in": f(w_in)[0], "w_out": f(w_out)[0], "w_mq": f(w_mq)[0], "w_mkv": f(w_mkv)[0], "w_mo": f(w_mo)[0],
        "w_gu": f(w_gu)[0], "w_down": f(w_down)[0],
        "norm_mix": f(norm_mix), "norm_cross": f(norm_cross), "norm_mem": f(norm_mem), "norm_ffn": f(norm_ffn),
        "q_norm_a": f(q_norm_a), "k_norm_a": f(k_norm_a), "sinks": f(sinks), "q_norm_m": f(q_norm_m), "k_norm_m": f(k_norm_m),
    }
    shared.update(consts)
    xp, xs, mp = f(x_prompt), f(x_sample), f(mem_prompt)
    ck, cv, sr, cmk, cmv = f(cache_swa_k)[0], f(cache_swa_v)[0], f(state_ret)[0], f(cache_mem_k)[0], f(cache_mem_v)[0]
    in_maps = []
    for c in range(NCORES):
        sl = slice(16 * c, 16 * (c + 1))
        m = dict(shared)
        m["x_p"] = xp[c]
        m["x_s"] = np.ascontiguousarray(xs[sl].reshape(128, 1024))
        m["mem_p"] = mp[c]
        m["c_k"] = np.ascontiguousarray(ck[sl].reshape(16, 128, 128))
        m["c_v"] = np.ascontiguousarray(cv[sl].reshape(16, 128, 128))
        m["s_ret"] = np.ascontiguousarray(sr[sl])
        m["c_mk"] = np.ascontiguousarray(cmk[sl].reshape(16, 256, 512))
        m["c_mv"] = np.ascontiguousarray(cmv[sl].reshape(16, 256, 512))
        in_maps.append(m)
    res = run_bass_kernel_spmd(nc, in_maps, core_ids=list(range(NCORES)))
    R = res.results
    cat = lambda k: np.stack([np.asarray(r[k], dtype=np.float32) for r in R], 0)
    y_prompt = cat("y_p").reshape(8, 4096, 1024)
    y_sample = cat("y_s").reshape(128, 8, 1024)
    swk_p = cat("o_swk_p").reshape(1, 8, 128, 2, 64)
    swv_p = cat("o_swv_p").reshape(1, 8, 128, 2, 64)
    ret_p = cat("o_ret_p").reshape(1, 8, 8, 64, 64)
    mk_p = cat("o_mk_p").reshape(1, 8, 256, 4, 128)
    mv_p = cat("o_mv_p").reshape(1, 8, 256, 4, 128)
    swk_s = cat("o_swk_s").reshape(1, 128, 128, 2, 64)
    swv_s = cat("o_swv_s").reshape(1, 128, 128, 2, 64)
    ret_s = cat("o_ret_s").reshape(1, 128, 8, 64, 64)
    return (y_prompt, y_sample, swk_p, swv_p, ret_p, mk_p, mv_p, swk_s, swv_s, ret_s)
```

```python
import contextlib
import numpy as np
import ml_dtypes
import concourse.bass as bass
import concourse.mybir as mybir
from concourse.bass_utils import run_bass_kernel_spmd

F32 = mybir.dt.float32
BF16 = mybir.dt.bfloat16
ALU = mybir.AluOpType
AF = mybir.ActivationFunctionType
AX = mybir.AxisListType

NCORES = 8
NT = 33
GRP = 2
GROUPS = [list(range(g, min(g + GRP, NT - 1))) for g in range(0, NT - 1, GRP)] + [[NT - 1]]
NG = len(GROUPS)
NJ = 22
EPS = 1e-6


class Op:
    __slots__ = ("idx", "stream", "fn", "deps", "chan", "is_dma", "sig", "sigval", "chanval", "cost", "nbytes")


class Prog:
    STREAMS = ("pe", "act", "dve", "pool", "sp")

    def __init__(self):
        self.ops = []
        self.tok = {}
        self.chan_n = {}
        self.chan_last = {}
        self.out_dmas = []
        self.cost = {"dve": 0.0, "pool": 0.0}

    DEFCOST = {"pe": 0.5, "act": 0.6, "dve": 0.5, "pool": 0.7, "sp": 0.1}

    def add(self, stream, fn, reads=(), writes=(), chan=None, is_out=False, cost=None, nbytes=0):
        op = Op()
        op.cost = self.DEFCOST[stream] if cost is None else cost
        op.nbytes = nbytes
        op.idx = len(self.ops)
        op.stream = stream
        op.fn = fn
        op.chan = chan
        op.is_dma = chan is not None
        op.sig = False
        op.sigval = 0
        op.chanval = 0
        deps = {}
        for t in reads:
            st = self.tok.setdefault(t, [None, []])
            if st[0] is not None:
                deps[st[0]] = "raw"
            if isinstance(t, tuple) and t[0] == "bk":
                for r in st[1]:
                    if self.ops[r].stream != stream:
                        deps.setdefault(r, "rr")
        for t in writes:
            st = self.tok.setdefault(t, [None, []])
            if st[0] is not None:
                deps.setdefault(st[0], "waw")
            for r in st[1]:
                deps.setdefault(r, "war")
        for t in reads:
            self.tok[t][1].append(op.idx)
        for t in writes:
            self.tok[t] = [op.idx, []]
        deps.pop(op.idx, None)
        op.deps = deps
        if op.is_dma:
            prev = self.chan_last.get(chan)
            if prev is not None:
                deps.setdefault(prev, "chan")
            self.chan_last[chan] = op.idx
            self.chan_n[chan] = self.chan_n.get(chan, 0) + 1
            op.chanval = 16 * self.chan_n[chan]
            if is_out:
                self.out_dmas.append(op.idx)
        self.ops.append(op)
        return op

    def ew(self, fn, reads=(), writes=(), cost=1.0, only=None):
        if only is None:
            eng = "dve" if self.cost["dve"] <= self.cost["pool"] + 2.0 * cost else "pool"
        else:
            eng = only
        self.cost[eng] += cost * (2.0 if eng == "pool" else 1.0)
        return self.add(eng, fn, reads, writes, cost=(0.15 + 1.0 * cost) if eng == "pool" else (0.1 + 0.6 * cost))

    do_schedule = True
    LAT = 0.45
    DMA_FIXED = 2.0
    DMA_BW = 300e3

    def schedule(self):
        import heapq
        ops = self.ops
        n = len(ops)
        succ = [[] for _ in range(n)]
        indeg = [0] * n
        for o in ops:
            for d in o.deps:
                succ[d].append(o.idx)
                indeg[o.idx] += 1
        dur = [(o.cost if not o.is_dma else self.DMA_FIXED + o.nbytes / self.DMA_BW) for o in ops]
        prio = [0.0] * n
        for i in range(n - 1, -1, -1):
            m = 0.0
            for s_ in succ[i]:
                if prio[s_] > m:
                    m = prio[s_]
            prio[i] = dur[i] + m
        eng_free = {s_: 0.0 for s_ in self.STREAMS}
        dma_free = 0.0
        finish = [0.0] * n
        ready_t = [0.0] * n
        avail = [i for i in range(n) if indeg[i] == 0]
        order = []
        while avail:
            best, bkey = None, None
            for i in avail:
                o = ops[i]
                st_ = max(eng_free[o.stream], ready_t[i])
                key = (int(st_ / 0.35), -prio[i], i)
                if bkey is None or key < bkey:
                    best, bkey = i, key
            avail.remove(best)
            o = ops[best]
            st_ = max(eng_free[o.stream], ready_t[best])
            if o.is_dma:
                issue = 1.0 if o.stream == "pool" else 0.08
                eng_free[o.stream] = st_ + issue
                t0 = max(st_ + issue, dma_free)
                dma_free = t0 + o.nbytes / self.DMA_BW
                finish[best] = dma_free + self.DMA_FIXED
            else:
                eng_free[o.stream] = st_ + o.cost
                finish[best] = st_ + o.cost
            order.append(best)
            for s_ in succ[best]:
                r = finish[best] + self.LAT
                if r > ready_t[s_]:
                    ready_t[s_] = r
                indeg[s_] -= 1
                if indeg[s_] == 0:
                    avail.append(s_)
        assert len(order) == n
        self.model_time = max(finish)
        return order

    def _needs_wait(self, c, p, kind):
        if p.is_dma:
            return True
        if c.stream == p.stream:
            if c.is_dma:
                return True
            if c.stream == "pe":
                return False
            return True
        return True

    def emit(self, nc, stack):
        ops = self.ops
        for c in ops:
            for d, kind in c.deps.items():
                if self._needs_wait(c, ops[d], kind):
                    ops[d].sig = True
        cnt = {s: 0 for s in self.STREAMS}
        order = self.schedule() if self.do_schedule else list(range(len(ops)))
        self._order = order
        for o in (ops[i] for i in order):
            if not o.is_dma and o.sig:
                cnt[o.stream] += 1
                o.sigval = cnt[o.stream]
        esem = {s: stack.enter_context(nc.semaphore("s_" + s)) for s in self.STREAMS if s != "sp"}
        csem = {c: stack.enter_context(nc.semaphore("c_" + c)) for c in self.chan_n}
        block = stack.enter_context(nc.Block())
        streams = {s: [ops[i] for i in order if ops[i].stream == s] for s in self.STREAMS}

        def run_stream(sname, eng):
            waited = {}
            for o in streams[sname]:
                for d, kind in o.deps.items():
                    p = ops[d]
                    if not self._needs_wait(o, p, kind):
                        continue
                    if p.is_dma:
                        sem, val, key = csem[p.chan], p.chanval, "c_" + p.chan
                    else:
                        sem, val, key = esem[p.stream], p.sigval, "e_" + p.stream
                    if waited.get(key, 0) >= val:
                        continue
                    waited[key] = val
                    eng.wait_ge(sem, val)
                ins = o.fn(eng)
                if o.is_dma:
                    ins.then_inc(csem[o.chan], 16)
                elif o.sig:
                    ins.then_inc(esem[o.stream], 1)
            if sname == "sp":
                done = set()
                for d in self.out_dmas:
                    p = ops[d]
                    if p.chan in done:
                        continue
                    done.add(p.chan)
                    eng.wait_ge(csem[p.chan], 16 * self.chan_n[p.chan])

        @block.tensor
        def _(e):
            run_stream("pe", e)

        @block.scalar
        def _(e):
            run_stream("act", e)

        @block.vector
        def _(e):
            run_stream("dve", e)

        @block.gpsimd
        def _(e):
            run_stream("pool", e)

        @block.sync
        def _(e):
            run_stream("sp", e)


def host_consts():
    bf = ml_dtypes.bfloat16
    k = np.arange(128)[:, None]
    q = np.arange(128)[None, :]
    masks = np.zeros((128, 4, 128), np.float32)
    masks[:, 0, :] = (k <= q)
    masks[:, 1, :] = (k > q)
    masks[:, 2, :] = (k // 8 == q // 8) & (k % 8 <= q % 8)
    masks[:, 3, :] = (k > q % 8)
    colmask = np.zeros((128, 16, 128), np.float32)
    for b in range(16):
        colmask[:, b, :] = (q // 8 == b)
    rowmask = np.zeros((128, 16), np.float32)
    for b in range(16):
        rowmask[:, b] = (np.arange(128) // 8 == b)
    f32 = np.float32
    tabs = np.zeros((NT, 128, 4, 64), np.float32)
    inv_a = (f32(1.0) / (f32(10000.0) ** (np.arange(32, dtype=f32) / f32(32)))).astype(f32)
    inv_r = (f32(10000.0) ** (-np.linspace(0.0, 1.0, 32, dtype=f32))).astype(f32)
    p = np.arange(128)
    for t in range(NT):
        pos = (t * 128 + p) if t < 32 else (16384 + p % 8)
        pos = pos.astype(f32)
        ang = (pos[:, None] * inv_a[None, :]).astype(f32).astype(np.float64)
        c, s = np.cos(ang), np.sin(ang)
        tabs[t, :, 0, :] = np.concatenate([c, c], -1)
        tabs[t, :, 1, :] = np.concatenate([-s, s], -1)
        ang = (pos[:, None] * inv_r[None, :]).astype(f32).astype(np.float64)
        c, s = np.cos(ang), np.sin(ang)
        tabs[t, :, 2, :] = np.stack([c, c], -1).reshape(128, 64)
        tabs[t, :, 3, :] = np.stack([-s, s], -1).reshape(128, 64)
    log_g = np.log(1.0 - np.exp2(-5.0 - np.arange(8, dtype=np.float64)))
    dec = np.zeros((128, 2, 2, 8), np.float32)
    for v, pp in enumerate([p, p % 8]):
        dec[:, v, 0, :] = np.exp((pp[:, None] + 1.0) * log_g[None, :])
        dec[:, v, 1, :] = np.exp(-(pp[:, None] + 1.0) * log_g[None, :]) / 8.0
    cdec = np.zeros((128, 2, 4), np.float32)
    for c_ in range(4):
        h = 2 * c_ + p // 64
        cdec[:, 0, c_] = np.exp(128.0 * log_g[h])
        cdec[:, 1, c_] = np.exp(8.0 * log_g[h])
    return {
        "c_ident": np.eye(128).astype(bf),
        "c_masks": masks.astype(bf),
        "c_colmask": colmask.astype(bf),
        "c_rowmask": rowmask,
        "c_tabs": tabs,
        "c_dec": dec,
        "c_cdec": cdec,
    }


class _Stop(Exception):
    pass


def build_program(stage=99, skip=()):
    nc = bass.Bass("TRN2", target_bir_lowering=False)
    P = Prog()

    def ckpt(k):
        if stage <= k:
            raise _Stop()
    din = lambda n, s, d=F32: nc.dram_tensor(n, list(s), d, kind="ExternalInput").ap()
    dout = lambda n, s: nc.dram_tensor(n, list(s), F32, kind="ExternalOutput").ap()
    x_p = din("x_p", [4096, 1024]); x_s = din("x_s", [128, 1024]); mem_p = din("mem_p", [256, 1024])
    c_k = din("c_k", [16, 128, 128]); c_v = din("c_v", [16, 128, 128]); s_ret = din("s_ret", [16, 8, 64, 64])
    c_mk = din("c_mk", [16, 256, 512]); c_mv = din("c_mv", [16, 256, 512])
    w_in = din("w_in", [1024, 2816]); w_out = din("w_out", [1024, 1024]); w_mq = din("w_mq", [1024, 512])
    w_mkv = din("w_mkv", [1024, 1024]); w_mo = din("w_mo", [512, 1024]); w_gu = din("w_gu", [1024, 5632])
    w_down = din("w_down", [2816, 1024])
    norm_mix = din("norm_mix", [1, 1024]); norm_cross = din("norm_cross", [1, 1024])
    norm_mem = din("norm_mem", [1, 1024]); norm_ffn = din("norm_ffn", [1, 1024])
    q_norm_a = din("q_norm_a", [1, 64]); k_norm_a = din("k_norm_a", [1, 64]); sinks = din("sinks", [1, 8])
    q_norm_m = din("q_norm_m", [1, 128]); k_norm_m = din("k_norm_m", [1, 128])
    c_ident = din("c_ident", [128, 128], BF16); c_masks = din("c_masks", [128, 4, 128], BF16)
    c_colmask = din("c_colmask", [128, 16, 128], BF16); c_rowmask = din("c_rowmask", [128, 16])
    c_tabs = din("c_tabs", [NT, 128, 4, 64]); c_dec = din("c_dec", [128, 2, 2, 8]); c_cdec = din("c_cdec", [128, 2, 4])
    y_p = dout("y_p", [4096, 1024]); y_s = dout("y_s", [128, 1024])
    o_swk_p = dout("o_swk_p", [128, 128]); o_swv_p = dout("o_swv_p", [128, 128]); o_ret_p = dout("o_ret_p", [8, 64, 64])
    o_mk_p = dout("o_mk_p", [256, 512]); o_mv_p = dout("o_mv_p", [256, 512])
    o_swk_s = dout("o_swk_s", [16, 128, 128]); o_swv_s = dout("o_swv_s", [16, 128, 128]); o_ret_s = dout("o_ret_s", [16, 8, 64, 64])
    wgu_d = nc.dram_tensor("wgu_d", [NJ, 128, 2048], BF16, kind="Internal").ap()
    wd_d = nc.dram_tensor("wd_d", [NJ, 128, 1024], BF16, kind="Internal").ap()

    def bc_row(ap, n):
        return bass.AP(ap.tensor, 0, [[0, 128], [1, n]])

    with contextlib.ExitStack() as st:
        st.enter_context(nc.allow_non_contiguous_dma(reason="small strided constant / cache-row loads"))
        sb = lambda n, s, d=F32: st.enter_context(nc.sbuf_tensor(n, list(s), d))
        W_in = sb("W_in", [128, 8, 2816], BF16); W_out = sb("W_out", [128, 8, 1024], BF16)
        W_mq = sb("W_mq", [128, 8, 512], BF16); W_mo = sb("W_mo", [128, 4, 1024], BF16)
        ident = sb("ident", [128, 128], BF16); masks = sb("masks", [128, 4, 128], BF16)
        ident32 = sb("ident32", [128, 128], F32)
        colmask = sb("colmask", [128, 16, 128], BF16); rowmask = sb("rowmask", [128, 16])
        dec = sb("dec", [128, 2, 2, 8]); cdec = sb("cdec", [128, 2, 4]); nh = sb("nh", [128, 8])
        gcols = sb("gcols", [128, 4, 8])
        gqa = sb("gqa", [128, 2, 64]); gka = sb("gka", [128, 2, 64])
        gqm = sb("gqm", [128, 128]); gkm = sb("gkm", [128, 128]); esink = sb("esink", [128, 8])
        xres = [sb("xres%d" % i, [128, 1024]) for i in range(2 * GRP)]

        class _Set:
            def __init__(self, n):
                self.n = n
                self.xn = sb("xn%d" % n, [128, 1024], BF16)
                self.xnT = sb("xnT%d" % n, [128, 8, 128], BF16)
                self.ss = sb("ss%d" % n, [128, 1]); self.rs = sb("rs%d" % n, [128, 1])
                self.tmpA = sb("tmpA%d" % n, [128, 512]); self.tmpB = sb("tmpB%d" % n, [128, 512]); self.tmpC = sb("tmpC%d" % n, [128, 512])
                self.st8 = sb("st8%d" % n, [128, 8]); self.rh8 = sb("rh8%d" % n, [128, 8]); self.den8 = sb("den8%d" % n, [128, 8])
                self.XN = [self.T("xnT", 0), self.T("xnT", 1)]

            def T(self, *tok):
                return ("S%d" % self.n,) + tok
        SETS = [_Set(0), _Set(1)]
        proj = sb("proj", [128, 2816])
        tab = [sb("tab%d" % i, [128, 4, 64]) for i in range(2)]
        tg = sb("tg", [128, 4, 64])
        qa_bf = sb("qa_bf", [128, 512], BF16); ka_f = sb("ka_f", [128, 128]); ka_bf = sb("ka_bf", [128, 128], BF16)
        qaT = sb("qaT", [128, 512], BF16)
        kaT = [sb("kaT%d" % i, [128, 128], BF16) for i in range(2)]
        Vext = [sb("Vext%d" % i, [128, 2, 65], BF16) for i in range(2)]
        qr_bf = sb("qr_bf", [128, 512], BF16); kr_bf = sb("kr_bf", [128, 512], BF16); vr_bf = sb("vr_bf", [128, 512], BF16)
        qkT = sb("qkT", [128, 8, 128], BF16)
        innT = sb("innT", [128, 8, 128], BF16)
        S_f = sb("S_f", [128, 4, 64]); S_bd = [sb("S_bd%d" % i, [128, 4, 128], BF16) for i in range(2)]
        kmT = sb("kmT", [128, 4, 256], BF16); Vm = sb("Vm", [128, 2, 4, 129], BF16)
        xn3T = sb("xn3T", [128, 8, GRP * 128], BF16)
        wgu = [sb("wgu%d" % i, [128, 8, 256], BF16) for i in range(3)]
        wdb = [sb("wdb%d" % i, [128, 1024], BF16) for i in range(2)]
        sgb = [sb("sgb%d" % i, [128, GRP * 128]) for i in range(2)]
        hT = [sb("hT%d" % i, [128, GRP * 128], BF16) for i in range(2)]
        stg = [sb("stg%d" % i, [128, 1024]) for i in range(2)]
        stg_bf = sb("stg_bf", [128, 1024], BF16)
        wdb.append(stg_bf)
        WDT = [("wdb", 0), ("wdb", 1), "stg_bf"]
        WGV = [w_[:] for w_ in wgu] + [stg[i][:].bitcast(BF16).rearrange("p (k c) -> p k c", k=8) for i in range(2)]
        WGT = [("wgu", 0), ("wgu", 1), ("wgu", 2), ("stg", 0), ("stg", 1)]
        NWG = 5
        kcT = sb("kcT", [128, 16, 128], BF16); Vc = sb("Vc", [128, 16, 2, 65], BF16)
        kmTb = sb("kmTb", [128, 1024], BF16); Vmb = sb("Vmb", [128, 2, 4, 129], BF16)
        L0 = SETS[0]
        L0.pT = [sb("pT%d" % i, [128, 512], BF16)[:] for i in range(2)]
        L0.mix_bf = sb("mix_bf", [128, 1024], BF16)[:]
        L0.qm_bf = sb("qm_bf", [128, 512], BF16)[:]; L0.qmT = sb("qmT", [128, 512], BF16)[:]
        L0.pmT = sb("pmT", [128, 1024], BF16)[:]; L0.om_bf = sb("om_bf", [128, 512], BF16)[:]; L0.omT = sb("omT", [128, 512], BF16)[:]
        L1 = SETS[1]
        kc_flat = kcT[:].rearrange("p b r -> p (b r)")
        vc_flat = Vc[:].rearrange("p b h d -> p (b h d)")
        L1.mix_bf = kc_flat[:, 0:1024]; L1.pmT = kc_flat[:, 1024:2048]
        L1.pT = [vc_flat[:, 0:512], vc_flat[:, 512:1024]]
        L1.qm_bf = vc_flat[:, 1024:1536]; L1.qmT = vc_flat[:, 1536:2048]
        L1.om_bf = kmTb[:, 0:512]; L1.omT = kmTb[:, 512:1024]
        LATE_TOKS = []
        ps = st.enter_context(nc.psum_tensor("ps", [128, 4096], F32))

        def bank(b, n=512):
            return ps[:, b * 512:b * 512 + n]

        def bankbf(b, n=1024):
            return ps[:, b * 512:(b + 1) * 512].bitcast(BF16)[:, 0:n]

        BK = lambda b: ("bk", b)

        class _Banks:
            def __init__(self, banks):
                self.free = list(banks)

            def get(self):
                assert self.free, "PSUM banks exhausted in program order"
                return self.free.pop(0)

            def put(self, *bs):
                for b in bs:
                    assert b not in self.free
                    self.free.append(b)
        BAS = [_Banks(range(0, 4)), _Banks(range(4, 8))]

        class _BAProxy:
            cur = 0

            def get(self):
                return BAS[self.cur].get()

            def put(self, *bs):
                for b in bs:
                    BAS[0 if b < 4 else 1].put(b)
        BA = _BAProxy()
        cnt = {"dma": 0}

        def dma(stream, out, in_, reads, writes, chan, is_out=False):
            if "wcast" in skip and stream == "pool":
                return None
            nb = out.size() * (2 if out.dtype == BF16 else 4)
            return P.add(stream, lambda e: e.dma_start(out=out, in_=in_), reads, writes, chan=chan, is_out=is_out, nbytes=nb)

        dma("sp", ident[:], c_ident, [], ["ident"], "k0")
        dma("sp", masks[:], c_masks, [], ["masks"], "k1")
        P.add("act", lambda e: e.activation(out=ident32[:], in_=ident[:], func=AF.Copy), ["ident"], ["ident32"])
        dma("sp", colmask[:], c_colmask, [], ["colmask"], "k2")
        dma("sp", rowmask[:], c_rowmask, [], ["rowmask"], "k3")
        dma("sp", dec[:], c_dec, [], ["dec"], "k4")
        dma("sp", cdec[:], c_cdec, [], ["cdec"], "k5")
        for i, nrm in enumerate([] if "gcols" in skip else [norm_mix, norm_cross, norm_ffn, norm_mem]):
            dma("sp", gcols[:, i, :], nrm[0].rearrange("(k p) -> p k", p=128), [], [("gcols", i)], "k6")
        for t_, src in (() if "bcast" in skip else ((gqa, q_norm_a), (gka, k_norm_a))):
            nm = "gqa" if t_ is gqa else "gka"
            dma("sp", t_[:, 0, :], bc_row(src, 64), [], [nm + "0"], "k7")
            dma("sp", t_[:, 1, 0:32], bass.AP(src.tensor, 32, [[0, 128], [1, 32]]), [], [nm + "1"], "k8")
            dma("sp", t_[:, 1, 32:64], bass.AP(src.tensor, 0, [[0, 128], [1, 32]]), [], [nm + "2"], "k9")
        GQA = ["gqa0", "gqa1", "gqa2"]; GKA = ["gka0", "gka1", "gka2"]
        if "bcast2" not in skip:
            dma("sp", gqm[:], bc_row(q_norm_m, 128), [], ["gqm"], "k10")
            dma("sp", gkm[:], bc_row(k_norm_m, 128), [], ["gkm"], "k11")
            dma("sp", esink[:], bc_row(sinks, 8), [], ["esink"], "k12")
            P.add("act", lambda e: e.activation(out=esink[:], in_=esink[:], func=AF.Exp), ["esink"], ["esink"])
        P.add("pool", lambda e: e.memset(nh[:], -0.5), [], ["nh"])
        P.add("pool", lambda e: e.memset(S_f[:], 0.0), [], ["S_f"])
        for i in range(2):
            P.add("pool", lambda e, i=i: e.memset(S_bd[i][:], 0.0), [], [("S_bd", i)])
            P.add("pool", lambda e, i=i: e.memset(Vext[i][:], 1.0), [], [("Vext", i)])
        P.add("pool", lambda e: e.memset(Vm[:], 1.0), [], ["Vm"])
        P.add("pool", lambda e: e.memset(Vmb[:], 1.0), [], ["Vmb"])
        WIN = []
        for kvh in range(2):
            for g_ in range(4):
                hh = kvh * 4 + g_
                src = w_in[:, hh * 64:(hh + 1) * 64].rearrange("(k p) d -> p k d", p=128)
                dst = W_in[:, :, g_ * 128 + kvh * 64: g_ * 128 + (kvh + 1) * 64]
                dma("pool", dst, src, [], [("W_in", kvh)], "w%d" % kvh)
            WIN.append(("W_in", kvh))
        for i in range(4):
            src = w_in[:, 768 + i * 512:768 + (i + 1) * 512].rearrange("(k p) n -> p k n", p=128)
            dma("pool", W_in[:, :, 512 + i * 512:1024 + i * 512], src, [], [("W_in", 2 + i)], "w%d" % (2 + i)); WIN.append(("W_in", 2 + i))
        dma("pool", W_in[:, :, 2560:2816], w_in[:, 512:768].rearrange("(k p) n -> p k n", p=128), [], [("W_in", 6)], "w6"); WIN.append(("W_in", 6))
        dma("pool", W_out[:], w_out.rearrange("(k p) n -> p k n", p=128), [], ["W_out"], "w7")
        dma("pool", W_mq[:], w_mq.rearrange("(k p) n -> p k n", p=128), [], ["W_mq"], "w8")
        dma("pool", W_mo[:], w_mo.rearrange("(k p) n -> p k n", p=128), [], ["W_mo"], "w9")

        def rstd_chain(S, src_tok_list, st_ap, rh_ap, n, scale, rd, wr):
            P.add("pool", lambda e: e.tensor_scalar(out=rh_ap, in0=st_ap, scalar1=scale, scalar2=EPS, op0=ALU.mult, op1=ALU.add), rd, wr, cost=0.25)
            P.add("pool", lambda e: e.tensor_tensor(out=rh_ap, in0=rh_ap, in1=nh[:, 0:n], op=ALU.pow), wr + ["nh"], wr, cost=0.3)

        def norm_T(S, src, src_tok, gi, dstT, dst_tok, defer=False):
            if defer:
                junk_ = S.tmpC[:].bitcast(BF16)
                P.add("act", lambda e: e.activation(out=junk_, in_=src, func=AF.Square, accum_out=S.ss[:]), [src_tok], [S.T("tmpC"), S.T("ss")], cost=1.1)
            else:
                P.add("act", lambda e: e.activation(out=S.xn[:], in_=src, func=AF.Square, accum_out=S.ss[:]), [src_tok], [S.T("xn"), S.T("ss")], cost=1.1)
            rstd_chain(S, None, S.ss[:], S.rs[:], 1, 1.0 / 1024, [S.T("ss")], [S.T("rs")])
            if defer:
                P.ew(lambda e: e.tensor_copy(out=S.xn[:], in_=src), [src_tok], [S.T("xn")], cost=2)
            else:
                P.add("act", lambda e: e.activation(out=S.xn[:], in_=src, func=AF.Copy, scale=S.rs[:]), [src_tok, S.T("rs")], [S.T("xn")], cost=1.1)

            ba, bd = BA.get(), BA.get()

            def tr(e):
                for k in range(8):
                    bb = ba if k < 4 else bd
                    ins = e.transpose(out=bankbf(bb)[:, (k % 4) * 128:(k % 4 + 1) * 128], in_=S.xn[:, k * 128:(k + 1) * 128], identity=ident[:])
                return ins
            P.add("pe", tr, [S.T("xn"), "ident"], [BK(ba), BK(bd)], cost=0.6)

            def ev_a(e):
                for k in range(0, 4):
                    ins = e.activation(out=dstT[:, k, :], in_=bankbf(ba)[:, k * 128:(k + 1) * 128], func=AF.Copy, scale=gcols[:, gi, k:k + 1])
                return ins

            def ev_d(e):
                for k in range(4, 8):
                    ins = e.tensor_scalar(out=dstT[:, k, :], in0=bankbf(bd)[:, (k - 4) * 128:(k - 3) * 128], scalar1=gcols[:, gi, k:k + 1], scalar2=None, op0=ALU.mult)
                return ins
            P.add("act", ev_a, [BK(ba), ("gcols", gi)], [dst_tok + (0,)], cost=0.9)
            P.add("dve", ev_d, [BK(bd), ("gcols", gi)], [dst_tok + (1,)], cost=0.7)
            BA.put(ba, bd)

        def head_rstd(S, src3, H, D, rd, scratch, scratch_tok):
            sq = scratch[:, 0:H * D].rearrange("p (h d) -> p h d", d=D)
            P.ew(lambda e: e.tensor_tensor(out=sq, in0=src3, in1=src3, op=ALU.mult), rd, [scratch_tok], cost=H * D / 512.0)
            P.add("dve", lambda e: e.tensor_reduce(out=S.st8[:, 0:H], in_=sq, axis=AX.X, op=ALU.add), [scratch_tok], [S.T("st8")])
            rstd_chain(S, None, S.st8[:, 0:H], S.rh8[:, 0:H], H, 1.0 / D, [S.T("st8")], [S.T("rh8")])

        def bc_h(ap2, H):
            return ap2.unsqueeze(1).to_broadcast([128, H, ap2.shape[1]])

        def bc_d(ap2, D):
            return ap2.unsqueeze(2).to_broadcast([128, ap2.shape[1], D])

        def rot_half(S, src3, H, CT, ST, rd):
            A = S.tmpA[:, 0:H * 64].rearrange("p (h d) -> p h d", d=64)
            B = S.tmpB[:, 0:H * 64].rearrange("p (h d) -> p h d", d=64)
            c = H * 64 / 512.0
            P.ew(lambda e: e.tensor_tensor(out=A, in0=src3, in1=bc_h(CT, H), op=ALU.mult), rd, [S.T("tmpA")], cost=c)
            P.ew(lambda e: e.tensor_tensor(out=B[:, :, 0:32], in0=src3[:, :, 32:64], in1=bc_h(ST[:, 0:32], H), op=ALU.mult), rd, [S.T("tmpB", 0)], cost=c / 2)
            P.ew(lambda e: e.tensor_tensor(out=B[:, :, 32:64], in0=src3[:, :, 0:32], in1=bc_h(ST[:, 32:64], H), op=ALU.mult), rd, [S.T("tmpB", 1)], cost=c / 2)
            P.ew(lambda e: e.tensor_tensor(out=A, in0=A, in1=B, op=ALU.add), [S.T("tmpA"), S.T("tmpB", 0), S.T("tmpB", 1)], [S.T("tmpA")], cost=c)
            return A

        def rot_pair(S, src3, CF, SF, rd):
            A = S.tmpA[:].rearrange("p (h d) -> p h d", d=64)
            B4 = S.tmpB[:].rearrange("p (h i two) -> p h i two", h=8, two=2)
            s4 = src3.rearrange("p h (i two) -> p h i two", two=2)
            SF3 = SF.rearrange("p (i two) -> p i two", two=2)
            P.ew(lambda e: e.tensor_tensor(out=A, in0=src3, in1=bc_h(CF, 8), op=ALU.mult), rd, [S.T("tmpA")], cost=1)
            P.ew(lambda e: e.tensor_tensor(out=B4[:, :, :, 0], in0=s4[:, :, :, 1], in1=bc_h(SF3[:, :, 0], 8), op=ALU.mult), rd, [S.T("tmpB", 0)], cost=0.5)
            P.ew(lambda e: e.tensor_tensor(out=B4[:, :, :, 1], in0=s4[:, :, :, 0], in1=bc_h(SF3[:, :, 1], 8), op=ALU.mult), rd, [S.T("tmpB", 1)], cost=0.5)
            P.ew(lambda e: e.tensor_tensor(out=S.tmpA[:], in0=S.tmpA[:], in1=S.tmpB[:], op=ALU.add), [S.T("tmpA"), S.T("tmpB", 0), S.T("tmpB", 1)], [S.T("tmpA")], cost=1)
            return A

        def transposes(pairs, rd, dst, dst_tok, n):
            b0 = BA.get()

            def tr(e):
                for i, src in enumerate(pairs):
                    ins = e.transpose(out=bankbf(b0)[:, i * 128:(i + 1) * 128], in_=src, identity=ident[:])
                return ins
            P.add("pe", tr, list(rd) + ["ident"], [BK(b0)], cost=0.08 * n)
            wr = dst_tok if isinstance(dst_tok, list) else [dst_tok]
            P.add("act", lambda e: e.activation(out=dst, in_=bankbf(b0)[:, 0:n * 128], func=AF.Copy), [BK(b0)], wr, cost=0.25 + n * 0.1)
            BA.put(b0)

        def memkv():
            S = SETS[0]
            memT = [xn3T[:, :, i * 128:(i + 1) * 128] for i in range(2)]
            for mt in range(2):
                dma("sp", stg[mt][:], mem_p[mt * 128:(mt + 1) * 128, :], [], [("stg", mt)], "sg%d" % mt)
                norm_T(S, stg[mt][:], ("stg", mt), 3, memT[mt], ("xn3T", mt))
            ckpt(1.1)
            for c in range(4):
                s = c % 3
                dma("pool", wgu[s][:], w_mkv[:, c * 256:(c + 1) * 256].rearrange("(k p) n -> p k n", p=128), [], [("wgu", s)], "pg%d" % s)
                for mt in range(2):
                    bk = BA.get()

                    def mmk(e, s=s, mt=mt, bk=bk):
                        for k in range(8):
                            ins = e.matmul(bank(bk, 256), lhsT=memT[mt][:, k, :], rhs=wgu[s][:, k, :], start=(k == 0), stop=(k == 7))
                        return ins
                    P.add("pe", mmk, [("xn3T", mt, 0), ("xn3T", mt, 1), ("wgu", s)], [BK(bk)])
                    P.add("act", lambda e, mt=mt, c=c, bk=bk: e.activation(out=proj[:, mt * 1024 + c * 256: mt * 1024 + (c + 1) * 256], in_=bank(bk, 256), func=AF.Copy),
                          [BK(bk)], [("proj", mt * 2 + c // 2)])
                    BA.put(bk)
            ckpt(1.2)
            for mt in range(2):
                kv = proj[:, mt * 1024:(mt + 1) * 1024]
                k3 = kv[:, 0:512].rearrange("p (h d) -> p h d", d=128)
                rdk = [("proj", mt * 2)]
                head_rstd(S, k3, 4, 128, rdk, S.tmpA, S.T("tmpA"))
                A3 = S.tmpA[:].rearrange("p (h d) -> p h d", d=128)
                P.ew(lambda e, k3=k3, A3=A3: e.tensor_tensor(out=A3, in0=k3, in1=bc_d(S.rh8[:, 0:4], 128), op=ALU.mult), rdk + [S.T("rh8"), S.T("tmpA")], [S.T("tmpA")])
                C3 = S.tmpC[:].rearrange("p (h d) -> p h d", d=128)
                P.ew(lambda e, A3=A3, C3=C3: e.tensor_tensor(out=C3, in0=A3, in1=bc_h(gkm[:], 4), op=ALU.mult), [S.T("tmpA"), "gkm"], [S.T("tmpC")])
                ckpt(1.3)
                dma("sp", o_mk_p[mt * 128:(mt + 1) * 128, :], S.tmpC[:], [S.T("tmpC")], [], "omk", is_out=True)
                dma("sp", o_mv_p[mt * 128:(mt + 1) * 128, :], kv[:, 512:1024], [("proj", mt * 2 + 1)], [], "omv", is_out=True)
                P.ew(lambda e: e.tensor_copy(out=stg_bf[:, 0:512], in_=S.tmpC[:]), [S.T("tmpC")], ["stg_bf"])
                P.ew(lambda e, mt=mt, kv=kv: e.tensor_copy(out=Vm[:, mt, :, 0:128], in_=kv[:, 512:1024].rearrange("p (h d) -> p h d", d=128)),
                     [("proj", mt * 2 + 1), "Vm"], ["Vm"])

                ckpt(1.4)

                b0 = BA.get()

                def trk(e, b0=b0):
                    for h in range(4):
                        ins = e.transpose(out=bankbf(b0)[:, h * 128:(h + 1) * 128], in_=stg_bf[:, h * 128:(h + 1) * 128], identity=ident[:])
                    return ins
                P.add("pe", trk, ["stg_bf", "ident"], [BK(b0)])
                P.add("act", lambda e, mt=mt, b0=b0: e.activation(out=kmT[:, :, mt * 128:(mt + 1) * 128], in_=bankbf(b0)[:, 0:512].rearrange("p (h m) -> p h m", h=4), func=AF.Copy),
                      [BK(b0), "kmT"], ["kmT"])
                BA.put(b0)

        if stage > 1:
            try:
                memkv()
            except _Stop:
                pass
        if stage >= 3:
            for j in range(NJ):
                s = j % 3
                for two in range(2):
                    src = w_gu[:, two * 2816 + j * 128: two * 2816 + (j + 1) * 128].rearrange("(k p) n -> p k n", p=128)
                    dma("pool", wgu[s][:, :, two * 128:(two + 1) * 128], src, [], [("wgu", s)], "pg%d" % s)
                dma("sp", wgu_d[j], wgu[s][:].rearrange("p k c -> p (k c)"), [("wgu", s)], [("wgud", j)], "cs%d" % s)
                s_d = j % 2
                dma("pool", wdb[s_d][:], w_down[j * 128:(j + 1) * 128, :], [], [WDT[s_d]], "pd%d" % s_d)
                dma("sp", wd_d[j], wdb[s_d][:], [WDT[s_d]], [("wdd", j)], "ct%d" % s_d)

        def ffn_load(j):
            s = j % NWG
            dma("sp", WGV[s].rearrange("p k c -> p (k c)"), wgu_d[j], [("wgud", j)], [WGT[s]], "fg%d" % s)

        def ffn_load_d(j):
            s = j % 3
            dma("sp", wdb[s][:], wd_d[j], [("wdd", j)], [WDT[s]], "fd%d" % s)

        mult, add = ALU.mult, ALU.add

        def tt(out, in0, in1, op):
            return lambda e: e.tensor_tensor(out=out, in0=in0, in1=in1, op=op)

        def cpy(out, in_):
            return lambda e: e.tensor_copy(out=out, in_=in_)

        def v3(ap, d):
            return ap.rearrange("p (h d) -> p h d", d=d)

        def sample_preload():
            S1_ = SETS[1]
            fence_reads = [S1_.T("mix", 0), S1_.T("mix", 1), S1_.T("mix", 2), S1_.T("pmT", 0), S1_.T("pmT", 1), S1_.T("pT", 0), S1_.T("pT", 1),
                           S1_.T("qm_bf"), S1_.T("qmT"), S1_.T("om_bf", 0), S1_.T("om_bf", 1), S1_.T("omT")]
            P.add("pool", lambda e: e.memset(Vc[:], 1.0), fence_reads, ["Vc", "kcT", "kmTb"])
            for hf in range(2):
                b0 = hf * 8
                dma("sp", v3(stg[0][:], 128), c_k[b0:b0 + 8].rearrange("b r f -> r b f"), [], [("stg", 0)], "sg0")
                dma("sp", v3(stg[1][:], 128), c_v[b0:b0 + 8].rearrange("b r f -> r b f"), [], [("stg", 1)], "sg1")
                P.ew(cpy(stg_bf[:], stg[0][:]), [("stg", 0)], ["stg_bf"], cost=2)
                P.ew(cpy(Vc[:, b0:b0 + 8, :, 0:64], stg[1][:].rearrange("p (b h d) -> p b h d", b=8, h=2)), [("stg", 1), "Vc"], ["Vc"], cost=2)

                bq = BA.get()

                def trc(e, bq=bq):
                    for b in range(8):
                        ins = e.transpose(out=bankbf(bq)[:, b * 128:(b + 1) * 128], in_=stg_bf[:, b * 128:(b + 1) * 128], identity=ident[:])
                    return ins
                P.add("pe", trc, ["stg_bf", "ident"], [BK(bq)])
                P.add("act", lambda e, b0=b0, bq=bq: e.activation(out=kcT[:, b0:b0 + 8, :].rearrange("p b r -> p (b r)"), in_=bankbf(bq), func=AF.Copy), [BK(bq), "kcT"], ["kcT"])
                BA.put(bq)
            dma("sp", o_swk_s[:, 0:120, :], c_k[:, 8:128, :], [], [], "osk", is_out=True)
            dma("sp", o_swv_s[:, 0:120, :], c_v[:, 8:128, :], [], [], "osv", is_out=True)

        def tile_mixer(T, i, xi):
            samp = (T == NT - 1)
            S = SETS[i % 2]
            BA.cur = i % 2
            X = xres[xi]; XT = ("xres", xi)
            sl = T % 2
            tb = tab[T % 2]; TB = ("tab", T % 2)
            dv = 1 if samp else 0
            dma("sp", tb[:], c_tabs[T], [], [TB], "tb%d" % (T % 2))
            norm_T(S, X[:], XT, 0, S.xnT, S.T("xnT"), defer=True)
            groups = [(0, 512), (512, 1024), (1024, 1536), (1536, 2048), (2048, 2560), (2560, 2816)]
            if T == 0:
                S1 = SETS[1]
                xn32 = proj[:, 0:1024]
                P.add("act", lambda e: e.activation(out=xn32, in_=X[:], func=AF.Copy, scale=S.rs[:]), [XT, S.T("rs")], [("proj", 0), ("proj", 1)], cost=1.1)
                b32a, b32b = BA.get(), BA.get()

                def tr32(e):
                    for k in range(8):
                        bb = b32a if k < 4 else b32b
                        ins = e.transpose(out=bank(bb)[:, (k % 4) * 128:(k % 4 + 1) * 128], in_=xn32[:, k * 128:(k + 1) * 128], identity=ident32[:])
                    return ins
                P.add("pe", tr32, [("proj", 0), ("proj", 1), "ident32"], [BK(b32a), BK(b32b)], cost=2.0)

                def ev32a(e):
                    for k in range(4):
                        ins = e.activation(out=S1.tmpA[:, k * 128:(k + 1) * 128], in_=bank(b32a)[:, k * 128:(k + 1) * 128], func=AF.Copy, scale=gcols[:, 0, k:k + 1])
                    return ins

                def ev32b(e):
                    for k in range(4, 8):
                        ins = e.tensor_scalar(out=S1.tmpB[:, (k - 4) * 128:(k - 3) * 128], in0=bank(b32b)[:, (k - 4) * 128:(k - 3) * 128], scalar1=gcols[:, 0, k:k + 1], scalar2=None, op0=ALU.mult)
                    return ins
                P.add("act", ev32a, [BK(b32a), ("gcols", 0)], [S1.T("tmpA")], cost=0.9)
                P.add("dve", ev32b, [BK(b32b), ("gcols", 0)], [S1.T("tmpB", 0), S1.T("tmpB", 1)], cost=0.7)
                BA.put(b32a, b32b)
                for c in range(4):
                    c0 = 768 + c * 256
                    for kh in range(2):
                        dma("sp", stg[kh][:].rearrange("p (k n) -> p k n", k=4), w_in[kh * 512:(kh + 1) * 512, c0:c0 + 256].rearrange("(k p) n -> p k n", p=128),
                            [], [("stg", kh)], "sg%d" % kh)
                    bk = BA.get()

                    def mm32(e, bk=bk):
                        for k in range(8):
                            xt_ = S1.tmpA if k < 4 else S1.tmpB
                            ins = e.matmul(bank(bk, 256), lhsT=xt_[:, (k % 4) * 128:(k % 4 + 1) * 128], rhs=stg[k // 4][:, (k % 4) * 256:(k % 4 + 1) * 256],
                                           start=(k == 0), stop=(k == 7))
                        return ins
                    P.add("pe", mm32, [S1.T("tmpA"), S1.T("tmpB", 0), S1.T("tmpB", 1), ("stg", 0), ("stg", 1)], [BK(bk)], cost=3.6)
                    P.add("act", lambda e, c=c, bk=bk: e.activation(out=proj[:, 512 + c * 256:768 + c * 256], in_=bank(bk, 256), func=AF.Copy), [BK(bk)], [("proj", 1 + c // 2)], cost=0.45)
                    BA.put(bk)
            for gi, (c0, c1) in enumerate(groups):
                if T == 0 and gi in (1, 2):
                    continue
                bk = BA.get()

                def mmp(e, c0=c0, c1=c1, bk=bk):
                    for k in range(8):
                        ins = e.matmul(bank(bk, c1 - c0), lhsT=S.xnT[:, k, :], rhs=W_in[:, k, c0:c1], start=(k == 0), stop=(k == 7))
                    return ins
                P.add("pe", mmp, S.XN + WIN, [BK(bk)], cost=0.03 + 8 * (c1 - c0) / 2400.0)
                P.add("act", lambda e, c0=c0, c1=c1, bk=bk: e.activation(out=proj[:, c0:c1], in_=bank(bk, c1 - c0), func=AF.Copy, scale=S.rs[:]), [BK(bk), S.T("rs")], [("proj", gi)], cost=0.75)
                BA.put(bk)
            ckpt(4)
            P.ew(tt(tg[:, 0, :], tb[:, 0, :], gqa[:, 0, :], mult), [TB] + GQA, [("tg", 0)], cost=0.15)
            P.ew(tt(tg[:, 1, :], tb[:, 1, :], gqa[:, 1, :], mult), [TB] + GQA, [("tg", 1)], cost=0.15)
            P.ew(tt(tg[:, 2, :], tb[:, 0, :], gka[:, 0, :], mult), [TB] + GKA, [("tg", 2)], cost=0.15)
            P.ew(tt(tg[:, 3, :], tb[:, 1, :], gka[:, 1, :], mult), [TB] + GKA, [("tg", 3)], cost=0.15)
            q3 = v3(proj[:, 0:512], 64)
            head_rstd(S, q3, 8, 64, [("proj", 0)], S.tmpC, S.T("tmpC"))
            A = rot_half(S, q3, 8, tg[:, 0, :], tg[:, 1, :], [("proj", 0), ("tg", 0), ("tg", 1)])
            P.ew(tt(v3(qa_bf[:], 64), A, bc_d(S.rh8[:, 0:8], 64), mult), [S.T("tmpA"), S.T("rh8")], ["qa_bf"])
            k3 = v3(proj[:, 2560:2688], 64)
            head_rstd(S, k3, 2, 64, [("proj", 5)], S.tmpC, S.T("tmpC"))
            A = rot_half(S, k3, 2, tg[:, 2, :], tg[:, 3, :], [("proj", 5), ("tg", 2), ("tg", 3)])
            P.ew(tt(v3(ka_f[:], 64), A, bc_d(S.rh8[:, 0:2], 64), mult), [S.T("tmpA"), S.T("rh8")], ["ka_f"], cost=0.25)
            P.ew(cpy(ka_bf[:], ka_f[:]), ["ka_f"], ["ka_bf"], cost=0.25)
            P.ew(cpy(Vext[sl][:, :, 0:64], v3(proj[:, 2688:2816], 64)), [("proj", 5), ("Vext", sl)], [("Vext", sl)], cost=0.25)

            bq, bk_ = BA.get(), BA.get()

            def tra(e):
                for c in range(4):
                    ins = e.transpose(out=bankbf(bq)[:, c * 128:(c + 1) * 128], in_=qa_bf[:, c * 128:(c + 1) * 128], identity=ident[:])
                return ins
            P.add("pe", tra, ["qa_bf", "ident"], [BK(bq)], cost=0.35)
            P.add("pe", lambda e: e.transpose(out=bankbf(bk_)[:, 0:128], in_=ka_bf[:], identity=ident[:]), ["ka_bf", "ident"], [BK(bk_)], cost=0.1)
            P.add("act", lambda e: e.activation(out=qaT[:], in_=bankbf(bq)[:, 0:512], func=AF.Copy), [BK(bq)], ["qaT"], cost=0.65)
            P.add("dve", cpy(kaT[sl][:], bankbf(bk_)[:, 0:128]), [BK(bk_)], [("kaT", sl)], cost=0.25)
            BA.put(bq, bk_)
            if T == NT - 2:
                dma("sp", o_swk_p, ka_f[:], ["ka_f"], [], "okp", is_out=True)
                dma("sp", o_swv_p, proj[:, 2688:2816], [("proj", 5)], [], "ovp", is_out=True)
            if samp:
                for b in range(16):
                    dma("sp", o_swk_s[b, 120:128, :], ka_f[b * 8:(b + 1) * 8, :], ["ka_f"], [], "osk%d" % (b % 4), is_out=True)
                    dma("sp", o_swv_s[b, 120:128, :], proj[b * 8:(b + 1) * 8, 2688:2816], [("proj", 5)], [], "osv%d" % (b % 4), is_out=True)
            blocks = []
            if samp:
                for b in range(16):
                    blocks.append((kcT[:, b, :], "kcT", Vc[:, b, :, :], "Vc", (colmask[:, b, :], masks[:, 3, :])))
                blocks.append((kaT[sl][:], ("kaT", sl), Vext[sl][:], ("Vext", sl), masks[:, 2, :]))
            else:
                if T > 0:
                    blocks.append((kaT[1 - sl][:], ("kaT", 1 - sl), Vext[1 - sl][:], ("Vext", 1 - sl), masks[:, 1, :]))
                blocks.append((kaT[sl][:], ("kaT", sl), Vext[sl][:], ("Vext", sl), masks[:, 0, :]))
            n = 0
            for kvh in range(2):
                ob = BA.get()
                for bi, (kT, kTt, Vx, Vt, mk) in enumerate(blocks):
                    sbk = BA.get()
                    pi = n % 2
                    n += 1
                    P.add("pe", lambda e, kT=kT, kvh=kvh, sbk=sbk: e.matmul(bank(sbk), lhsT=kT[kvh * 64:(kvh + 1) * 64, :], rhs=qaT[kvh * 64:(kvh + 1) * 64, :], start=True, stop=True),
                          [kTt, "qaT"], [BK(sbk)])
                    P.add("act", lambda e, sbk=sbk, pi=pi: e.activation(out=S.pT[pi][:], in_=bank(sbk), func=AF.Exp, scale=0.125), [BK(sbk)], [S.T("pT", pi)], cost=0.65)
                    BA.put(sbk)
                    for mk1 in (mk if isinstance(mk, tuple) else (mk,)):
                        P.ew(tt(v3(S.pT[pi][:], 128), v3(S.pT[pi][:], 128), bc_h(mk1, 4), mult), [S.T("pT", pi), "masks", "colmask"], [S.T("pT", pi)])

                    def pv(e, pi=pi, Vx=Vx, kvh=kvh, ob=ob, bi=bi, nb=len(blocks)):
                        for g in range(4):
                            ins = e.matmul(bank(ob)[:, g * 65:(g + 1) * 65], lhsT=S.pT[pi][:, g * 128:(g + 1) * 128], rhs=Vx[:, kvh, :],
                                           start=(bi == 0 and g == 0), stop=(bi == nb - 1 and g == 3), skip_group_check=True)
                        return ins
                    P.add("pe", pv, [S.T("pT", pi), Vt], [BK(ob)])
                o3 = bank(ob)[:, 0:260].rearrange("p (g d) -> p g d", d=65)
                P.add("dve", tt(S.den8[:, 0:4], o3[:, :, 64], esink[:, kvh * 4:(kvh + 1) * 4], add), [BK(ob), "esink"], [S.T("den8")])
                P.add("dve", lambda e: e.reciprocal(out=S.den8[:, 0:4], in_=S.den8[:, 0:4]), [S.T("den8")], [S.T("den8")])
                P.add("dve", tt(v3(S.mix_bf[:, kvh * 256:(kvh + 1) * 256], 64), o3[:, :, 0:64], bc_d(S.den8[:, 0:4], 64), mult), [BK(ob), S.T("den8")], [S.T("mix", kvh)])
                BA.put(ob)
            ckpt(5)
            q_lo, k_lo = S.qm_bf, S.om_bf
            QLO, KLO = [S.T("qm_bf")], [S.T("om_bf", 0), S.T("om_bf", 1)]
            hilo = (T == 0)
            A = rot_pair(S, v3(proj[:, 512:1024], 64), tb[:, 2, :], tb[:, 3, :], [("proj", 1), TB])
            if hilo:
                P.ew(tt(A, A, bc_d(dec[:, dv, 0, :], 64), mult), [S.T("tmpA"), "dec"], [S.T("tmpA")])
                P.ew(cpy(qr_bf[:], S.tmpA[:]), [S.T("tmpA")], ["qr_bf"])
                P.ew(tt(q_lo[:], S.tmpA[:], qr_bf[:], ALU.subtract), [S.T("tmpA"), "qr_bf"], QLO)
            else:
                P.ew(tt(v3(qr_bf[:], 64), A, bc_d(dec[:, dv, 0, :], 64), mult), [S.T("tmpA"), "dec"], ["qr_bf"])
            A = rot_pair(S, v3(proj[:, 1024:1536], 64), tb[:, 2, :], tb[:, 3, :], [("proj", 2), TB])
            if hilo:
                P.ew(tt(A, A, bc_d(dec[:, dv, 1, :], 64), mult), [S.T("tmpA"), "dec"], [S.T("tmpA")])
                P.ew(cpy(kr_bf[:], S.tmpA[:]), [S.T("tmpA")], ["kr_bf"])
                P.ew(tt(k_lo[:], S.tmpA[:], kr_bf[:], ALU.subtract), [S.T("tmpA"), "kr_bf"], KLO)
            else:
                P.ew(tt(v3(kr_bf[:], 64), A, bc_d(dec[:, dv, 1, :], 64), mult), [S.T("tmpA"), "dec"], ["kr_bf"])
            P.ew(cpy(vr_bf[:], proj[:, 1536:2048]), [("proj", 3)], ["vr_bf"])
            transposes([qr_bf[:, c * 128:(c + 1) * 128] for c in range(4)] + [kr_bf[:, c * 128:(c + 1) * 128] for c in range(4)],
                       ["qr_bf", "kr_bf"], qkT[:].rearrange("p a t -> p (a t)"), "qkT", 8)
            if hilo:
                transposes([q_lo[:, c * 128:(c + 1) * 128] for c in range(4)] + [k_lo[:, c * 128:(c + 1) * 128] for c in range(4)],
                           QLO + KLO, innT[:].rearrange("p a t -> p (a t)"), [("innT", 0), ("innT", 1)], 8)

            bi0, bi1 = BA.get(), BA.get()
            bis = (bi0, bi1)

            def inn(e):
                for c in range(4):
                    for hl in range(2):
                        pr = slice(hl * 64, (hl + 1) * 64)
                        o_ = bank(bis[hl])[:, c * 128:(c + 1) * 128]
                        ins = e.matmul(o_, lhsT=qkT[pr, 4 + c, :], rhs=qkT[pr, c, :], start=(c == 0), stop=(c == 3 and not hilo), skip_group_check=True)
                        if hilo:
                            e.matmul(o_, lhsT=qkT[pr, 4 + c, :], rhs=innT[pr, c, :], start=False, stop=False, skip_group_check=True)
                            ins = e.matmul(o_, lhsT=innT[pr, 4 + c, :], rhs=qkT[pr, c, :], start=False, stop=(c == 3), skip_group_check=True)
                return ins
            P.add("pe", inn, ["qkT"] + ([("innT", 0), ("innT", 1)] if hilo else []), [BK(bi0), BK(bi1)], cost=1.6 if hilo else 0.6)
            mk = masks[:, 2 if samp else 0, :]
            for hl in range(2):
                P.add("dve", tt(innT[:, hl * 4:(hl + 1) * 4, :], v3(bank(bis[hl]), 128), bc_h(mk, 4), mult), [BK(bis[hl]), "masks"], [("innT", hl)], cost=0.65)
            BA.put(bi0, bi1)
            b7 = BA.get()

            def orm(e):
                for h in range(8):
                    c, hl = h // 2, h % 2
                    ins = e.matmul(bank(b7)[:, h * 64:(h + 1) * 64], lhsT=innT[:, hl * 4 + c, :], rhs=vr_bf[:, h * 64:(h + 1) * 64],
                                   start=(h == 0), stop=False, skip_group_check=True)
                return ins
            P.add("pe", orm, [("innT", 0), ("innT", 1), "vr_bf"], [BK(b7)], cost=0.5)
            if not samp:
                cur = T % 2

                def crs(e):
                    for c in range(4):
                        ins = e.matmul(bank(b7)[:, c * 128:(c + 1) * 128], lhsT=qkT[:, c, :], rhs=S_bd[cur][:, c, :], start=False, stop=(c == 3), skip_group_check=True)
                    return ins
                P.add("pe", crs, ["qkT", ("S_bd", cur)], [BK(b7)], cost=0.3)
                bkv = BA.get()
                def kvm(e):
                    for c in range(4):
                        ins = e.matmul(bank(bkv)[:, c * 128:(c + 1) * 128], lhsT=kr_bf[:, c * 128:(c + 1) * 128], rhs=vr_bf[:, c * 128:(c + 1) * 128],
                                       start=(c == 0), stop=(c == 3), skip_group_check=True)
                    return ins
                P.add("pe", kvm, ["kr_bf", "vr_bf"], [BK(bkv)], cost=0.3)
                kv4 = v3(bank(bkv), 128)
                P.add("dve", tt(S_f[0:64], kv4[0:64, :, 0:64], S_f[0:64], add), [BK(bkv), "S_f"], ["S_f"], cost=0.3)
                P.add("dve", tt(S_f[64:128], kv4[64:128, :, 64:128], S_f[64:128], add), [BK(bkv), "S_f"], ["S_f"], cost=0.3)
                BA.put(bkv)
                P.ew(tt(S_f[:], S_f[:], bc_d(cdec[:, 0, :], 64), mult), ["S_f", "cdec"], ["S_f"], cost=0.5)
                nx = 1 - cur
                P.ew(cpy(S_bd[nx][0:64, :, 0:64], S_f[0:64]), ["S_f", ("S_bd", nx)], [("S_bd", nx)], cost=0.25)
                P.ew(cpy(S_bd[nx][64:128, :, 64:128], S_f[64:128]), ["S_f", ("S_bd", nx)], [("S_bd", nx)], cost=0.25)
                if T == NT - 2:
                    dma("sp", o_ret_p.rearrange("(c hl) d e -> (hl d) c e", hl=2), S_f[:], ["S_f"], [], "orp", is_out=True)
            else:
                for b in range(16):
                    s2 = b % 2
                    S0 = stg[s2][:, 0:256].rearrange("p (c e) -> p c e", e=64)
                    dma("sp", S0, s_ret[b].rearrange("(c hl) d e -> (hl d) c e", hl=2), [], [("stg", s2)], "sg%d" % s2)
                    P.ew(cpy(S_bd[s2][0:64, :, 0:64], S0[0:64]), [("stg", s2), ("S_bd", s2)], [("S_bd", s2)], cost=0.25)
                    P.ew(cpy(S_bd[s2][64:128, :, 64:128], S0[64:128]), [("stg", s2), ("S_bd", s2)], [("S_bd", s2)], cost=0.25)
                    padb = (qa_bf, S.qm_bf)
                    padt = ("qa_bf", S.T("qm_bf"))[s2]
                    P.ew(tt(v3(padb[s2][:], 128), qkT[:, 0:4, :], bc_h(colmask[:, b, :], 4), mult), ["qkT", "colmask"], [padt])

                    def crs(e, b=b, s2=s2, padb=padb):
                        for c in range(4):
                            ins = e.matmul(bank(b7)[:, c * 128:(c + 1) * 128], lhsT=padb[s2][:, c * 128:(c + 1) * 128], rhs=S_bd[s2][:, c, :],
                                           start=False, stop=(b == 15 and c == 3), skip_group_check=True)
                        return ins
                    P.add("pe", crs, [padt, ("S_bd", s2)], [BK(b7)])
                    P.ew(lambda e, b=b, s2=s2: e.tensor_scalar(out=S.pT[s2][:], in0=kr_bf[:], scalar1=rowmask[:, b:b + 1], scalar2=None, op0=mult),
                         ["kr_bf", "rowmask"], [S.T("pT", s2)])
                    kb = BA.get()

                    def kvm(e, s2=s2, kb=kb):
                        for c in range(4):
                            ins = e.matmul(bank(kb)[:, c * 128:(c + 1) * 128], lhsT=S.pT[s2][:, c * 128:(c + 1) * 128], rhs=vr_bf[:, c * 128:(c + 1) * 128],
                                           start=(c == 0), stop=(c == 3), skip_group_check=True)
                        return ins
                    P.add("pe", kvm, [S.T("pT", s2), "vr_bf"], [BK(kb)])
                    kv4 = v3(bank(kb), 128)
                    So = S.tmpC[:, s2 * 256:(s2 + 1) * 256].rearrange("p (c e) -> p c e", e=64)
                    SoT = S.T("tmpC2", s2)
                    P.add("dve", tt(So[0:64], kv4[0:64, :, 0:64], S0[0:64], add), [BK(kb), ("stg", s2), S.T("tmpC")], [SoT])
                    P.add("dve", tt(So[64:128], kv4[64:128, :, 64:128], S0[64:128], add), [BK(kb), ("stg", s2), SoT], [SoT])
                    BA.put(kb)
                    P.ew(tt(So, So, bc_d(cdec[:, 1, :], 64), mult), [SoT, "cdec"], [SoT], cost=0.5)
                    dma("sp", o_ret_s[b].rearrange("(c hl) d e -> (hl d) c e", hl=2), So, [SoT], [], "ors%d" % s2, is_out=True)
            P.add("act", lambda e: e.activation(out=S.tmpA[:], in_=bank(b7), func=AF.Square), [BK(b7)], [S.T("tmpA")], cost=0.65)
            P.add("dve", lambda e: e.tensor_reduce(out=S.st8[:], in_=v3(S.tmpA[:], 64), axis=AX.X, op=add), [S.T("tmpA")], [S.T("st8")])
            rstd_chain(S, None, S.st8[:], S.rh8[:], 8, 1.0 / 64, [S.T("st8")], [S.T("rh8")])
            P.add("dve", tt(v3(S.tmpB[:], 64), v3(bank(b7), 64), bc_d(S.rh8[:], 64), mult), [BK(b7), S.T("rh8")], [S.T("tmpB", 0), S.T("tmpB", 1)], cost=0.65)
            BA.put(b7)
            P.add("act", lambda e: e.activation(out=S.tmpC[:], in_=proj[:, 2048:2560], func=AF.Silu), [("proj", 4)], [S.T("tmpC"), S.T("tmpC2", 0), S.T("tmpC2", 1)])
            P.ew(tt(S.mix_bf[:, 512:1024], S.tmpB[:], S.tmpC[:], mult), [S.T("tmpB", 0), S.T("tmpB", 1), S.T("tmpC")], [S.T("mix", 2)])
            ckpt(6)
            transposes([S.mix_bf[:, k * 128:(k + 1) * 128] for k in range(8)], [S.T("mix", 0), S.T("mix", 1), S.T("mix", 2)], S.xnT[:].rearrange("p k t -> p (k t)"), S.XN, 8)

            for hf in range(2):
                by = BA.get()

                def mmo(e, hf=hf, by=by):
                    for k in range(8):
                        ins = e.matmul(bank(by), lhsT=S.xnT[:, k, :], rhs=W_out[:, k, hf * 512:(hf + 1) * 512], start=(k == 0), stop=(k == 7))
                    return ins
                P.add("pe", mmo, S.XN + ["W_out"], [BK(by)], cost=1.75)
                P.add("dve", tt(X[:, hf * 512:(hf + 1) * 512], X[:, hf * 512:(hf + 1) * 512], bank(by), add), [XT, BK(by)], [XT], cost=0.65)
                BA.put(by)

        def tile_cross(T, i, xi):
            samp = (T == NT - 1)
            S = SETS[i % 2]
            X = xres[xi]; XT = ("xres", xi)
            norm_T(S, X[:], XT, 1, S.xnT, S.T("xnT"), defer=True)

            bq_ = BA.get()

            def mmq(e):
                for k in range(8):
                    ins = e.matmul(bank(bq_), lhsT=S.xnT[:, k, :], rhs=W_mq[:, k, :], start=(k == 0), stop=(k == 7))
                return ins
            P.add("pe", mmq, S.XN + ["W_mq"], [BK(bq_)], cost=1.75)
            P.add("act", lambda e: e.activation(out=S.tmpA[:], in_=bank(bq_), func=AF.Copy, scale=S.rs[:]), [BK(bq_), S.T("rs")], [S.T("tmpA")], cost=0.75)
            BA.put(bq_)
            A3 = v3(S.tmpA[:], 128)
            head_rstd(S, A3, 4, 128, [S.T("tmpA")], S.tmpC, S.T("tmpC"))
            P.ew(tt(A3, A3, bc_d(S.rh8[:, 0:4], 128), mult), [S.T("tmpA"), S.T("rh8")], [S.T("tmpA")])
            P.ew(tt(v3(S.qm_bf[:], 128), A3, bc_h(gqm[:], 4), mult), [S.T("tmpA"), "gqm"], [S.T("qm_bf")])
            transposes([S.qm_bf[:, h * 128:(h + 1) * 128] for h in range(4)], [S.T("qm_bf")], S.qmT[:], S.T("qmT"), 4)
            SC = 128.0 ** -0.5
            nb = 16 if samp else 1
            bo = (BA.get(), BA.get())
            for b in range(nb):
                if samp:
                    s2 = b % 2
                    kb = stg_bf[:] if s2 == 0 else stg[0][:].bitcast(BF16)[:, 0:1024]
                    kb_tok = "stg_bf" if s2 == 0 else ("stg", 0)
                    vb = Vmb[:] if s2 == 0 else stg[1][:].bitcast(BF16)[:, 0:1032].rearrange("p (mc h d) -> p mc h d", mc=2, h=4)
                    vb_tok = "Vmb" if s2 == 0 else ("stg", 1)
                    if b == 1:
                        P.add("pool", lambda e, vb=vb: e.memset(vb, 1.0), [], [vb_tok])
                    dma("pool", kb.rearrange("p (mc f) -> p mc f", mc=2), c_mk[b].rearrange("(mc m) f -> m mc f", m=128), [], [kb_tok], "pk%d" % s2)
                    for mc in range(2):
                        dma("pool", vb[:, mc, :, 0:128], c_mv[b, mc * 128:(mc + 1) * 128, :].rearrange("m (h d) -> m h d", h=4), [vb_tok], [vb_tok], "pv%d" % s2)
                    bt = BAS[1].get()

                    def trk(e, bt=bt, kb=kb):
                        for h in range(4):
                            for mc in range(2):
                                s8 = h * 2 + mc
                                ins = e.transpose(out=bankbf(bt)[:, s8 * 128:(s8 + 1) * 128], in_=kb[:, mc * 512 + h * 128: mc * 512 + (h + 1) * 128], identity=ident[:])
                        return ins
                    P.add("pe", trk, [kb_tok, "ident"], [BK(bt)], cost=0.7)
                    kmb = kmTb[:] if s2 == 0 else qkT[:].rearrange("p a t -> p (a t)")
                    kmb_tok = "kmTb" if s2 == 0 else "qkT"
                    P.add("act", lambda e, bt=bt, kmb=kmb: e.activation(out=kmb, in_=bankbf(bt), func=AF.Copy), [BK(bt)], [kmb_tok], cost=1.1)
                    BA.put(bt)
                    kt_tok, v_tok = kmb_tok, vb_tok
                    kt_of = lambda h, mc, kmb=kmb: kmb[:, (h * 2 + mc) * 128:(h * 2 + mc + 1) * 128]
                    v_of = lambda h, mc, vb=vb: vb[:, mc, h, :]
                else:
                    kt_tok, v_tok = "kmT", "Vm"
                    kt_of = lambda h, mc: kmT[:, h, mc * 128:(mc + 1) * 128]
                    v_of = lambda h, mc: Vm[:, mc, h, :]

                bs = (BAS[1].get(), BAS[1].get()) if samp else (BA.get(), BA.get())
                if samp and b % 2 == 1:
                    pm = (S.pT[0][:], S.pT[1][:]); pm_tok = [S.T("pT", 0), S.T("pT", 1)]
                else:
                    pm = (S.pmT[:, 0:512], S.pmT[:, 512:1024]); pm_tok = [S.T("pmT", 0), S.T("pmT", 1)]

                def scm(e, kt_of=kt_of, bs=bs):
                    for h in range(4):
                        for mc in range(2):
                            s8 = h * 2 + mc
                            ins = e.matmul(bank(bs[s8 // 4])[:, (s8 % 4) * 128:(s8 % 4 + 1) * 128], lhsT=kt_of(h, mc), rhs=S.qmT[:, h * 128:(h + 1) * 128],
                                           start=(s8 % 4 == 0), stop=(s8 % 4 == 3), skip_group_check=True)
                    return ins
                P.add("pe", scm, [kt_tok, S.T("qmT")], [BK(bs[0]), BK(bs[1])], cost=0.7)
                for hb in range(2):
                    P.add("act", lambda e, hb=hb, bs=bs, pm=pm: e.activation(out=pm[hb], in_=bank(bs[hb]), func=AF.Exp, scale=SC), [BK(bs[hb])], [pm_tok[hb]], cost=0.65)
                BA.put(*bs)
                if samp:
                    for hb in range(2):
                        P.ew(tt(v3(pm[hb], 128), v3(pm[hb], 128), bc_h(colmask[:, b, :], 4), mult), [pm_tok[hb], "colmask"], [pm_tok[hb]], cost=1)

                def pvm(e, b=b, v_of=v_of, pm=pm):
                    for h in range(4):
                        for mc in range(2):
                            s8 = h * 2 + mc
                            ins = e.matmul(bank(bo[h // 2])[:, (h % 2) * 256:(h % 2) * 256 + 129], lhsT=pm[s8 // 4][:, (s8 % 4) * 128:(s8 % 4 + 1) * 128], rhs=v_of(h, mc),
                                           start=(b == 0 and h % 2 == 0 and mc == 0), stop=(b == nb - 1 and h % 2 == 1 and mc == 1), skip_group_check=True)
                    return ins
                P.add("pe", pvm, pm_tok + [v_tok], [BK(bo[0]), BK(bo[1])], cost=0.8)
            for hh in range(2):
                om2 = bank(bo[hh]).rearrange("p (h d) -> p h d", d=256)
                P.add("dve", lambda e, hh=hh, om2=om2: e.reciprocal(out=S.den8[:, 4 + 2 * hh:6 + 2 * hh], in_=om2[:, :, 128]), [BK(bo[hh])], [S.T("den8b", hh)], cost=0.15)
                P.add("dve", tt(v3(S.om_bf[:, hh * 256:(hh + 1) * 256], 128), om2[:, :, 0:128], bc_d(S.den8[:, 4 + 2 * hh:6 + 2 * hh], 128), mult),
                      [BK(bo[hh]), S.T("den8b", hh)], [S.T("om_bf", hh)], cost=0.4)
            BA.put(*bo)
            transposes([S.om_bf[:, h * 128:(h + 1) * 128] for h in range(4)], [S.T("om_bf", 0), S.T("om_bf", 1)], S.omT[:], S.T("omT"), 4)
            for hf in range(2):
                by = BA.get()

                def mmo(e, hf=hf, by=by):
                    for c in range(4):
                        ins = e.matmul(bank(by), lhsT=S.omT[:, c * 128:(c + 1) * 128], rhs=W_mo[:, c, hf * 512:(hf + 1) * 512], start=(c == 0), stop=(c == 3))
                    return ins
                P.add("pe", mmo, [S.T("omT"), "W_mo"], [BK(by)], cost=0.9)
                P.add("dve", tt(X[:, hf * 512:(hf + 1) * 512], X[:, hf * 512:(hf + 1) * 512], bank(by), add), [XT, BK(by)], [XT], cost=0.65)
                BA.put(by)
            norm_T(S, X[:], XT, 2, xn3T[:, :, i * 128:(i + 1) * 128], ("xn3T", i))

        def ffn_group(g):
            tiles = GROUPS[g]
            nt_ = len(tiles)
            xo = (g % 2) * GRP
            XR = [("xn3T", i, h) for i in range(nt_) for h in range(2)]
            NTOK = nt_ * 128
            if (NT - 1) in tiles:
                ffn_load_d(2)
            if (NT - 1) in tiles or 0 in tiles:
                for j_ in range(3, NWG):
                    ffn_load(j_)
            yb = []
            for i in range(nt_):
                yb += [BAS[i % 2].get(), BAS[i % 2].get()]
            for j in range(NJ):
                s = j % NWG
                hs = j % 2
                bg, bu = BAS[0].get(), BAS[1].get()

                def gmm(e, s=s, bg=bg):
                    for k in range(8):
                        ins = e.matmul(bank(bg, NTOK), lhsT=WGV[s][:, k, 0:128], rhs=xn3T[:, k, 0:NTOK], start=(k == 0), stop=(k == 7))
                    return ins

                def umm(e, s=s, bu=bu):
                    for k in range(8):
                        ins = e.matmul(bank(bu, NTOK), lhsT=WGV[s][:, k, 128:256], rhs=xn3T[:, k, 0:NTOK], start=(k == 0), stop=(k == 7))
                    return ins
                P.add("pe", gmm, XR + [WGT[s]], [BK(bg)], cost=0.05 + 8 * max(NTOK, 128) / 2400.0)
                P.add("pe", umm, XR + [WGT[s]], [BK(bu)], cost=0.05 + 8 * max(NTOK, 128) / 2400.0)
                P.add("act", lambda e, hs=hs, bg=bg: e.activation(out=sgb[hs][:, 0:NTOK], in_=bank(bg, NTOK), func=AF.Silu), [BK(bg)], [("sgb", hs)], cost=0.45)
                P.add("dve", tt(hT[hs][:, 0:NTOK], sgb[hs][:, 0:NTOK], bank(bu, NTOK), mult), [("sgb", hs), BK(bu)], [("hT", hs)], cost=0.45)
                BA.put(bg, bu)

                def dn(e, s=s, hs=hs, j=j):
                    for i in range(nt_):
                        for hf in range(2):
                            ins = e.matmul(bank(yb[2 * i + hf]), lhsT=hT[hs][:, i * 128:(i + 1) * 128], rhs=wdb[j % 3][:, hf * 512:(hf + 1) * 512],
                                           start=(j == 0), stop=(j == NJ - 1))
                    return ins
                P.add("pe", dn, [("hT", hs), WDT[j % 3]], [BK(b_) for b_ in yb], cost=0.05 + 2 * nt_ * 0.22)
                if j + NWG < NJ:
                    ffn_load(j + NWG)
                if j + 3 < NJ:
                    ffn_load_d(j + 3)
            for i, T in enumerate(tiles):
                xi = xo + i
                for hf in range(2):
                    P.add("dve", tt(xres[xi][:, hf * 512:(hf + 1) * 512], xres[xi][:, hf * 512:(hf + 1) * 512], bank(yb[2 * i + hf]), add),
                          [("xres", xi), BK(yb[2 * i + hf])], [("xres", xi)], cost=0.65)
                dst = y_s if T == NT - 1 else y_p[T * 128:(T + 1) * 128, :]
                dma("sp", dst, xres[xi][:], [("xres", xi)], [], "yo%d" % xi, is_out=True)
            BA.put(*yb)

        def main_loop():
          ckpt(3)
          for g in range(NG):
              xo = (g % 2) * GRP
              for i, T in enumerate(GROUPS[g]):
                  src = x_s if T == NT - 1 else x_p[T * 128:(T + 1) * 128, :]
                  dma("sp", xres[xo + i][:], src, [], [("xres", xo + i)], "xl%d" % (xo + i))
              has_samp = (NT - 1) in GROUPS[g]
              uses_stg = has_samp or (0 in GROUPS[g])
              for j in range(3 if uses_stg else NWG):
                  ffn_load(j)
              for j in range(2 if has_samp else 3):
                  ffn_load_d(j)
              for i, T in enumerate(GROUPS[g]):
                  if T == NT - 1:
                      sample_preload()
                  tile_mixer(T, i, xo + i)
                  ckpt(7)
                  tile_cross(T, i, xo + i)
                  ckpt(8)
              ckpt(9)
              ffn_group(g)
              ckpt(10 + g)
        try:
            main_loop()
        except _Stop:
            pass
        P.emit(nc, st)
    return nc


_CACHE = {}


def kernel(x_prompt, x_sample, mem_prompt, cache_swa_k, cache_swa_v, state_ret, cache_mem_k, cache_mem_v,
           norm_mix, w_in, q_norm_a, k_norm_a, sinks, w_out, norm_cross, norm_mem, w_mq, w_mkv,
           q_norm_m, k_norm_m, w_mo, norm_ffn, w_gu, w_down):
    f = lambda a: np.ascontiguousarray(np.asarray(a, dtype=np.float32))
    if "nc" not in _CACHE:
        _CACHE["nc"] = build_program()
        _CACHE["consts"] = host_consts()
    nc = _CACHE["nc"]
    consts = _CACHE["consts"]
    shared = {
        "w_in": f(w_in)[0], "w_out": f(w_out)[0], "w_mq": f(w_mq)[0], "w_mkv": f(w_mkv)[0], "w_mo": f(w_mo)[0],
        "w_gu": f(w_gu)[0], "w_down": f(w_down)[0],
        "norm_mix": f(norm_mix), "norm_cross": f(norm_cross), "norm_mem": f(norm_mem), "norm_ffn": f(norm_ffn),
        "q_norm_a": f(q_norm_a), "k_norm_a": f(k_norm_a), "sinks": f(sinks), "q_norm_m": f(q_norm_m), "k_norm_m": f(k_norm_m),
    }
    shared.update(consts)
    xp, xs, mp = f(x_prompt), f(x_sample), f(mem_prompt)
    ck, cv, sr, cmk, cmv = f(cache_swa_k)[0], f(cache_swa_v)[0], f(state_ret)[0], f(cache_mem_k)[0], f(cache_mem_v)[0]
    in_maps = []
    for c in range(NCORES):
        sl = slice(16 * c, 16 * (c + 1))
        m = dict(shared)
        m["x_p"] = xp[c]
        m["x_s"] = np.ascontiguousarray(xs[sl].reshape(128, 1024))
        m["mem_p"] = mp[c]
        m["c_k"] = np.ascontiguousarray(ck[sl].reshape(16, 128, 128))
        m["c_v"] = np.ascontiguousarray(cv[sl].reshape(16, 128, 128))
        m["s_ret"] = np.ascontiguousarray(sr[sl])
        m["c_mk"] = np.ascontiguousarray(cmk[sl].reshape(16, 256, 512))
        m["c_mv"] = np.ascontiguousarray(cmv[sl].reshape(16, 256, 512))
        in_maps.append(m)
    res = run_bass_kernel_spmd(nc, in_maps, core_ids=list(range(NCORES)))
    R = res.results
    cat = lambda k: np.stack([np.asarray(r[k], dtype=np.float32) for r in R], 0)
    y_prompt = cat("y_p").reshape(8, 4096, 1024)
    y_sample = cat("y_s").reshape(128, 8, 1024)
    swk_p = cat("o_swk_p").reshape(1, 8, 128, 2, 64)
    swv_p = cat("o_swv_p").reshape(1, 8, 128, 2, 64)
    ret_p = cat("o_ret_p").reshape(1, 8, 8, 64, 64)
    mk_p = cat("o_mk_p").reshape(1, 8, 256, 4, 128)
    mv_p = cat("o_mv_p").reshape(1, 8, 256, 4, 128)
    swk_s = cat("o_swk_s").reshape(1, 128, 128, 2, 64)
    swv_s = cat("o_swv_s").reshape(1, 128, 128, 2, 64)
    ret_s = cat("o_ret_s").reshape(1, 128, 8, 64, 64)
    return (y_prompt, y_sample, swk_p, swv_p, ret_p, mk_p, mv_p, swk_s, swv_s, ret_s)
```

```python
import contextlib
import numpy as np
import ml_dtypes
import concourse.bass as bass
import concourse.mybir as mybir
from concourse.bass_utils import run_bass_kernel_spmd

F32 = mybir.dt.float32
BF16 = mybir.dt.bfloat16
ALU = mybir.AluOpType
AF = mybir.ActivationFunctionType
AX = mybir.AxisListType

NCORES = 8
NT = 33
GRP = 2
GROUPS = [list(range(g, min(g + GRP, NT - 1))) for g in range(0, NT - 1, GRP)] + [[NT - 1]]
NG = len(GROUPS)
NJ = 22
EPS = 1e-6


class Op:
    __slots__ = ("idx", "stream", "fn", "deps", "chan", "is_dma", "sig", "sigval", "chanval", "cost", "nbytes")


class Prog:
    STREAMS = ("pe", "act", "dve", "pool", "sp")

    def __init__(self):
        self.ops = []
        self.tok = {}
        self.chan_n = {}
        self.chan_last = {}
        self.out_dmas = []
        self.cost = {"dve": 0.0, "pool": 0.0}

    DEFCOST = {"pe": 0.5, "act": 0.6, "dve": 0.5, "pool": 0.7, "sp": 0.1}

    def add(self, stream, fn, reads=(), writes=(), chan=None, is_out=False, cost=None, nbytes=0):
        op = Op()
        op.cost = self.DEFCOST[stream] if cost is None else cost
        op.nbytes = nbytes
        op.idx = len(self.ops)
        op.stream = stream
        op.fn = fn
        op.chan = chan
        op.is_dma = chan is not None
        op.sig = False
        op.sigval = 0
        op.chanval = 0
        deps = {}
        for t in reads:
            st = self.tok.setdefault(t, [None, []])
            if st[0] is not None:
                deps[st[0]] = "raw"
            if isinstance(t, tuple) and t[0] == "bk":
                for r in st[1]:
                    if self.ops[r].stream != stream:
                        deps.setdefault(r, "rr")
        for t in writes:
            st = self.tok.setdefault(t, [None, []])
            if st[0] is not None:
                deps.setdefault(st[0], "waw")
            for r in st[1]:
                deps.setdefault(r, "war")
        for t in reads:
            self.tok[t][1].append(op.idx)
        for t in writes:
            self.tok[t] = [op.idx, []]
        deps.pop(op.idx, None)
        op.deps = deps
        if op.is_dma:
            prev = self.chan_last.get(chan)
            if prev is not None:
                deps.setdefault(prev, "chan")
            self.chan_last[chan] = op.idx
            self.chan_n[chan] = self.chan_n.get(chan, 0) + 1
            op.chanval = 16 * self.chan_n[chan]
            if is_out:
                self.out_dmas.append(op.idx)
        self.ops.append(op)
        return op

    def ew(self, fn, reads=(), writes=(), cost=1.0, only=None):
        if only is None:
            eng = "dve" if self.cost["dve"] <= self.cost["pool"] + 2.0 * cost else "pool"
        else:
            eng = only
        self.cost[eng] += cost * (2.0 if eng == "pool" else 1.0)
        return self.add(eng, fn, reads, writes, cost=(0.15 + 1.0 * cost) if eng == "pool" else (0.1 + 0.6 * cost))

    do_schedule = True
    LAT = 0.45
    DMA_FIXED = 2.0
    DMA_BW = 300e3

    def schedule(self):
        import heapq
        ops = self.ops
        n = len(ops)
        succ = [[] for _ in range(n)]
        indeg = [0] * n
        for o in ops:
            for d in o.deps:
                succ[d].append(o.idx)
                indeg[o.idx] += 1
        dur = [(o.cost if not o.is_dma else self.DMA_FIXED + o.nbytes / self.DMA_BW) for o in ops]
        prio = [0.0] * n
        for i in range(n - 1, -1, -1):
            m = 0.0
            for s_ in succ[i]:
                if prio[s_] > m:
                    m = prio[s_]
            prio[i] = dur[i] + m
        eng_free = {s_: 0.0 for s_ in self.STREAMS}
        dma_free = 0.0
        finish = [0.0] * n
        ready_t = [0.0] * n
        avail = [i for i in range(n) if indeg[i] == 0]
        order = []
        while avail:
            best, bkey = None, None
            for i in avail:
                o = ops[i]
                st_ = max(eng_free[o.stream], ready_t[i])
                key = (int(st_ / 0.35), -prio[i], i)
                if bkey is None or key < bkey:
                    best, bkey = i, key
            avail.remove(best)
            o = ops[best]
            st_ = max(eng_free[o.stream], ready_t[best])
            if o.is_dma:
                issue = 1.0 if o.stream == "pool" else 0.08
                eng_free[o.stream] = st_ + issue
                t0 = max(st_ + issue, dma_free)
                dma_free = t0 + o.nbytes / self.DMA_BW
                finish[best] = dma_free + self.DMA_FIXED
            else:
                eng_free[o.stream] = st_ + o.cost
                finish[best] = st_ + o.cost
            order.append(best)
            for s_ in succ[best]:
                r = finish[best] + self.LAT
                if r > ready_t[s_]:
                    ready_t[s_] = r
                indeg[s_] -= 1
                if indeg[s_] == 0:
                    avail.append(s_)
        assert len(order) == n
        self.model_time = max(finish)
        return order

    def _needs_wait(self, c, p, kind):
        if p.is_dma:
            return True
        if c.stream == p.stream:
            if c.is_dma:
                return True
            if c.stream == "pe":
                return False
            return True
        return True

    def emit(self, nc, stack):
        ops = self.ops
        for c in ops:
            for d, kind in c.deps.items():
                if self._needs_wait(c, ops[d], kind):
                    ops[d].sig = True
        cnt = {s: 0 for s in self.STREAMS}
        order = self.schedule() if self.do_schedule else list(range(len(ops)))
        self._order = order
        for o in (ops[i] for i in order):
            if not o.is_dma and o.sig:
                cnt[o.stream] += 1
                o.sigval = cnt[o.stream]
        esem = {s: stack.enter_context(nc.semaphore("s_" + s)) for s in self.STREAMS if s != "sp"}
        csem = {c: stack.enter_context(nc.semaphore("c_" + c)) for c in self.chan_n}
        block = stack.enter_context(nc.Block())
        streams = {s: [ops[i] for i in order if ops[i].stream == s] for s in self.STREAMS}

        def run_stream(sname, eng):
            waited = {}
            for o in streams[sname]:
                for d, kind in o.deps.items():
                    p = ops[d]
                    if not self._needs_wait(o, p, kind):
                        continue
                    if p.is_dma:
                        sem, val, key = csem[p.chan], p.chanval, "c_" + p.chan
                    else:
                        sem, val, key = esem[p.stream], p.sigval, "e_" + p.stream
                    if waited.get(key, 0) >= val:
                        continue
                    waited[key] = val
                    eng.wait_ge(sem, val)
                ins = o.fn(eng)
                if o.is_dma:
                    ins.then_inc(csem[o.chan], 16)
                elif o.sig:
                    ins.then_inc(esem[o.stream], 1)
            if sname == "sp":
                done = set()
                for d in self.out_dmas:
                    p = ops[d]
                    if p.chan in done:
                        continue
                    done.add(p.chan)
                    eng.wait_ge(csem[p.chan], 16 * self.chan_n[p.chan])

        @block.tensor
        def _(e):
            run_stream("pe", e)

        @block.scalar
        def _(e):
            run_stream("act", e)

        @block.vector
        def _(e):
            run_stream("dve", e)

        @block.gpsimd
        def _(e):
            run_stream("pool", e)

        @block.sync
        def _(e):
            run_stream("sp", e)


def host_consts():
    bf = ml_dtypes.bfloat16
    k = np.arange(128)[:, None]
    q = np.arange(128)[None, :]
    masks = np.zeros((128, 4, 128), np.float32)
    masks[:, 0, :] = (k <= q)
    masks[:, 1, :] = (k > q)
    masks[:, 2, :] = (k // 8 == q // 8) & (k % 8 <= q % 8)
    masks[:, 3, :] = (k > q % 8)
    colmask = np.zeros((128, 16, 128), np.float32)
    for b in range(16):
        colmask[:, b, :] = (q // 8 == b)
    rowmask = np.zeros((128, 16), np.float32)
    for b in range(16):
        rowmask[:, b] = (np.arange(128) // 8 == b)
    f32 = np.float32
    tabs = np.zeros((NT, 128, 4, 64), np.float32)
    inv_a = (f32(1.0) / (f32(10000.0) ** (np.arange(32, dtype=f32) / f32(32)))).astype(f32)
    inv_r = (f32(10000.0) ** (-np.linspace(0.0, 1.0, 32, dtype=f32))).astype(f32)
    p = np.arange(128)
    for t in range(NT):
        pos = (t * 128 + p) if t < 32 else (16384 + p % 8)
        pos = pos.astype(f32)
        ang = (pos[:, None] * inv_a[None, :]).astype(f32).astype(np.float64)
        c, s = np.cos(ang), np.sin(ang)
        tabs[t, :, 0, :] = np.concatenate([c, c], -1)
        tabs[t, :, 1, :] = np.concatenate([-s, s], -1)
        ang = (pos[:, None] * inv_r[None, :]).astype(f32).astype(np.float64)
        c, s = np.cos(ang), np.sin(ang)
        tabs[t, :, 2, :] = np.stack([c, c], -1).reshape(128, 64)
        tabs[t, :, 3, :] = np.stack([-s, s], -1).reshape(128, 64)
    log_g = np.log(1.0 - np.exp2(-5.0 - np.arange(8, dtype=np.float64)))
    dec = np.zeros((128, 2, 2, 8), np.float32)
    for v, pp in enumerate([p, p % 8]):
        dec[:, v, 0, :] = np.exp((pp[:, None] + 1.0) * log_g[None, :])
        dec[:, v, 1, :] = np.exp(-(pp[:, None] + 1.0) * log_g[None, :]) / 8.0
    cdec = np.zeros((128, 2, 4), np.float32)
    for c_ in range(4):
        h = 2 * c_ + p // 64
        cdec[:, 0, c_] = np.exp(128.0 * log_g[h])
        cdec[:, 1, c_] = np.exp(8.0 * log_g[h])
    return {
        "c_ident": np.eye(128).astype(bf),
        "c_masks": masks.astype(bf),
        "c_colmask": colmask.astype(bf),
        "c_rowmask": rowmask,
        "c_tabs": tabs,
        "c_dec": dec,
        "c_cdec": cdec,
    }


class _Stop(Exception):
    pass


def build_program(stage=99, skip=()):
    nc = bass.Bass("TRN2", target_bir_lowering=False)
    P = Prog()

    def ckpt(k):
        if stage <= k:
            raise _Stop()
    din = lambda n, s, d=F32: nc.dram_tensor(n, list(s), d, kind="ExternalInput").ap()
    dout = lambda n, s: nc.dram_tensor(n, list(s), F32, kind="ExternalOutput").ap()
    x_p = din("x_p", [4096, 1024]); x_s = din("x_s", [128, 1024]); mem_p = din("mem_p", [256, 1024])
    c_k = din("c_k", [16, 128, 128]); c_v = din("c_v", [16, 128, 128]); s_ret = din("s_ret", [16, 8, 64, 64])
    c_mk = din("c_mk", [16, 256, 512]); c_mv = din("c_mv", [16, 256, 512])
    w_in = din("w_in", [1024, 2816]); w_out = din("w_out", [1024, 1024]); w_mq = din("w_mq", [1024, 512])
    w_mkv = din("w_mkv", [1024, 1024]); w_mo = din("w_mo", [512, 1024]); w_gu = din("w_gu", [1024, 5632])
    w_down = din("w_down", [2816, 1024])
    norm_mix = din("norm_mix", [1, 1024]); norm_cross = din("norm_cross", [1, 1024])
    norm_mem = din("norm_mem", [1, 1024]); norm_ffn = din("norm_ffn", [1, 1024])
    q_norm_a = din("q_norm_a", [1, 64]); k_norm_a = din("k_norm_a", [1, 64]); sinks = din("sinks", [1, 8])
    q_norm_m = din("q_norm_m", [1, 128]); k_norm_m = din("k_norm_m", [1, 128])
    c_ident = din("c_ident", [128, 128], BF16); c_masks = din("c_masks", [128, 4, 128], BF16)
    c_colmask = din("c_colmask", [128, 16, 128], BF16); c_rowmask = din("c_rowmask", [128, 16])
    c_tabs = din("c_tabs", [NT, 128, 4, 64]); c_dec = din("c_dec", [128, 2, 2, 8]); c_cdec = din("c_cdec", [128, 2, 4])
    y_p = dout("y_p", [4096, 1024]); y_s = dout("y_s", [128, 1024])
    o_swk_p = dout("o_swk_p", [128, 128]); o_swv_p = dout("o_swv_p", [128, 128]); o_ret_p = dout("o_ret_p", [8, 64, 64])
    o_mk_p = dout("o_mk_p", [256, 512]); o_mv_p = dout("o_mv_p", [256, 512])
    o_swk_s = dout("o_swk_s", [16, 128, 128]); o_swv_s = dout("o_swv_s", [16, 128, 128]); o_ret_s = dout("o_ret_s", [16, 8, 64, 64])
    wgu_d = nc.dram_tensor("wgu_d", [NJ, 128, 2048], BF16, kind="Internal").ap()
    wd_d = nc.dram_tensor("wd_d", [NJ, 128, 1024], BF16, kind="Internal").ap()

    def bc_row(ap, n):
        return bass.AP(ap.tensor, 0, [[0, 128], [1, n]])

    with contextlib.ExitStack() as st:
        st.enter_context(nc.allow_non_contiguous_dma(reason="small strided constant / cache-row loads"))
        sb = lambda n, s, d=F32: st.enter_context(nc.sbuf_tensor(n, list(s), d))
        W_in = sb("W_in", [128, 8, 2816], BF16); W_out = sb("W_out", [128, 8, 1024], BF16)
        W_mq = sb("W_mq", [128, 8, 512], BF16); W_mo = sb("W_mo", [128, 4, 1024], BF16)
        ident = sb("ident", [128, 128], BF16); masks = sb("masks", [128, 4, 128], BF16)
        ident32 = sb("ident32", [128, 128], F32)
        colmask = sb("colmask", [128, 16, 128], BF16); rowmask = sb("rowmask", [128, 16])
        dec = sb("dec", [128, 2, 2, 8]); cdec = sb("cdec", [128, 2, 4]); nh = sb("nh", [128, 8])
        gcols = sb("gcols", [128, 4, 8])
        gqa = sb("gqa", [128, 2, 64]); gka = sb("gka", [128, 2, 64])
        gqm = sb("gqm", [128, 128]); gkm = sb("gkm", [128, 128]); esink = sb("esink", [128, 8])
        xres = [sb("xres%d" % i, [128, 1024]) for i in range(2 * GRP)]

        class _Set:
            def __init__(self, n):
                self.n = n
                self.xn = sb("xn%d" % n, [128, 1024], BF16)
                self.xnT = sb("xnT%d" % n, [128, 8, 128], BF16)
                self.ss = sb("ss%d" % n, [128, 1]); self.rs = sb("rs%d" % n, [128, 1])
                self.tmpA = sb("tmpA%d" % n, [128, 512]); self.tmpB = sb("tmpB%d" % n, [128, 512]); self.tmpC = sb("tmpC%d" % n, [128, 512])
                self.st8 = sb("st8%d" % n, [128, 8]); self.rh8 = sb("rh8%d" % n, [128, 8]); self.den8 = sb("den8%d" % n, [128, 8])
                self.XN = [self.T("xnT", 0), self.T("xnT", 1)]

            def T(self, *tok):
                return ("S%d" % self.n,) + tok
        SETS = [_Set(0), _Set(1)]
        proj = sb("proj", [128, 2816])
        tab = [sb("tab%d" % i, [128, 4, 64]) for i in range(2)]
        tg = sb("tg", [128, 4, 64])
        qa_bf = sb("qa_bf", [128, 512], BF16); ka_f = sb("ka_f", [128, 128]); ka_bf = sb("ka_bf", [128, 128], BF16)
        qaT = sb("qaT", [128, 512], BF16)
        kaT = [sb("kaT%d" % i, [128, 128], BF16) for i in range(2)]
        Vext = [sb("Vext%d" % i, [128, 2, 65], BF16) for i in range(2)]
        qr_bf = sb("qr_bf", [128, 512], BF16); kr_bf = sb("kr_bf", [128, 512], BF16); vr_bf = sb("vr_bf", [128, 512], BF16)
        qkT = sb("qkT", [128, 8, 128], BF16)
        innT = sb("innT", [128, 8, 128], BF16)
        S_f = sb("S_f", [128, 4, 64]); S_bd = [sb("S_bd%d" % i, [128, 4, 128], BF16) for i in range(2)]
        kmT = sb("kmT", [128, 4, 256], BF16); Vm = sb("Vm", [128, 2, 4, 129], BF16)
        xn3T = sb("xn3T", [128, 8, GRP * 128], BF16)
        wgu = [sb("wgu%d" % i, [128, 8, 256], BF16) for i in range(3)]
        wdb = [sb("wdb%d" % i, [128, 1024], BF16) for i in range(2)]
        sgb = [sb("sgb%d" % i, [128, GRP * 128]) for i in range(2)]
        hT = [sb("hT%d" % i, [128, GRP * 128], BF16) for i in range(2)]
        stg = [sb("stg%d" % i, [128, 1024]) for i in range(2)]
        stg_bf = sb("stg_bf", [128, 1024], BF16)
        wdb.append(stg_bf)
        WDT = [("wdb", 0), ("wdb", 1), "stg_bf"]
        WGV = [w_[:] for w_ in wgu] + [stg[i][:].bitcast(BF16).rearrange("p (k c) -> p k c", k=8) for i in range(2)]
        WGT = [("wgu", 0), ("wgu", 1), ("wgu", 2), ("stg", 0), ("stg", 1)]
        NWG = 5
        kcT = sb("kcT", [128, 16, 128], BF16); Vc = sb("Vc", [128, 16, 2, 65], BF16)
        kmTb = sb("kmTb", [128, 1024], BF16); Vmb = sb("Vmb", [128, 2, 4, 129], BF16)
        L0 = SETS[0]
        L0.pT = [sb("pT%d" % i, [128, 512], BF16)[:] for i in range(2)]
        L0.mix_bf = sb("mix_bf", [128, 1024], BF16)[:]
        L0.qm_bf = sb("qm_bf", [128, 512], BF16)[:]; L0.qmT = sb("qmT", [128, 512], BF16)[:]
        L0.pmT = sb("pmT", [128, 1024], BF16)[:]; L0.om_bf = sb("om_bf", [128, 512], BF16)[:]; L0.omT = sb("omT", [128, 512], BF16)[:]
        L1 = SETS[1]
        kc_flat = kcT[:].rearrange("p b r -> p (b r)")
        vc_flat = Vc[:].rearrange("p b h d -> p (b h d)")
        L1.mix_bf = kc_flat[:, 0:1024]; L1.pmT = kc_flat[:, 1024:2048]
        L1.pT = [vc_flat[:, 0:512], vc_flat[:, 512:1024]]
        L1.qm_bf = vc_flat[:, 1024:1536]; L1.qmT = vc_flat[:, 1536:2048]
        L1.om_bf = kmTb[:, 0:512]; L1.omT = kmTb[:, 512:1024]
        LATE_TOKS = []
        ps = st.enter_context(nc.psum_tensor("ps", [128, 4096], F32))

        def bank(b, n=512):
            return ps[:, b * 512:b * 512 + n]

        def bankbf(b, n=1024):
            return ps[:, b * 512:(b + 1) * 512].bitcast(BF16)[:, 0:n]

        BK = lambda b: ("bk", b)

        class _Banks:
            def __init__(self, banks):
                self.free = list(banks)

            def get(self):
                assert self.free, "PSUM banks exhausted in program order"
                return self.free.pop(0)

            def put(self, *bs):
                for b in bs:
                    assert b not in self.free
                    self.free.append(b)
        BAS = [_Banks(range(0, 4)), _Banks(range(4, 8))]

        class _BAProxy:
            cur = 0

            def get(self):
                return BAS[self.cur].get()

            def put(self, *bs):
                for b in bs:
                    BAS[0 if b < 4 else 1].put(b)
        BA = _BAProxy()
        cnt = {"dma": 0}

        def dma(stream, out, in_, reads, writes, chan, is_out=False):
            if "wcast" in skip and stream == "pool":
                return None
            nb = out.size() * (2 if out.dtype == BF16 else 4)
            return P.add(stream, lambda e: e.dma_start(out=out, in_=in_), reads, writes, chan=chan, is_out=is_out, nbytes=nb)

        dma("sp", ident[:], c_ident, [], ["ident"], "k0")
        dma("sp", masks[:], c_masks, [], ["masks"], "k1")
        P.add("act", lambda e: e.activation(out=ident32[:], in_=ident[:], func=AF.Copy), ["ident"], ["ident32"])
        dma("sp", colmask[:], c_colmask, [], ["colmask"], "k2")
        dma("sp", rowmask[:], c_rowmask, [], ["rowmask"], "k3")
        dma("sp", dec[:], c_dec, [], ["dec"], "k4")
        dma("sp", cdec[:], c_cdec, [], ["cdec"], "k5")
        for i, nrm in enumerate([] if "gcols" in skip else [norm_mix, norm_cross, norm_ffn, norm_mem]):
            dma("sp", gcols[:, i, :], nrm[0].rearrange("(k p) -> p k", p=128), [], [("gcols", i)], "k6")
        for t_, src in (() if "bcast" in skip else ((gqa, q_norm_a), (gka, k_norm_a))):
            nm = "gqa" if t_ is gqa else "gka"
            dma("sp", t_[:, 0, :], bc_row(src, 64), [], [nm + "0"], "k7")
            dma("sp", t_[:, 1, 0:32], bass.AP(src.tensor, 32, [[0, 128], [1, 32]]), [], [nm + "1"], "k8")
            dma("sp", t_[:, 1, 32:64], bass.AP(src.tensor, 0, [[0, 128], [1, 32]]), [], [nm + "2"], "k9")
        GQA = ["gqa0", "gqa1", "gqa2"]; GKA = ["gka0", "gka1", "gka2"]
        if "bcast2" not in skip:
            dma("sp", gqm[:], bc_row(q_norm_m, 128), [], ["gqm"], "k10")
            dma("sp", gkm[:], bc_row(k_norm_m, 128), [], ["gkm"], "k11")
            dma("sp", esink[:], bc_row(sinks, 8), [], ["esink"], "k12")
            P.add("act", lambda e: e.activation(out=esink[:], in_=esink[:], func=AF.Exp), ["esink"], ["esink"])
        P.add("pool", lambda e: e.memset(nh[:], -0.5), [], ["nh"])
        P.add("pool", lambda e: e.memset(S_f[:], 0.0), [], ["S_f"])
        for i in range(2):
            P.add("pool", lambda e, i=i: e.memset(S_bd[i][:], 0.0), [], [("S_bd", i)])
            P.add("pool", lambda e, i=i: e.memset(Vext[i][:], 1.0), [], [("Vext", i)])
        P.add("pool", lambda e: e.memset(Vm[:], 1.0), [], ["Vm"])
        P.add("pool", lambda e: e.memset(Vmb[:], 1.0), [], ["Vmb"])
        WIN = []
        for kvh in range(2):
            for g_ in range(4):
                hh = kvh * 4 + g_
                src = w_in[:, hh * 64:(hh + 1) * 64].rearrange("(k p) d -> p k d", p=128)
                dst = W_in[:, :, g_ * 128 + kvh * 64: g_ * 128 + (kvh + 1) * 64]
                dma("pool", dst, src, [], [("W_in", kvh)], "w%d" % kvh)
            WIN.append(("W_in", kvh))
        for i in range(4):
            src = w_in[:, 768 + i * 512:768 + (i + 1) * 512].rearrange("(k p) n -> p k n", p=128)
            dma("pool", W_in[:, :, 512 + i * 512:1024 + i * 512], src, [], [("W_in", 2 + i)], "w%d" % (2 + i)); WIN.append(("W_in", 2 + i))
        dma("pool", W_in[:, :, 2560:2816], w_in[:, 512:768].rearrange("(k p) n -> p k n", p=128), [], [("W_in", 6)], "w6"); WIN.append(("W_in", 6))
        dma("pool", W_out[:], w_out.rearrange("(k p) n -> p k n", p=128), [], ["W_out"], "w7")
        dma("pool", W_mq[:], w_mq.rearrange("(k p) n -> p k n", p=128), [], ["W_mq"], "w8")
        dma("pool", W_mo[:], w_mo.rearrange("(k p) n -> p k n", p=128), [], ["W_mo"], "w9")

        def rstd_chain(S, src_tok_list, st_ap, rh_ap, n, scale, rd, wr):
            P.add("pool", lambda e: e.tensor_scalar(out=rh_ap, in0=st_ap, scalar1=scale, scalar2=EPS, op0=ALU.mult, op1=ALU.add), rd, wr, cost=0.25)
            P.add("pool", lambda e: e.tensor_tensor(out=rh_ap, in0=rh_ap, in1=nh[:, 0:n], op=ALU.pow), wr + ["nh"], wr, cost=0.3)

        def norm_T(S, src, src_tok, gi, dstT, dst_tok, defer=False):
            P.add("act", lambda e: e.activation(out=S.xn[:], in_=src, func=AF.Square, accum_out=S.ss[:]), [src_tok], [S.T("xn"), S.T("ss")], cost=1.1)
            rstd_chain(S, None, S.ss[:], S.rs[:], 1, 1.0 / 1024, [S.T("ss")], [S.T("rs")])
            if defer:
                P.add("act", lambda e: e.activation(out=S.xn[:], in_=src, func=AF.Copy), [src_tok], [S.T("xn")], cost=1.1)
            else:
                P.add("act", lambda e: e.activation(out=S.xn[:], in_=src, func=AF.Copy, scale=S.rs[:]), [src_tok, S.T("rs")], [S.T("xn")], cost=1.1)

            ba, bd = BA.get(), BA.get()

            def tr(e):
                for k in range(8):
                    bb = ba if k < 4 else bd
                    ins = e.transpose(out=bankbf(bb)[:, (k % 4) * 128:(k % 4 + 1) * 128], in_=S.xn[:, k * 128:(k + 1) * 128], identity=ident[:])
                return ins
            P.add("pe", tr, [S.T("xn"), "ident"], [BK(ba), BK(bd)], cost=0.6)

            def ev_a(e):
                for k in range(0, 4):
                    ins = e.activation(out=dstT[:, k, :], in_=bankbf(ba)[:, k * 128:(k + 1) * 128], func=AF.Copy, scale=gcols[:, gi, k:k + 1])
                return ins

            def ev_d(e):
                for k in range(4, 8):
                    ins = e.tensor_scalar(out=dstT[:, k, :], in0=bankbf(bd)[:, (k - 4) * 128:(k - 3) * 128], scalar1=gcols[:, gi, k:k + 1], scalar2=None, op0=ALU.mult)
                return ins
            P.add("act", ev_a, [BK(ba), ("gcols", gi)], [dst_tok + (0,)], cost=0.9)
            P.add("dve", ev_d, [BK(bd), ("gcols", gi)], [dst_tok + (1,)], cost=0.7)
            BA.put(ba, bd)

        def head_rstd(S, src3, H, D, rd, scratch, scratch_tok):
            sq = scratch[:, 0:H * D].rearrange("p (h d) -> p h d", d=D)
            P.ew(lambda e: e.tensor_tensor(out=sq, in0=src3, in1=src3, op=ALU.mult), rd, [scratch_tok], cost=H * D / 512.0, only="dve")
            P.add("dve", lambda e: e.tensor_reduce(out=S.st8[:, 0:H], in_=sq, axis=AX.X, op=ALU.add), [scratch_tok], [S.T("st8")])
            rstd_chain(S, None, S.st8[:, 0:H], S.rh8[:, 0:H], H, 1.0 / D, [S.T("st8")], [S.T("rh8")])

        def bc_h(ap2, H):
            return ap2.unsqueeze(1).to_broadcast([128, H, ap2.shape[1]])

        def bc_d(ap2, D):
            return ap2.unsqueeze(2).to_broadcast([128, ap2.shape[1], D])

        def rot_half(S, src3, H, CT, ST, rd):
            A = S.tmpA[:, 0:H * 64].rearrange("p (h d) -> p h d", d=64)
            B = S.tmpB[:, 0:H * 64].rearrange("p (h d) -> p h d", d=64)
            c = H * 64 / 512.0
            P.ew(lambda e: e.tensor_tensor(out=A, in0=src3, in1=bc_h(CT, H), op=ALU.mult), rd, [S.T("tmpA")], cost=c)
            P.ew(lambda e: e.tensor_tensor(out=B[:, :, 0:32], in0=src3[:, :, 32:64], in1=bc_h(ST[:, 0:32], H), op=ALU.mult), rd, [S.T("tmpB", 0)], cost=c / 2)
            P.ew(lambda e: e.tensor_tensor(out=B[:, :, 32:64], in0=src3[:, :, 0:32], in1=bc_h(ST[:, 32:64], H), op=ALU.mult), rd, [S.T("tmpB", 1)], cost=c / 2)
            P.ew(lambda e: e.tensor_tensor(out=A, in0=A, in1=B, op=ALU.add), [S.T("tmpA"), S.T("tmpB", 0), S.T("tmpB", 1)], [S.T("tmpA")], cost=c)
            return A

        def rot_pair(S, src3, CF, SF, rd):
            A = S.tmpA[:].rearrange("p (h d) -> p h d", d=64)
            B4 = S.tmpB[:].rearrange("p (h i two) -> p h i two", h=8, two=2)
            s4 = src3.rearrange("p h (i two) -> p h i two", two=2)
            SF3 = SF.rearrange("p (i two) -> p i two", two=2)
            P.ew(lambda e: e.tensor_tensor(out=A, in0=src3, in1=bc_h(CF, 8), op=ALU.mult), rd, [S.T("tmpA")], cost=1)
            P.ew(lambda e: e.tensor_tensor(out=B4[:, :, :, 0], in0=s4[:, :, :, 1], in1=bc_h(SF3[:, :, 0], 8), op=ALU.mult), rd, [S.T("tmpB", 0)], cost=0.5)
            P.ew(lambda e: e.tensor_tensor(out=B4[:, :, :, 1], in0=s4[:, :, :, 0], in1=bc_h(SF3[:, :, 1], 8), op=ALU.mult), rd, [S.T("tmpB", 1)], cost=0.5)
            P.ew(lambda e: e.tensor_tensor(out=S.tmpA[:], in0=S.tmpA[:], in1=S.tmpB[:], op=ALU.add), [S.T("tmpA"), S.T("tmpB", 0), S.T("tmpB", 1)], [S.T("tmpA")], cost=1)
            return A

        def transposes(pairs, rd, dst, dst_tok, n):
            b0 = BA.get()

            def tr(e):
                for i, src in enumerate(pairs):
                    ins = e.transpose(out=bankbf(b0)[:, i * 128:(i + 1) * 128], in_=src, identity=ident[:])
                return ins
            P.add("pe", tr, list(rd) + ["ident"], [BK(b0)], cost=0.08 * n)
            wr = dst_tok if isinstance(dst_tok, list) else [dst_tok]
            P.add("act", lambda e: e.activation(out=dst, in_=bankbf(b0)[:, 0:n * 128], func=AF.Copy), [BK(b0)], wr, cost=0.25 + n * 0.1)
            BA.put(b0)

        def memkv():
            S = SETS[0]
            memT = [xn3T[:, :, i * 128:(i + 1) * 128] for i in range(2)]
            for mt in range(2):
                dma("sp", stg[mt][:], mem_p[mt * 128:(mt + 1) * 128, :], [], [("stg", mt)], "sg%d" % mt)
                norm_T(S, stg[mt][:], ("stg", mt), 3, memT[mt], ("xn3T", mt))
            ckpt(1.1)
            for c in range(4):
                s = c % 3
                dma("pool", wgu[s][:], w_mkv[:, c * 256:(c + 1) * 256].rearrange("(k p) n -> p k n", p=128), [], [("wgu", s)], "pg%d" % s)
                for mt in range(2):
                    bk = BA.get()

                    def mmk(e, s=s, mt=mt, bk=bk):
                        for k in range(8):
                            ins = e.matmul(bank(bk, 256), lhsT=memT[mt][:, k, :], rhs=wgu[s][:, k, :], start=(k == 0), stop=(k == 7))
                        return ins
                    P.add("pe", mmk, [("xn3T", mt, 0), ("xn3T", mt, 1), ("wgu", s)], [BK(bk)])
                    P.add("act", lambda e, mt=mt, c=c, bk=bk: e.activation(out=proj[:, mt * 1024 + c * 256: mt * 1024 + (c + 1) * 256], in_=bank(bk, 256), func=AF.Copy),
                          [BK(bk)], [("proj", mt * 2 + c // 2)])
                    BA.put(bk)
            ckpt(1.2)
            for mt in range(2):
                kv = proj[:, mt * 1024:(mt + 1) * 1024]
                k3 = kv[:, 0:512].rearrange("p (h d) -> p h d", d=128)
                rdk = [("proj", mt * 2)]
                head_rstd(S, k3, 4, 128, rdk, S.tmpA, S.T("tmpA"))
                A3 = S.tmpA[:].rearrange("p (h d) -> p h d", d=128)
                P.ew(lambda e, k3=k3, A3=A3: e.tensor_tensor(out=A3, in0=k3, in1=bc_d(S.rh8[:, 0:4], 128), op=ALU.mult), rdk + [S.T("rh8"), S.T("tmpA")], [S.T("tmpA")])
                C3 = S.tmpC[:].rearrange("p (h d) -> p h d", d=128)
                P.ew(lambda e, A3=A3, C3=C3: e.tensor_tensor(out=C3, in0=A3, in1=bc_h(gkm[:], 4), op=ALU.mult), [S.T("tmpA"), "gkm"], [S.T("tmpC")])
                ckpt(1.3)
                dma("sp", o_mk_p[mt * 128:(mt + 1) * 128, :], S.tmpC[:], [S.T("tmpC")], [], "omk", is_out=True)
                dma("sp", o_mv_p[mt * 128:(mt + 1) * 128, :], kv[:, 512:1024], [("proj", mt * 2 + 1)], [], "omv", is_out=True)
                P.ew(lambda e: e.tensor_copy(out=stg_bf[:, 0:512], in_=S.tmpC[:]), [S.T("tmpC")], ["stg_bf"])
                P.ew(lambda e, mt=mt, kv=kv: e.tensor_copy(out=Vm[:, mt, :, 0:128], in_=kv[:, 512:1024].rearrange("p (h d) -> p h d", d=128)),
                     [("proj", mt * 2 + 1), "Vm"], ["Vm"])

                ckpt(1.4)

                b0 = BA.get()

                def trk(e, b0=b0):
                    for h in range(4):
                        ins = e.transpose(out=bankbf(b0)[:, h * 128:(h + 1) * 128], in_=stg_bf[:, h * 128:(h + 1) * 128], identity=ident[:])
                    return ins
                P.add("pe", trk, ["stg_bf", "ident"], [BK(b0)])
                P.add("act", lambda e, mt=mt, b0=b0: e.activation(out=kmT[:, :, mt * 128:(mt + 1) * 128], in_=bankbf(b0)[:, 0:512].rearrange("p (h m) -> p h m", h=4), func=AF.Copy),
                      [BK(b0), "kmT"], ["kmT"])
                BA.put(b0)

        if stage > 1:
            try:
                memkv()
            except _Stop:
                pass
        if stage >= 3:
            for j in range(NJ):
                s = j % 3
                for two in range(2):
                    src = w_gu[:, two * 2816 + j * 128: two * 2816 + (j + 1) * 128].rearrange("(k p) n -> p k n", p=128)
                    dma("pool", wgu[s][:, :, two * 128:(two + 1) * 128], src, [], [("wgu", s)], "pg%d" % s)
                dma("sp", wgu_d[j], wgu[s][:].rearrange("p k c -> p (k c)"), [("wgu", s)], [("wgud", j)], "cs%d" % s)
                s_d = j % 2
                dma("pool", wdb[s_d][:], w_down[j * 128:(j + 1) * 128, :], [], [WDT[s_d]], "pd%d" % s_d)
                dma("sp", wd_d[j], wdb[s_d][:], [WDT[s_d]], [("wdd", j)], "ct%d" % s_d)

        def ffn_load(j):
            s = j % NWG
            dma("sp", WGV[s].rearrange("p k c -> p (k c)"), wgu_d[j], [("wgud", j)], [WGT[s]], "fg%d" % s)

        def ffn_load_d(j):
            s = j % 3
            dma("sp", wdb[s][:], wd_d[j], [("wdd", j)], [WDT[s]], "fd%d" % s)

        mult, add = ALU.mult, ALU.add

        def tt(out, in0, in1, op):
            return lambda e: e.tensor_tensor(out=out, in0=in0, in1=in1, op=op)

        def cpy(out, in_):
            return lambda e: e.tensor_copy(out=out, in_=in_)

        def v3(ap, d):
            return ap.rearrange("p (h d) -> p h d", d=d)

        def sample_preload():
            S1_ = SETS[1]
            fence_reads = [S1_.T("mix", 0), S1_.T("mix", 1), S1_.T("mix", 2), S1_.T("pmT", 0), S1_.T("pmT", 1), S1_.T("pT", 0), S1_.T("pT", 1),
                           S1_.T("qm_bf"), S1_.T("qmT"), S1_.T("om_bf", 0), S1_.T("om_bf", 1), S1_.T("omT")]
            P.add("pool", lambda e: e.memset(Vc[:], 1.0), fence_reads, ["Vc", "kcT", "kmTb"])
            for hf in range(2):
                b0 = hf * 8
                dma("sp", v3(stg[0][:], 128), c_k[b0:b0 + 8].rearrange("b r f -> r b f"), [], [("stg", 0)], "sg0")
                dma("sp", v3(stg[1][:], 128), c_v[b0:b0 + 8].rearrange("b r f -> r b f"), [], [("stg", 1)], "sg1")
                P.ew(cpy(stg_bf[:], stg[0][:]), [("stg", 0)], ["stg_bf"], cost=2)
                P.ew(cpy(Vc[:, b0:b0 + 8, :, 0:64], stg[1][:].rearrange("p (b h d) -> p b h d", b=8, h=2)), [("stg", 1), "Vc"], ["Vc"], cost=2)

                bq = BA.get()

                def trc(e, bq=bq):
                    for b in range(8):
                        ins = e.transpose(out=bankbf(bq)[:, b * 128:(b + 1) * 128], in_=stg_bf[:, b * 128:(b + 1) * 128], identity=ident[:])
                    return ins
                P.add("pe", trc, ["stg_bf", "ident"], [BK(bq)])
                P.add("act", lambda e, b0=b0, bq=bq: e.activation(out=kcT[:, b0:b0 + 8, :].rearrange("p b r -> p (b r)"), in_=bankbf(bq), func=AF.Copy), [BK(bq), "kcT"], ["kcT"])
                BA.put(bq)
            dma("sp", o_swk_s[:, 0:120, :], c_k[:, 8:128, :], [], [], "osk", is_out=True)
            dma("sp", o_swv_s[:, 0:120, :], c_v[:, 8:128, :], [], [], "osv", is_out=True)

        def tile_mixer(T, i, xi):
            samp = (T == NT - 1)
            S = SETS[i % 2]
            BA.cur = i % 2
            X = xres[xi]; XT = ("xres", xi)
            sl = T % 2
            tb = tab[T % 2]; TB = ("tab", T % 2)
            dv = 1 if samp else 0
            dma("sp", tb[:], c_tabs[T], [], [TB], "tb%d" % (T % 2))
            norm_T(S, X[:], XT, 0, S.xnT, S.T("xnT"), defer=True)
            groups = [(0, 512), (512, 1024), (1024, 1536), (1536, 2048), (2048, 2560), (2560, 2816)]
            if T == 0:
                S1 = SETS[1]
                xn32 = proj[:, 0:1024]
                P.add("act", lambda e: e.activation(out=xn32, in_=X[:], func=AF.Copy, scale=S.rs[:]), [XT, S.T("rs")], [("proj", 0), ("proj", 1)], cost=1.1)
                b32a, b32b = BA.get(), BA.get()

                def tr32(e):
                    for k in range(8):
                        bb = b32a if k < 4 else b32b
                        ins = e.transpose(out=bank(bb)[:, (k % 4) * 128:(k % 4 + 1) * 128], in_=xn32[:, k * 128:(k + 1) * 128], identity=ident32[:])
                    return ins
                P.add("pe", tr32, [("proj", 0), ("proj", 1), "ident32"], [BK(b32a), BK(b32b)], cost=2.0)

                def ev32a(e):
                    for k in range(4):
                        ins = e.activation(out=S1.tmpA[:, k * 128:(k + 1) * 128], in_=bank(b32a)[:, k * 128:(k + 1) * 128], func=AF.Copy, scale=gcols[:, 0, k:k + 1])
                    return ins

                def ev32b(e):
                    for k in range(4, 8):
                        ins = e.tensor_scalar(out=S1.tmpB[:, (k - 4) * 128:(k - 3) * 128], in0=bank(b32b)[:, (k - 4) * 128:(k - 3) * 128], scalar1=gcols[:, 0, k:k + 1], scalar2=None, op0=ALU.mult)
                    return ins
                P.add("act", ev32a, [BK(b32a), ("gcols", 0)], [S1.T("tmpA")], cost=0.9)
                P.add("dve", ev32b, [BK(b32b), ("gcols", 0)], [S1.T("tmpB", 0), S1.T("tmpB", 1)], cost=0.7)
                BA.put(b32a, b32b)
                for c in range(4):
                    c0 = 768 + c * 256
                    for kh in range(2):
                        dma("sp", stg[kh][:].rearrange("p (k n) -> p k n", k=4), w_in[kh * 512:(kh + 1) * 512, c0:c0 + 256].rearrange("(k p) n -> p k n", p=128),
                            [], [("stg", kh)], "sg%d" % kh)
                    bk = BA.get()

                    def mm32(e, bk=bk):
                        for k in range(8):
                            xt_ = S1.tmpA if k < 4 else S1.tmpB
                            ins = e.matmul(bank(bk, 256), lhsT=xt_[:, (k % 4) * 128:(k % 4 + 1) * 128], rhs=stg[k // 4][:, (k % 4) * 256:(k % 4 + 1) * 256],
                                           start=(k == 0), stop=(k == 7))
                        return ins
                    P.add("pe", mm32, [S1.T("tmpA"), S1.T("tmpB", 0), S1.T("tmpB", 1), ("stg", 0), ("stg", 1)], [BK(bk)], cost=3.6)
                    P.add("act", lambda e, c=c, bk=bk: e.activation(out=proj[:, 512 + c * 256:768 + c * 256], in_=bank(bk, 256), func=AF.Copy), [BK(bk)], [("proj", 1 + c // 2)], cost=0.45)
                    BA.put(bk)
            for gi, (c0, c1) in enumerate(groups):
                if T == 0 and gi in (1, 2):
                    continue
                bk = BA.get()

                def mmp(e, c0=c0, c1=c1, bk=bk):
                    for k in range(8):
                        ins = e.matmul(bank(bk, c1 - c0), lhsT=S.xnT[:, k, :], rhs=W_in[:, k, c0:c1], start=(k == 0), stop=(k == 7))
                    return ins
                P.add("pe", mmp, S.XN + WIN, [BK(bk)], cost=0.03 + 8 * (c1 - c0) / 2400.0)
                P.add("act", lambda e, c0=c0, c1=c1, bk=bk: e.activation(out=proj[:, c0:c1], in_=bank(bk, c1 - c0), func=AF.Copy, scale=S.rs[:]), [BK(bk), S.T("rs")], [("proj", gi)], cost=0.75)
                BA.put(bk)
            ckpt(4)
            P.ew(tt(tg[:, 0, :], tb[:, 0, :], gqa[:, 0, :], mult), [TB] + GQA, [("tg", 0)], cost=0.15)
            P.ew(tt(tg[:, 1, :], tb[:, 1, :], gqa[:, 1, :], mult), [TB] + GQA, [("tg", 1)], cost=0.15)
            P.ew(tt(tg[:, 2, :], tb[:, 0, :], gka[:, 0, :], mult), [TB] + GKA, [("tg", 2)], cost=0.15)
            P.ew(tt(tg[:, 3, :], tb[:, 1, :], gka[:, 1, :], mult), [TB] + GKA, [("tg", 3)], cost=0.15)
            q3 = v3(proj[:, 0:512], 64)
            head_rstd(S, q3, 8, 64, [("proj", 0)], S.tmpC, S.T("tmpC"))
            A = rot_half(S, q3, 8, tg[:, 0, :], tg[:, 1, :], [("proj", 0), ("tg", 0), ("tg", 1)])
            P.ew(tt(v3(qa_bf[:], 64), A, bc_d(S.rh8[:, 0:8], 64), mult), [S.T("tmpA"), S.T("rh8")], ["qa_bf"])
            k3 = v3(proj[:, 2560:2688], 64)
            head_rstd(S, k3, 2, 64, [("proj", 5)], S.tmpC, S.T("tmpC"))
            A = rot_half(S, k3, 2, tg[:, 2, :], tg[:, 3, :], [("proj", 5), ("tg", 2), ("tg", 3)])
            P.ew(tt(v3(ka_f[:], 64), A, bc_d(S.rh8[:, 0:2], 64), mult), [S.T("tmpA"), S.T("rh8")], ["ka_f"], cost=0.25)
            P.ew(cpy(ka_bf[:], ka_f[:]), ["ka_f"], ["ka_bf"], cost=0.25)
            P.ew(cpy(Vext[sl][:, :, 0:64], v3(proj[:, 2688:2816], 64)), [("proj", 5), ("Vext", sl)], [("Vext", sl)], cost=0.25)

            bq, bk_ = BA.get(), BA.get()

            def tra(e):
                for c in range(4):
                    ins = e.transpose(out=bankbf(bq)[:, c * 128:(c + 1) * 128], in_=qa_bf[:, c * 128:(c + 1) * 128], identity=ident[:])
                return ins
            P.add("pe", tra, ["qa_bf", "ident"], [BK(bq)], cost=0.35)
            P.add("pe", lambda e: e.transpose(out=bankbf(bk_)[:, 0:128], in_=ka_bf[:], identity=ident[:]), ["ka_bf", "ident"], [BK(bk_)], cost=0.1)
            P.add("act", lambda e: e.activation(out=qaT[:], in_=bankbf(bq)[:, 0:512], func=AF.Copy), [BK(bq)], ["qaT"], cost=0.65)
            P.add("dve", cpy(kaT[sl][:], bankbf(bk_)[:, 0:128]), [BK(bk_)], [("kaT", sl)], cost=0.25)
            BA.put(bq, bk_)
            if T == NT - 2:
                dma("sp", o_swk_p, ka_f[:], ["ka_f"], [], "okp", is_out=True)
                dma("sp", o_swv_p, proj[:, 2688:2816], [("proj", 5)], [], "ovp", is_out=True)
            if samp:
                for b in range(16):
                    dma("sp", o_swk_s[b, 120:128, :], ka_f[b * 8:(b + 1) * 8, :], ["ka_f"], [], "osk%d" % (b % 4), is_out=True)
                    dma("sp", o_swv_s[b, 120:128, :], proj[b * 8:(b + 1) * 8, 2688:2816], [("proj", 5)], [], "osv%d" % (b % 4), is_out=True)
            blocks = []
            if samp:
                for b in range(16):
                    blocks.append((kcT[:, b, :], "kcT", Vc[:, b, :, :], "Vc", (colmask[:, b, :], masks[:, 3, :])))
                blocks.append((kaT[sl][:], ("kaT", sl), Vext[sl][:], ("Vext", sl), masks[:, 2, :]))
            else:
                if T > 0:
                    blocks.append((kaT[1 - sl][:], ("kaT", 1 - sl), Vext[1 - sl][:], ("Vext", 1 - sl), masks[:, 1, :]))
                blocks.append((kaT[sl][:], ("kaT", sl), Vext[sl][:], ("Vext", sl), masks[:, 0, :]))
            n = 0
            for kvh in range(2):
                ob = BA.get()
                for bi, (kT, kTt, Vx, Vt, mk) in enumerate(blocks):
                    sbk = BA.get()
                    pi = n % 2
                    n += 1
                    P.add("pe", lambda e, kT=kT, kvh=kvh, sbk=sbk: e.matmul(bank(sbk), lhsT=kT[kvh * 64:(kvh + 1) * 64, :], rhs=qaT[kvh * 64:(kvh + 1) * 64, :], start=True, stop=True),
                          [kTt, "qaT"], [BK(sbk)])
                    P.add("act", lambda e, sbk=sbk, pi=pi: e.activation(out=S.pT[pi][:], in_=bank(sbk), func=AF.Exp, scale=0.125), [BK(sbk)], [S.T("pT", pi)], cost=0.65)
                    BA.put(sbk)
                    for mk1 in (mk if isinstance(mk, tuple) else (mk,)):
                        P.ew(tt(v3(S.pT[pi][:], 128), v3(S.pT[pi][:], 128), bc_h(mk1, 4), mult), [S.T("pT", pi), "masks", "colmask"], [S.T("pT", pi)])

                    def pv(e, pi=pi, Vx=Vx, kvh=kvh, ob=ob, bi=bi, nb=len(blocks)):
                        for g in range(4):
                            ins = e.matmul(bank(ob)[:, g * 65:(g + 1) * 65], lhsT=S.pT[pi][:, g * 128:(g + 1) * 128], rhs=Vx[:, kvh, :],
                                           start=(bi == 0 and g == 0), stop=(bi == nb - 1 and g == 3), skip_group_check=True)
                        return ins
                    P.add("pe", pv, [S.T("pT", pi), Vt], [BK(ob)])
                o3 = bank(ob)[:, 0:260].rearrange("p (g d) -> p g d", d=65)
                P.add("dve", tt(S.den8[:, 0:4], o3[:, :, 64], esink[:, kvh * 4:(kvh + 1) * 4], add), [BK(ob), "esink"], [S.T("den8")])
                P.add("dve", lambda e: e.reciprocal(out=S.den8[:, 0:4], in_=S.den8[:, 0:4]), [S.T("den8")], [S.T("den8")])
                P.add("dve", tt(v3(S.mix_bf[:, kvh * 256:(kvh + 1) * 256], 64), o3[:, :, 0:64], bc_d(S.den8[:, 0:4], 64), mult), [BK(ob), S.T("den8")], [S.T("mix", kvh)])
                BA.put(ob)
            ckpt(5)
            q_lo, k_lo = S.qm_bf, S.om_bf
            QLO, KLO = [S.T("qm_bf")], [S.T("om_bf", 0), S.T("om_bf", 1)]
            hilo = (T == 0)
            A = rot_pair(S, v3(proj[:, 512:1024], 64), tb[:, 2, :], tb[:, 3, :], [("proj", 1), TB])
            if hilo:
                P.ew(tt(A, A, bc_d(dec[:, dv, 0, :], 64), mult), [S.T("tmpA"), "dec"], [S.T("tmpA")])
                P.ew(cpy(qr_bf[:], S.tmpA[:]), [S.T("tmpA")], ["qr_bf"])
                P.ew(tt(q_lo[:], S.tmpA[:], qr_bf[:], ALU.subtract), [S.T("tmpA"), "qr_bf"], QLO)
            else:
                P.ew(tt(v3(qr_bf[:], 64), A, bc_d(dec[:, dv, 0, :], 64), mult), [S.T("tmpA"), "dec"], ["qr_bf"])
            A = rot_pair(S, v3(proj[:, 1024:1536], 64), tb[:, 2, :], tb[:, 3, :], [("proj", 2), TB])
            if hilo:
                P.ew(tt(A, A, bc_d(dec[:, dv, 1, :], 64), mult), [S.T("tmpA"), "dec"], [S.T("tmpA")])
                P.ew(cpy(kr_bf[:], S.tmpA[:]), [S.T("tmpA")], ["kr_bf"])
                P.ew(tt(k_lo[:], S.tmpA[:], kr_bf[:], ALU.subtract), [S.T("tmpA"), "kr_bf"], KLO)
            else:
                P.ew(tt(v3(kr_bf[:], 64), A, bc_d(dec[:, dv, 1, :], 64), mult), [S.T("tmpA"), "dec"], ["kr_bf"])
            P.ew(cpy(vr_bf[:], proj[:, 1536:2048]), [("proj", 3)], ["vr_bf"])
            transposes([qr_bf[:, c * 128:(c + 1) * 128] for c in range(4)] + [kr_bf[:, c * 128:(c + 1) * 128] for c in range(4)],
                       ["qr_bf", "kr_bf"], qkT[:].rearrange("p a t -> p (a t)"), "qkT", 8)
            if hilo:
                transposes([q_lo[:, c * 128:(c + 1) * 128] for c in range(4)] + [k_lo[:, c * 128:(c + 1) * 128] for c in range(4)],
                           QLO + KLO, innT[:].rearrange("p a t -> p (a t)"), [("innT", 0), ("innT", 1)], 8)

            bi0, bi1 = BA.get(), BA.get()
            bis = (bi0, bi1)

            def inn(e):
                for c in range(4):
                    for hl in range(2):
                        pr = slice(hl * 64, (hl + 1) * 64)
                        o_ = bank(bis[hl])[:, c * 128:(c + 1) * 128]
                        ins = e.matmul(o_, lhsT=qkT[pr, 4 + c, :], rhs=qkT[pr, c, :], start=(c == 0), stop=(c == 3 and not hilo), skip_group_check=True)
                        if hilo:
                            e.matmul(o_, lhsT=qkT[pr, 4 + c, :], rhs=innT[pr, c, :], start=False, stop=False, skip_group_check=True)
                            ins = e.matmul(o_, lhsT=innT[pr, 4 + c, :], rhs=qkT[pr, c, :], start=False, stop=(c == 3), skip_group_check=True)
                return ins
            P.add("pe", inn, ["qkT"] + ([("innT", 0), ("innT", 1)] if hilo else []), [BK(bi0), BK(bi1)], cost=1.6 if hilo else 0.6)
            mk = masks[:, 2 if samp else 0, :]
            for hl in range(2):
                P.add("dve", tt(innT[:, hl * 4:(hl + 1) * 4, :], v3(bank(bis[hl]), 128), bc_h(mk, 4), mult), [BK(bis[hl]), "masks"], [("innT", hl)], cost=0.65)
            BA.put(bi0, bi1)
            b7 = BA.get()

            def orm(e):
                for h in range(8):
                    c, hl = h // 2, h % 2
                    ins = e.matmul(bank(b7)[:, h * 64:(h + 1) * 64], lhsT=innT[:, hl * 4 + c, :], rhs=vr_bf[:, h * 64:(h + 1) * 64],
                                   start=(h == 0), stop=False, skip_group_check=True)
                return ins
            P.add("pe", orm, [("innT", 0), ("innT", 1), "vr_bf"], [BK(b7)], cost=0.5)
            if not samp:
                cur = T % 2

                def crs(e):
                    for c in range(4):
                        ins = e.matmul(bank(b7)[:, c * 128:(c + 1) * 128], lhsT=qkT[:, c, :], rhs=S_bd[cur][:, c, :], start=False, stop=(c == 3), skip_group_check=True)
                    return ins
                P.add("pe", crs, ["qkT", ("S_bd", cur)], [BK(b7)], cost=0.3)
                bkv = BA.get()
                def kvm(e):
                    for c in range(4):
                        ins = e.matmul(bank(bkv)[:, c * 128:(c + 1) * 128], lhsT=kr_bf[:, c * 128:(c + 1) * 128], rhs=vr_bf[:, c * 128:(c + 1) * 128],
                                       start=(c == 0), stop=(c == 3), skip_group_check=True)
                    return ins
                P.add("pe", kvm, ["kr_bf", "vr_bf"], [BK(bkv)], cost=0.3)
                kv4 = v3(bank(bkv), 128)
                P.add("dve", tt(S_f[0:64], kv4[0:64, :, 0:64], S_f[0:64], add), [BK(bkv), "S_f"], ["S_f"], cost=0.3)
                P.add("dve", tt(S_f[64:128], kv4[64:128, :, 64:128], S_f[64:128], add), [BK(bkv), "S_f"], ["S_f"], cost=0.3)
                BA.put(bkv)
                P.ew(tt(S_f[:], S_f[:], bc_d(cdec[:, 0, :], 64), mult), ["S_f", "cdec"], ["S_f"], cost=0.5)
                nx = 1 - cur
                P.ew(cpy(S_bd[nx][0:64, :, 0:64], S_f[0:64]), ["S_f", ("S_bd", nx)], [("S_bd", nx)], cost=0.25)
                P.ew(cpy(S_bd[nx][64:128, :, 64:128], S_f[64:128]), ["S_f", ("S_bd", nx)], [("S_bd", nx)], cost=0.25)
                if T == NT - 2:
                    dma("sp", o_ret_p.rearrange("(c hl) d e -> (hl d) c e", hl=2), S_f[:], ["S_f"], [], "orp", is_out=True)
            else:
                for b in range(16):
                    s2 = b % 2
                    S0 = stg[s2][:, 0:256].rearrange("p (c e) -> p c e", e=64)
                    dma("sp", S0, s_ret[b].rearrange("(c hl) d e -> (hl d) c e", hl=2), [], [("stg", s2)], "sg%d" % s2)
                    P.ew(cpy(S_bd[s2][0:64, :, 0:64], S0[0:64]), [("stg", s2), ("S_bd", s2)], [("S_bd", s2)], cost=0.25)
                    P.ew(cpy(S_bd[s2][64:128, :, 64:128], S0[64:128]), [("stg", s2), ("S_bd", s2)], [("S_bd", s2)], cost=0.25)
                    padb = (qa_bf, S.qm_bf)
                    padt = ("qa_bf", S.T("qm_bf"))[s2]
                    P.ew(tt(v3(padb[s2][:], 128), qkT[:, 0:4, :], bc_h(colmask[:, b, :], 4), mult), ["qkT", "colmask"], [padt])

                    def crs(e, b=b, s2=s2, padb=padb):
                        for c in range(4):
                            ins = e.matmul(bank(b7)[:, c * 128:(c + 1) * 128], lhsT=padb[s2][:, c * 128:(c + 1) * 128], rhs=S_bd[s2][:, c, :],
                                           start=False, stop=(b == 15 and c == 3), skip_group_check=True)
                        return ins
                    P.add("pe", crs, [padt, ("S_bd", s2)], [BK(b7)])
                    P.ew(lambda e, b=b, s2=s2: e.tensor_scalar(out=S.pT[s2][:], in0=kr_bf[:], scalar1=rowmask[:, b:b + 1], scalar2=None, op0=mult),
                         ["kr_bf", "rowmask"], [S.T("pT", s2)])
                    kb = BA.get()

                    def kvm(e, s2=s2, kb=kb):
                        for c in range(4):
                            ins = e.matmul(bank(kb)[:, c * 128:(c + 1) * 128], lhsT=S.pT[s2][:, c * 128:(c + 1) * 128], rhs=vr_bf[:, c * 128:(c + 1) * 128],
                                           start=(c == 0), stop=(c == 3), skip_group_check=True)
                        return ins
                    P.add("pe", kvm, [S.T("pT", s2), "vr_bf"], [BK(kb)])
                    kv4 = v3(bank(kb), 128)
                    So = S.tmpC[:, s2 * 256:(s2 + 1) * 256].rearrange("p (c e) -> p c e", e=64)
                    SoT = S.T("tmpC2", s2)
                    P.add("dve", tt(So[0:64], kv4[0:64, :, 0:64], S0[0:64], add), [BK(kb), ("stg", s2), S.T("tmpC")], [SoT])
                    P.add("dve", tt(So[64:128], kv4[64:128, :, 64:128], S0[64:128], add), [BK(kb), ("stg", s2), SoT], [SoT])
                    BA.put(kb)
                    P.ew(tt(So, So, bc_d(cdec[:, 1, :], 64), mult), [SoT, "cdec"], [SoT], cost=0.5)
                    dma("sp", o_ret_s[b].rearrange("(c hl) d e -> (hl d) c e", hl=2), So, [SoT], [], "ors%d" % s2, is_out=True)
            P.add("act", lambda e: e.activation(out=S.tmpA[:], in_=bank(b7), func=AF.Square), [BK(b7)], [S.T("tmpA")], cost=0.65)
            P.add("dve", lambda e: e.tensor_reduce(out=S.st8[:], in_=v3(S.tmpA[:], 64), axis=AX.X, op=add), [S.T("tmpA")], [S.T("st8")])
            rstd_chain(S, None, S.st8[:], S.rh8[:], 8, 1.0 / 64, [S.T("st8")], [S.T("rh8")])
            P.add("dve", tt(v3(S.tmpB[:], 64), v3(bank(b7), 64), bc_d(S.rh8[:], 64), mult), [BK(b7), S.T("rh8")], [S.T("tmpB", 0), S.T("tmpB", 1)], cost=0.65)
            BA.put(b7)
            P.add("act", lambda e: e.activation(out=S.tmpC[:], in_=proj[:, 2048:2560], func=AF.Silu), [("proj", 4)], [S.T("tmpC"), S.T("tmpC2", 0), S.T("tmpC2", 1)])
            P.ew(tt(S.mix_bf[:, 512:1024], S.tmpB[:], S.tmpC[:], mult), [S.T("tmpB", 0), S.T("tmpB", 1), S.T("tmpC")], [S.T("mix", 2)])
            ckpt(6)
            transposes([S.mix_bf[:, k * 128:(k + 1) * 128] for k in range(8)], [S.T("mix", 0), S.T("mix", 1), S.T("mix", 2)], S.xnT[:].rearrange("p k t -> p (k t)"), S.XN, 8)

            for hf in range(2):
                by = BA.get()

                def mmo(e, hf=hf, by=by):
                    for k in range(8):
                        ins = e.matmul(bank(by), lhsT=S.xnT[:, k, :], rhs=W_out[:, k, hf * 512:(hf + 1) * 512], start=(k == 0), stop=(k == 7))
                    return ins
                P.add("pe", mmo, S.XN + ["W_out"], [BK(by)], cost=1.75)
                P.add("dve", tt(X[:, hf * 512:(hf + 1) * 512], X[:, hf * 512:(hf + 1) * 512], bank(by), add), [XT, BK(by)], [XT], cost=0.65)
                BA.put(by)

        def tile_cross(T, i, xi):
            samp = (T == NT - 1)
            S = SETS[i % 2]
            X = xres[xi]; XT = ("xres", xi)
            norm_T(S, X[:], XT, 1, S.xnT, S.T("xnT"), defer=True)

            bq_ = BA.get()

            def mmq(e):
                for k in range(8):
                    ins = e.matmul(bank(bq_), lhsT=S.xnT[:, k, :], rhs=W_mq[:, k, :], start=(k == 0), stop=(k == 7))
                return ins
            P.add("pe", mmq, S.XN + ["W_mq"], [BK(bq_)], cost=1.75)
            P.add("act", lambda e: e.activation(out=S.tmpA[:], in_=bank(bq_), func=AF.Copy, scale=S.rs[:]), [BK(bq_), S.T("rs")], [S.T("tmpA")], cost=0.75)
            BA.put(bq_)
            A3 = v3(S.tmpA[:], 128)
            head_rstd(S, A3, 4, 128, [S.T("tmpA")], S.tmpC, S.T("tmpC"))
            P.ew(tt(A3, A3, bc_d(S.rh8[:, 0:4], 128), mult), [S.T("tmpA"), S.T("rh8")], [S.T("tmpA")])
            P.ew(tt(v3(S.qm_bf[:], 128), A3, bc_h(gqm[:], 4), mult), [S.T("tmpA"), "gqm"], [S.T("qm_bf")])
            transposes([S.qm_bf[:, h * 128:(h + 1) * 128] for h in range(4)], [S.T("qm_bf")], S.qmT[:], S.T("qmT"), 4)
            SC = 128.0 ** -0.5
            nb = 16 if samp else 1
            bo = (BA.get(), BA.get())
            for b in range(nb):
                if samp:
                    s2 = b % 2
                    kb = stg_bf[:] if s2 == 0 else stg[0][:].bitcast(BF16)[:, 0:1024]
                    kb_tok = "stg_bf" if s2 == 0 else ("stg", 0)
                    vb = Vmb[:] if s2 == 0 else stg[1][:].bitcast(BF16)[:, 0:1032].rearrange("p (mc h d) -> p mc h d", mc=2, h=4)
                    vb_tok = "Vmb" if s2 == 0 else ("stg", 1)
                    if b == 1:
                        P.add("pool", lambda e, vb=vb: e.memset(vb, 1.0), [], [vb_tok])
                    dma("pool", kb.rearrange("p (mc f) -> p mc f", mc=2), c_mk[b].rearrange("(mc m) f -> m mc f", m=128), [], [kb_tok], "pk%d" % s2)
                    for mc in range(2):
                        dma("pool", vb[:, mc, :, 0:128], c_mv[b, mc * 128:(mc + 1) * 128, :].rearrange("m (h d) -> m h d", h=4), [vb_tok], [vb_tok], "pv%d" % s2)
                    bt = BAS[1].get()

                    def trk(e, bt=bt, kb=kb):
                        for h in range(4):
                            for mc in range(2):
                                s8 = h * 2 + mc
                                ins = e.transpose(out=bankbf(bt)[:, s8 * 128:(s8 + 1) * 128], in_=kb[:, mc * 512 + h * 128: mc * 512 + (h + 1) * 128], identity=ident[:])
                        return ins
                    P.add("pe", trk, [kb_tok, "ident"], [BK(bt)], cost=0.7)
                    kmb = kmTb[:] if s2 == 0 else qkT[:].rearrange("p a t -> p (a t)")
                    kmb_tok = "kmTb" if s2 == 0 else "qkT"
                    P.add("act", lambda e, bt=bt, kmb=kmb: e.activation(out=kmb, in_=bankbf(bt), func=AF.Copy), [BK(bt)], [kmb_tok], cost=1.1)
                    BA.put(bt)
                    kt_tok, v_tok = kmb_tok, vb_tok
                    kt_of = lambda h, mc, kmb=kmb: kmb[:, (h * 2 + mc) * 128:(h * 2 + mc + 1) * 128]
                    v_of = lambda h, mc, vb=vb: vb[:, mc, h, :]
                else:
                    kt_tok, v_tok = "kmT", "Vm"
                    kt_of = lambda h, mc: kmT[:, h, mc * 128:(mc + 1) * 128]
                    v_of = lambda h, mc: Vm[:, mc, h, :]

                bs = (BAS[1].get(), BAS[1].get()) if samp else (BA.get(), BA.get())
                if samp and b % 2 == 1:
                    pm = (S.pT[0][:], S.pT[1][:]); pm_tok = [S.T("pT", 0), S.T("pT", 1)]
                else:
                    pm = (S.pmT[:, 0:512], S.pmT[:, 512:1024]); pm_tok = [S.T("pmT", 0), S.T("pmT", 1)]

                def scm(e, kt_of=kt_of, bs=bs):
                    for h in range(4):
                        for mc in range(2):
                            s8 = h * 2 + mc
                            ins = e.matmul(bank(bs[s8 // 4])[:, (s8 % 4) * 128:(s8 % 4 + 1) * 128], lhsT=kt_of(h, mc), rhs=S.qmT[:, h * 128:(h + 1) * 128],
                                           start=(s8 % 4 == 0), stop=(s8 % 4 == 3), skip_group_check=True)
                    return ins
                P.add("pe", scm, [kt_tok, S.T("qmT")], [BK(bs[0]), BK(bs[1])], cost=0.7)
                for hb in range(2):
                    P.add("act", lambda e, hb=hb, bs=bs, pm=pm: e.activation(out=pm[hb], in_=bank(bs[hb]), func=AF.Exp, scale=SC), [BK(bs[hb])], [pm_tok[hb]], cost=0.65)
                BA.put(*bs)
                if samp:
                    for hb in range(2):
                        P.ew(tt(v3(pm[hb], 128), v3(pm[hb], 128), bc_h(colmask[:, b, :], 4), mult), [pm_tok[hb], "colmask"], [pm_tok[hb]], cost=1)

                def pvm(e, b=b, v_of=v_of, pm=pm):
                    for h in range(4):
                        for mc in range(2):
                            s8 = h * 2 + mc
                            ins = e.matmul(bank(bo[h // 2])[:, (h % 2) * 256:(h % 2) * 256 + 129], lhsT=pm[s8 // 4][:, (s8 % 4) * 128:(s8 % 4 + 1) * 128], rhs=v_of(h, mc),
                                           start=(b == 0 and h % 2 == 0 and mc == 0), stop=(b == nb - 1 and h % 2 == 1 and mc == 1), skip_group_check=True)
                    return ins
                P.add("pe", pvm, pm_tok + [v_tok], [BK(bo[0]), BK(bo[1])], cost=0.8)
            for hh in range(2):
                om2 = bank(bo[hh]).rearrange("p (h d) -> p h d", d=256)
                P.add("dve", lambda e, hh=hh, om2=om2: e.reciprocal(out=S.den8[:, 4 + 2 * hh:6 + 2 * hh], in_=om2[:, :, 128]), [BK(bo[hh])], [S.T("den8b", hh)], cost=0.15)
                P.add("dve", tt(v3(S.om_bf[:, hh * 256:(hh + 1) * 256], 128), om2[:, :, 0:128], bc_d(S.den8[:, 4 + 2 * hh:6 + 2 * hh], 128), mult),
                      [BK(bo[hh]), S.T("den8b", hh)], [S.T("om_bf", hh)], cost=0.4)
            BA.put(*bo)
            transposes([S.om_bf[:, h * 128:(h + 1) * 128] for h in range(4)], [S.T("om_bf", 0), S.T("om_bf", 1)], S.omT[:], S.T("omT"), 4)
            for hf in range(2):
                by = BA.get()

                def mmo(e, hf=hf, by=by):
                    for c in range(4):
                        ins = e.matmul(bank(by), lhsT=S.omT[:, c * 128:(c + 1) * 128], rhs=W_mo[:, c, hf * 512:(hf + 1) * 512], start=(c == 0), stop=(c == 3))
                    return ins
                P.add("pe", mmo, [S.T("omT"), "W_mo"], [BK(by)], cost=0.9)
                P.add("dve", tt(X[:, hf * 512:(hf + 1) * 512], X[:, hf * 512:(hf + 1) * 512], bank(by), add), [XT, BK(by)], [XT], cost=0.65)
                BA.put(by)
            norm_T(S, X[:], XT, 2, xn3T[:, :, i * 128:(i + 1) * 128], ("xn3T", i))

        def ffn_group(g):
            tiles = GROUPS[g]
            nt_ = len(tiles)
            xo = (g % 2) * GRP
            XR = [("xn3T", i, h) for i in range(nt_) for h in range(2)]
            NTOK = nt_ * 128
            if (NT - 1) in tiles:
                ffn_load_d(2)
            if (NT - 1) in tiles or 0 in tiles:
                for j_ in range(3, NWG):
                    ffn_load(j_)
            yb = []
            for i in range(nt_):
                yb += [BAS[i % 2].get(), BAS[i % 2].get()]
            for j in range(NJ):
                s = j % NWG
                hs = j % 2
                bg, bu = BAS[0].get(), BAS[1].get()

                def gmm(e, s=s, bg=bg):
                    for k in range(8):
                        ins = e.matmul(bank(bg, NTOK), lhsT=WGV[s][:, k, 0:128], rhs=xn3T[:, k, 0:NTOK], start=(k == 0), stop=(k == 7))
                    return ins

                def umm(e, s=s, bu=bu):
                    for k in range(8):
                        ins = e.matmul(bank(bu, NTOK), lhsT=WGV[s][:, k, 128:256], rhs=xn3T[:, k, 0:NTOK], start=(k == 0), stop=(k == 7))
                    return ins
                P.add("pe", gmm, XR + [WGT[s]], [BK(bg)], cost=0.05 + 8 * max(NTOK, 128) / 2400.0)
                P.add("pe", umm, XR + [WGT[s]], [BK(bu)], cost=0.05 + 8 * max(NTOK, 128) / 2400.0)
                P.add("act", lambda e, hs=hs, bg=bg: e.activation(out=sgb[hs][:, 0:NTOK], in_=bank(bg, NTOK), func=AF.Silu), [BK(bg)], [("sgb", hs)], cost=0.45)
                P.add("dve", tt(hT[hs][:, 0:NTOK], sgb[hs][:, 0:NTOK], bank(bu, NTOK), mult), [("sgb", hs), BK(bu)], [("hT", hs)], cost=0.45)
                BA.put(bg, bu)

                def dn(e, s=s, hs=hs, j=j):
                    for i in range(nt_):
                        for hf in range(2):
                            ins = e.matmul(bank(yb[2 * i + hf]), lhsT=hT[hs][:, i * 128:(i + 1) * 128], rhs=wdb[j % 3][:, hf * 512:(hf + 1) * 512],
                                           start=(j == 0), stop=(j == NJ - 1))
                    return ins
                P.add("pe", dn, [("hT", hs), WDT[j % 3]], [BK(b_) for b_ in yb], cost=0.05 + 2 * nt_ * 0.22)
                if j + NWG < NJ:
                    ffn_load(j + NWG)
                if j + 3 < NJ:
                    ffn_load_d(j + 3)
            for i, T in enumerate(tiles):
                xi = xo + i
                for hf in range(2):
                    P.add("dve", tt(xres[xi][:, hf * 512:(hf + 1) * 512], xres[xi][:, hf * 512:(hf + 1) * 512], bank(yb[2 * i + hf]), add),
                          [("xres", xi), BK(yb[2 * i + hf])], [("xres", xi)], cost=0.65)
                dst = y_s if T == NT - 1 else y_p[T * 128:(T + 1) * 128, :]
                dma("sp", dst, xres[xi][:], [("xres", xi)], [], "yo%d" % xi, is_out=True)
            BA.put(*yb)

        def main_loop():
          ckpt(3)
          for g in range(NG):
              xo = (g % 2) * GRP
              for i, T in enumerate(GROUPS[g]):
                  src = x_s if T == NT - 1 else x_p[T * 128:(T + 1) * 128, :]
                  dma("sp", xres[xo + i][:], src, [], [("xres", xo + i)], "xl%d" % (xo + i))
              has_samp = (NT - 1) in GROUPS[g]
              uses_stg = has_samp or (0 in GROUPS[g])
              for j in range(3 if uses_stg else NWG):
                  ffn_load(j)
              for j in range(2 if has_samp else 3):
                  ffn_load_d(j)
              for i, T in enumerate(GROUPS[g]):
                  if T == NT - 1:
                      sample_preload()
                  tile_mixer(T, i, xo + i)
                  ckpt(7)
                  tile_cross(T, i, xo + i)
                  ckpt(8)
              ckpt(9)
              ffn_group(g)
              ckpt(10 + g)
        try:
            main_loop()
        except _Stop:
            pass
        P.emit(nc, st)
    return nc


_CACHE = {}


def kernel(x_prompt, x_sample, mem_prompt, cache_swa_k, cache_swa_v, state_ret, cache_mem_k, cache_mem_v,
           norm_mix, w_in, q_norm_a, k_norm_a, sinks, w_out, norm_cross, norm_mem, w_mq, w_mkv,
           q_norm_m, k_norm_m, w_mo, norm_ffn, w_gu, w_down):
    f = lambda a: np.ascontiguousarray(np.asarray(a, dtype=np.float32))
    if "nc" not in _CACHE:
        _CACHE["nc"] = build_program()
        _CACHE["consts"] = host_consts()
    nc = _CACHE["nc"]
    consts = _CACHE["consts"]
    shared = {
        "w_in": f(w_in)[0], "w_out": f(w_out)[0], "w_mq": f(w_mq)[0], "w_mkv": f(w_mkv)[0], "w_mo": f(w_mo)[0],
        "w_gu": f(w_gu)[0], "w_down": f(w_down)[0],
        "norm_mix": f(norm_mix), "norm_cross": f(norm_cross), "norm_mem": f(norm_mem), "norm_ffn": f(norm_ffn),
        "q_norm_a": f(q_norm_a), "k_norm_a": f(k_norm_a), "sinks": f(sinks), "q_norm_m": f(q_norm_m), "k_norm_m": f(k_norm_m),
    }
    shared.update(consts)
    xp, xs, mp = f(x_prompt), f(x_sample), f(mem_prompt)
    ck, cv, sr, cmk, cmv = f(cache_swa_k)[0], f(cache_swa_v)[0], f(state_ret)[0], f(cache_mem_k)[0], f(cache_mem_v)[0]
    in_maps = []
    for c in range(NCORES):
        sl = slice(16 * c, 16 * (c + 1))
        m = dict(shared)
        m["x_p"] = xp[c]
        m["x_s"] = np.ascontiguousarray(xs[sl].reshape(128, 1024))
        m["mem_p"] = mp[c]
        m["c_k"] = np.ascontiguousarray(ck[sl].reshape(16, 128, 128))
        m["c_v"] = np.ascontiguousarray(cv[sl].reshape(16, 128, 128))
        m["s_ret"] = np.ascontiguousarray(sr[sl])
        m["c_mk"] = np.ascontiguousarray(cmk[sl].reshape(16, 256, 512))
        m["c_mv"] = np.ascontiguousarray(cmv[sl].reshape(16, 256, 512))
        in_maps.append(m)
    res = run_bass_kernel_spmd(nc, in_maps, core_ids=list(range(NCORES)))
    R = res.results
    cat = lambda k: np.stack([np.asarray(r[k], dtype=np.float32) for r in R], 0)
    y_prompt = cat("y_p").reshape(8, 4096, 1024)
    y_sample = cat("y_s").reshape(128, 8, 1024)
    swk_p = cat("o_swk_p").reshape(1, 8, 128, 2, 64)
    swv_p = cat("o_swv_p").reshape(1, 8, 128, 2, 64)
    ret_p = cat("o_ret_p").reshape(1, 8, 8, 64, 64)
    mk_p = cat("o_mk_p").reshape(1, 8, 256, 4, 128)
    mv_p = cat("o_mv_p").reshape(1, 8, 256, 4, 128)
    swk_s = cat("o_swk_s").reshape(1, 128, 128, 2, 64)
    swv_s = cat("o_swv_s").reshape(1, 128, 128, 2, 64)
    ret_s = cat("o_ret_s").reshape(1, 128, 8, 64, 64)
    return (y_prompt, y_sample, swk_p, swv_p, ret_p, mk_p, mv_p, swk_s, swv_s, ret_s)
```

```python
import contextlib
import numpy as np
import ml_dtypes
import concourse.bass as bass
import concourse.mybir as mybir
from concourse.bass_utils import run_bass_kernel_spmd

F32 = mybir.dt.float32
BF16 = mybir.dt.bfloat16
ALU = mybir.AluOpType
AF = mybir.ActivationFunctionType
AX = mybir.AxisListType

NCORES = 8
NT = 33
GRP = 2
GROUPS = [list(range(g, min(g + GRP, NT - 1))) for g in range(0, NT - 1, GRP)] + [[NT - 1]]
NG = len(GROUPS)
NJ = 22
EPS = 1e-6


class Op:
    __slots__ = ("idx", "stream", "fn", "deps", "chan", "is_dma", "sig", "sigval", "chanval", "cost", "nbytes")


class Prog:
    STREAMS = ("pe", "act", "dve", "pool", "sp")

    def __init__(self):
        self.ops = []
        self.tok = {}
        self.chan_n = {}
        self.chan_last = {}
        self.out_dmas = []
        self.cost = {"dve": 0.0, "pool": 0.0}

    DEFCOST = {"pe": 0.5, "act": 0.6, "dve": 0.5, "pool": 0.7, "sp": 0.1}

    def add(self, stream, fn, reads=(), writes=(), chan=None, is_out=False, cost=None, nbytes=0):
        op = Op()
        op.cost = self.DEFCOST[stream] if cost is None else cost
        op.nbytes = nbytes
        op.idx = len(self.ops)
        op.stream = stream
        op.fn = fn
        op.chan = chan
        op.is_dma = chan is not None
        op.sig = False
        op.sigval = 0
        op.chanval = 0
        deps = {}
        for t in reads:
            st = self.tok.setdefault(t, [None, []])
            if st[0] is not None:
                deps[st[0]] = "raw"
            if isinstance(t, tuple) and t[0] == "bk":
                for r in st[1]:
                    if self.ops[r].stream != stream:
                        deps.setdefault(r, "rr")
        for t in writes:
            st = self.tok.setdefault(t, [None, []])
            if st[0] is not None:
                deps.setdefault(st[0], "waw")
            for r in st[1]:
                deps.setdefault(r, "war")
        for t in reads:
            self.tok[t][1].append(op.idx)
        for t in writes:
            self.tok[t] = [op.idx, []]
        deps.pop(op.idx, None)
        op.deps = deps
        if op.is_dma:
            prev = self.chan_last.get(chan)
            if prev is not None:
                deps.setdefault(prev, "chan")
            self.chan_last[chan] = op.idx
            self.chan_n[chan] = self.chan_n.get(chan, 0) + 1
            op.chanval = 16 * self.chan_n[chan]
            if is_out:
                self.out_dmas.append(op.idx)
        self.ops.append(op)
        return op

    def ew(self, fn, reads=(), writes=(), cost=1.0, only=None):
        if only is None:
            eng = "dve" if self.cost["dve"] <= self.cost["pool"] + 2.0 * cost else "pool"
        else:
            eng = only
        self.cost[eng] += cost * (2.0 if eng == "pool" else 1.0)
        return self.add(eng, fn, reads, writes, cost=(0.15 + 1.0 * cost) if eng == "pool" else (0.1 + 0.6 * cost))

    do_schedule = True
    LAT = 0.45
    DMA_FIXED = 2.0
    DMA_BW = 300e3

    def schedule(self):
        import heapq
        ops = self.ops
        n = len(ops)
        succ = [[] for _ in range(n)]
        indeg = [0] * n
        for o in ops:
            for d in o.deps:
                succ[d].append(o.idx)
                indeg[o.idx] += 1
        dur = [(o.cost if not o.is_dma else self.DMA_FIXED + o.nbytes / self.DMA_BW) for o in ops]
        prio = [0.0] * n
        for i in range(n - 1, -1, -1):
            m = 0.0
            for s_ in succ[i]:
                if prio[s_] > m:
                    m = prio[s_]
            prio[i] = dur[i] + m
        eng_free = {s_: 0.0 for s_ in self.STREAMS}
        dma_free = 0.0
        finish = [0.0] * n
        ready_t = [0.0] * n
        avail = [i for i in range(n) if indeg[i] == 0]
        order = []
        while avail:
            best, bkey = None, None
            for i in avail:
                o = ops[i]
                st_ = max(eng_free[o.stream], ready_t[i])
                key = (int(st_ / 0.35), -prio[i], i)
                if bkey is None or key < bkey:
                    best, bkey = i, key
            avail.remove(best)
            o = ops[best]
            st_ = max(eng_free[o.stream], ready_t[best])
            if o.is_dma:
                issue = 1.0 if o.stream == "pool" else 0.08
                eng_free[o.stream] = st_ + issue
                t0 = max(st_ + issue, dma_free)
                dma_free = t0 + o.nbytes / self.DMA_BW
                finish[best] = dma_free + self.DMA_FIXED
            else:
                eng_free[o.stream] = st_ + o.cost
                finish[best] = st_ + o.cost
            order.append(best)
            for s_ in succ[best]:
                r = finish[best] + self.LAT
                if r > ready_t[s_]:
                    ready_t[s_] = r
                indeg[s_] -= 1
                if indeg[s_] == 0:
                    avail.append(s_)
        assert len(order) == n
        self.model_time = max(finish)
        return order

    def _needs_wait(self, c, p, kind):
        if p.is_dma:
            return True
        if c.stream == p.stream:
            if c.is_dma:
                return True
            if c.stream == "pe":
                return False
            return True
        return True

    def emit(self, nc, stack):
        ops = self.ops
        for c in ops:
            for d, kind in c.deps.items():
                if self._needs_wait(c, ops[d], kind):
                    ops[d].sig = True
        cnt = {s: 0 for s in self.STREAMS}
        order = self.schedule() if self.do_schedule else list(range(len(ops)))
        self._order = order
        for o in (ops[i] for i in order):
            if not o.is_dma and o.sig:
                cnt[o.stream] += 1
                o.sigval = cnt[o.stream]
        esem = {s: stack.enter_context(nc.semaphore("s_" + s)) for s in self.STREAMS if s != "sp"}
        csem = {c: stack.enter_context(nc.semaphore("c_" + c)) for c in self.chan_n}
        block = stack.enter_context(nc.Block())
        streams = {s: [ops[i] for i in order if ops[i].stream == s] for s in self.STREAMS}

        def run_stream(sname, eng):
            waited = {}
            for o in streams[sname]:
                for d, kind in o.deps.items():
                    p = ops[d]
                    if not self._needs_wait(o, p, kind):
                        continue
                    if p.is_dma:
                        sem, val, key = csem[p.chan], p.chanval, "c_" + p.chan
                    else:
                        sem, val, key = esem[p.stream], p.sigval, "e_" + p.stream
                    if waited.get(key, 0) >= val:
                        continue
                    waited[key] = val
                    eng.wait_ge(sem, val)
                ins = o.fn(eng)
                if o.is_dma:
                    ins.then_inc(csem[o.chan], 16)
                elif o.sig:
                    ins.then_inc(esem[o.stream], 1)
            if sname == "sp":
                done = set()
                for d in self.out_dmas:
                    p = ops[d]
                    if p.chan in done:
                        continue
                    done.add(p.chan)
                    eng.wait_ge(csem[p.chan], 16 * self.chan_n[p.chan])

        @block.tensor
        def _(e):
            run_stream("pe", e)

        @block.scalar
        def _(e):
            run_stream("act", e)

        @block.vector
        def _(e):
            run_stream("dve", e)

        @block.gpsimd
        def _(e):
            run_stream("pool", e)

        @block.sync
        def _(e):
            run_stream("sp", e)


def host_consts():
    bf = ml_dtypes.bfloat16
    k = np.arange(128)[:, None]
    q = np.arange(128)[None, :]
    masks = np.zeros((128, 4, 128), np.float32)
    masks[:, 0, :] = (k <= q)
    masks[:, 1, :] = (k > q)
    masks[:, 2, :] = (k // 8 == q // 8) & (k % 8 <= q % 8)
    masks[:, 3, :] = (k > q % 8)
    colmask = np.zeros((128, 16, 128), np.float32)
    for b in range(16):
        colmask[:, b, :] = (q // 8 == b)
    rowmask = np.zeros((128, 16), np.float32)
    for b in range(16):
        rowmask[:, b] = (np.arange(128) // 8 == b)
    f32 = np.float32
    tabs = np.zeros((NT, 128, 4, 64), np.float32)
    inv_a = (f32(1.0) / (f32(10000.0) ** (np.arange(32, dtype=f32) / f32(32)))).astype(f32)
    inv_r = (f32(10000.0) ** (-np.linspace(0.0, 1.0, 32, dtype=f32))).astype(f32)
    p = np.arange(128)
    for t in range(NT):
        pos = (t * 128 + p) if t < 32 else (16384 + p % 8)
        pos = pos.astype(f32)
        ang = (pos[:, None] * inv_a[None, :]).astype(f32).astype(np.float64)
        c, s = np.cos(ang), np.sin(ang)
        tabs[t, :, 0, :] = np.concatenate([c, c], -1)
        tabs[t, :, 1, :] = np.concatenate([-s, s], -1)
        ang = (pos[:, None] * inv_r[None, :]).astype(f32).astype(np.float64)
        c, s = np.cos(ang), np.sin(ang)
        tabs[t, :, 2, :] = np.stack([c, c], -1).reshape(128, 64)
        tabs[t, :, 3, :] = np.stack([-s, s], -1).reshape(128, 64)
    log_g = np.log(1.0 - np.exp2(-5.0 - np.arange(8, dtype=np.float64)))
    dec = np.zeros((128, 2, 2, 8), np.float32)
    for v, pp in enumerate([p, p % 8]):
        dec[:, v, 0, :] = np.exp((pp[:, None] + 1.0) * log_g[None, :])
        dec[:, v, 1, :] = np.exp(-(pp[:, None] + 1.0) * log_g[None, :]) / 8.0
    cdec = np.zeros((128, 2, 4), np.float32)
    for c_ in range(4):
        h = 2 * c_ + p // 64
        cdec[:, 0, c_] = np.exp(128.0 * log_g[h])
        cdec[:, 1, c_] = np.exp(8.0 * log_g[h])
    return {
        "c_ident": np.eye(128).astype(bf),
        "c_masks": masks.astype(bf),
        "c_colmask": colmask.astype(bf),
        "c_rowmask": rowmask,
        "c_tabs": tabs,
        "c_dec": dec,
        "c_cdec": cdec,
    }


class _Stop(Exception):
    pass


def build_program(stage=99, skip=()):
    nc = bass.Bass("TRN2", target_bir_lowering=False)
    P = Prog()

    def ckpt(k):
        if stage <= k:
            raise _Stop()
    din = lambda n, s, d=F32: nc.dram_tensor(n, list(s), d, kind="ExternalInput").ap()
    dout = lambda n, s: nc.dram_tensor(n, list(s), F32, kind="ExternalOutput").ap()
    x_p = din("x_p", [4096, 1024]); x_s = din("x_s", [128, 1024]); mem_p = din("mem_p", [256, 1024])
    c_k = din("c_k", [16, 128, 128]); c_v = din("c_v", [16, 128, 128]); s_ret = din("s_ret", [16, 8, 64, 64])
    c_mk = din("c_mk", [16, 256, 512]); c_mv = din("c_mv", [16, 256, 512])
    w_in = din("w_in", [1024, 2816]); w_out = din("w_out", [1024, 1024]); w_mq = din("w_mq", [1024, 512])
    w_mkv = din("w_mkv", [1024, 1024]); w_mo = din("w_mo", [512, 1024]); w_gu = din("w_gu", [1024, 5632])
    w_down = din("w_down", [2816, 1024])
    norm_mix = din("norm_mix", [1, 1024]); norm_cross = din("norm_cross", [1, 1024])
    norm_mem = din("norm_mem", [1, 1024]); norm_ffn = din("norm_ffn", [1, 1024])
    q_norm_a = din("q_norm_a", [1, 64]); k_norm_a = din("k_norm_a", [1, 64]); sinks = din("sinks", [1, 8])
    q_norm_m = din("q_norm_m", [1, 128]); k_norm_m = din("k_norm_m", [1, 128])
    c_ident = din("c_ident", [128, 128], BF16); c_masks = din("c_masks", [128, 4, 128], BF16)
    c_colmask = din("c_colmask", [128, 16, 128], BF16); c_rowmask = din("c_rowmask", [128, 16])
    c_tabs = din("c_tabs", [NT, 128, 4, 64]); c_dec = din("c_dec", [128, 2, 2, 8]); c_cdec = din("c_cdec", [128, 2, 4])
    y_p = dout("y_p", [4096, 1024]); y_s = dout("y_s", [128, 1024])
    o_swk_p = dout("o_swk_p", [128, 128]); o_swv_p = dout("o_swv_p", [128, 128]); o_ret_p = dout("o_ret_p", [8, 64, 64])
    o_mk_p = dout("o_mk_p", [256, 512]); o_mv_p = dout("o_mv_p", [256, 512])
    o_swk_s = dout("o_swk_s", [16, 128, 128]); o_swv_s = dout("o_swv_s", [16, 128, 128]); o_ret_s = dout("o_ret_s", [16, 8, 64, 64])
    wgu_d = nc.dram_tensor("wgu_d", [NJ, 128, 2048], BF16, kind="Internal").ap()
    wd_d = nc.dram_tensor("wd_d", [NJ, 128, 1024], BF16, kind="Internal").ap()

    def bc_row(ap, n):
        return bass.AP(ap.tensor, 0, [[0, 128], [1, n]])

    with contextlib.ExitStack() as st:
        st.enter_context(nc.allow_non_contiguous_dma(reason="small strided constant / cache-row loads"))
        sb = lambda n, s, d=F32: st.enter_context(nc.sbuf_tensor(n, list(s), d))
        W_in = sb("W_in", [128, 8, 2816], BF16); W_out = sb("W_out", [128, 8, 1024], BF16)
        W_mq = sb("W_mq", [128, 8, 512], BF16); W_mo = sb("W_mo", [128, 4, 1024], BF16)
        ident = sb("ident", [128, 128], BF16); masks = sb("masks", [128, 4, 128], BF16)
        ident32 = sb("ident32", [128, 128], F32)
        colmask = sb("colmask", [128, 16, 128], BF16); rowmask = sb("rowmask", [128, 16])
        dec = sb("dec", [128, 2, 2, 8]); cdec = sb("cdec", [128, 2, 4]); nh = sb("nh", [128, 8])
        gcols = sb("gcols", [128, 4, 8])
        gqa = sb("gqa", [128, 2, 64]); gka = sb("gka", [128, 2, 64])
        gqm = sb("gqm", [128, 128]); gkm = sb("gkm", [128, 128]); esink = sb("esink", [128, 8])
        xres = [sb("xres%d" % i, [128, 1024]) for i in range(2 * GRP)]

        class _Set:
            def __init__(self, n):
                self.n = n
                self.xn = sb("xn%d" % n, [128, 1024], BF16)
                self.xnT = sb("xnT%d" % n, [128, 8, 128], BF16)
                self.ss = sb("ss%d" % n, [128, 1]); self.rs = sb("rs%d" % n, [128, 1])
                self.tmpA = sb("tmpA%d" % n, [128, 512]); self.tmpB = sb("tmpB%d" % n, [128, 512]); self.tmpC = sb("tmpC%d" % n, [128, 512])
                self.st8 = sb("st8%d" % n, [128, 8]); self.rh8 = sb("rh8%d" % n, [128, 8]); self.den8 = sb("den8%d" % n, [128, 8])
                self.XN = [self.T("xnT", 0), self.T("xnT", 1)]

            def T(self, *tok):
                return ("S%d" % self.n,) + tok
        SETS = [_Set(0), _Set(1)]
        proj = sb("proj", [128, 2816])
        tab = [sb("tab%d" % i, [128, 4, 64]) for i in range(2)]
        tg = sb("tg", [128, 4, 64])
        qa_bf = sb("qa_bf", [128, 512], BF16); ka_f = sb("ka_f", [128, 128]); ka_bf = sb("ka_bf", [128, 128], BF16)
        qaT = sb("qaT", [128, 512], BF16)
        kaT = [sb("kaT%d" % i, [128, 128], BF16) for i in range(2)]
        Vext = [sb("Vext%d" % i, [128, 2, 65], BF16) for i in range(2)]
        qr_bf = sb("qr_bf", [128, 512], BF16); kr_bf = sb("kr_bf", [128, 512], BF16); vr_bf = sb("vr_bf", [128, 512], BF16)
        qkT = sb("qkT", [128, 8, 128], BF16)
        innT = sb("innT", [128, 8, 128], BF16)
        S_f = sb("S_f", [128, 4, 64]); S_bd = [sb("S_bd%d" % i, [128, 4, 128], BF16) for i in range(2)]
        kmT = sb("kmT", [128, 4, 256], BF16); Vm = sb("Vm", [128, 2, 4, 129], BF16)
        xn3T = sb("xn3T", [128, 8, GRP * 128], BF16)
        wgu = [sb("wgu%d" % i, [128, 8, 256], BF16) for i in range(3)]
        wdb = [sb("wdb%d" % i, [128, 1024], BF16) for i in range(2)]
        sgb = [sb("sgb%d" % i, [128, GRP * 128]) for i in range(2)]
        hT = [sb("hT%d" % i, [128, GRP * 128], BF16) for i in range(2)]
        stg = [sb("stg%d" % i, [128, 1024]) for i in range(2)]
        stg_bf = sb("stg_bf", [128, 1024], BF16)
        wdb.append(stg_bf)
        WDT = [("wdb", 0), ("wdb", 1), "stg_bf"]
        WGV = [w_[:] for w_ in wgu] + [stg[i][:].bitcast(BF16).rearrange("p (k c) -> p k c", k=8) for i in range(2)]
        WGT = [("wgu", 0), ("wgu", 1), ("wgu", 2), ("stg", 0), ("stg", 1)]
        NWG = 5
        kcT = sb("kcT", [128, 16, 128], BF16); Vc = sb("Vc", [128, 16, 2, 65], BF16)
        kmTb = sb("kmTb", [128, 1024], BF16); Vmb = sb("Vmb", [128, 2, 4, 129], BF16)
        L0 = SETS[0]
        L0.pT = [sb("pT%d" % i, [128, 512], BF16)[:] for i in range(2)]
        L0.mix_bf = sb("mix_bf", [128, 1024], BF16)[:]
        L0.qm_bf = sb("qm_bf", [128, 512], BF16)[:]; L0.qmT = sb("qmT", [128, 512], BF16)[:]
        L0.pmT = sb("pmT", [128, 1024], BF16)[:]; L0.om_bf = sb("om_bf", [128, 512], BF16)[:]; L0.omT = sb("omT", [128, 512], BF16)[:]
        L1 = SETS[1]
        kc_flat = kcT[:].rearrange("p b r -> p (b r)")
        vc_flat = Vc[:].rearrange("p b h d -> p (b h d)")
        L1.mix_bf = kc_flat[:, 0:1024]; L1.pmT = kc_flat[:, 1024:2048]
        L1.pT = [vc_flat[:, 0:512], vc_flat[:, 512:1024]]
        L1.qm_bf = vc_flat[:, 1024:1536]; L1.qmT = vc_flat[:, 1536:2048]
        L1.om_bf = kmTb[:, 0:512]; L1.omT = kmTb[:, 512:1024]
        LATE_TOKS = []
        ps = st.enter_context(nc.psum_tensor("ps", [128, 4096], F32))

        def bank(b, n=512):
            return ps[:, b * 512:b * 512 + n]

        def bankbf(b, n=1024):
            return ps[:, b * 512:(b + 1) * 512].bitcast(BF16)[:, 0:n]

        BK = lambda b: ("bk", b)

        class _Banks:
            def __init__(self, banks):
                self.free = list(banks)

            def get(self):
                assert self.free, "PSUM banks exhausted in program order"
                return self.free.pop(0)

            def put(self, *bs):
                for b in bs:
                    assert b not in self.free
                    self.free.append(b)
        BAS = [_Banks(range(0, 4)), _Banks(range(4, 8))]

        class _BAProxy:
            cur = 0

            def get(self):
                return BAS[self.cur].get()

            def put(self, *bs):
                for b in bs:
                    BAS[0 if b < 4 else 1].put(b)
        BA = _BAProxy()
        cnt = {"dma": 0}

        def dma(stream, out, in_, reads, writes, chan, is_out=False):
            if "wcast" in skip and stream == "pool":
                return None
            nb = out.size() * (2 if out.dtype == BF16 else 4)
            return P.add(stream, lambda e: e.dma_start(out=out, in_=in_), reads, writes, chan=chan, is_out=is_out, nbytes=nb)

        dma("sp", ident[:], c_ident, [], ["ident"], "k0")
        dma("sp", masks[:], c_masks, [], ["masks"], "k1")
        P.add("act", lambda e: e.activation(out=ident32[:], in_=ident[:], func=AF.Copy), ["ident"], ["ident32"])
        dma("sp", colmask[:], c_colmask, [], ["colmask"], "k2")
        dma("sp", rowmask[:], c_rowmask, [], ["rowmask"], "k3")
        dma("sp", dec[:], c_dec, [], ["dec"], "k4")
        dma("sp", cdec[:], c_cdec, [], ["cdec"], "k5")
        for i, nrm in enumerate([] if "gcols" in skip else [norm_mix, norm_cross, norm_ffn, norm_mem]):
            dma("sp", gcols[:, i, :], nrm[0].rearrange("(k p) -> p k", p=128), [], [("gcols", i)], "k6")
        for t_, src in (() if "bcast" in skip else ((gqa, q_norm_a), (gka, k_norm_a))):
            nm = "gqa" if t_ is gqa else "gka"
            dma("sp", t_[:, 0, :], bc_row(src, 64), [], [nm + "0"], "k7")
            dma("sp", t_[:, 1, 0:32], bass.AP(src.tensor, 32, [[0, 128], [1, 32]]), [], [nm + "1"], "k8")
            dma("sp", t_[:, 1, 32:64], bass.AP(src.tensor, 0, [[0, 128], [1, 32]]), [], [nm + "2"], "k9")
        GQA = ["gqa0", "gqa1", "gqa2"]; GKA = ["gka0", "gka1", "gka2"]
        if "bcast2" not in skip:
            dma("sp", gqm[:], bc_row(q_norm_m, 128), [], ["gqm"], "k10")
            dma("sp", gkm[:], bc_row(k_norm_m, 128), [], ["gkm"], "k11")
            dma("sp", esink[:], bc_row(sinks, 8), [], ["esink"], "k12")
            P.add("act", lambda e: e.activation(out=esink[:], in_=esink[:], func=AF.Exp), ["esink"], ["esink"])
        P.add("pool", lambda e: e.memset(nh[:], -0.5), [], ["nh"])
        P.add("pool", lambda e: e.memset(S_f[:], 0.0), [], ["S_f"])
        for i in range(2):
            P.add("pool", lambda e, i=i: e.memset(S_bd[i][:], 0.0), [], [("S_bd", i)])
            P.add("pool", lambda e, i=i: e.memset(Vext[i][:], 1.0), [], [("Vext", i)])
        P.add("pool", lambda e: e.memset(Vm[:], 1.0), [], ["Vm"])
        P.add("pool", lambda e: e.memset(Vmb[:], 1.0), [], ["Vmb"])
        WIN = []
        for kvh in range(2):
            for g_ in range(4):
                hh = kvh * 4 + g_
                src = w_in[:, hh * 64:(hh + 1) * 64].rearrange("(k p) d -> p k d", p=128)
                dst = W_in[:, :, g_ * 128 + kvh * 64: g_ * 128 + (kvh + 1) * 64]
                dma("pool", dst, src, [], [("W_in", kvh)], "w%d" % kvh)
            WIN.append(("W_in", kvh))
        for i in range(4):
            src = w_in[:, 768 + i * 512:768 + (i + 1) * 512].rearrange("(k p) n -> p k n", p=128)
            dma("pool", W_in[:, :, 512 + i * 512:1024 + i * 512], src, [], [("W_in", 2 + i)], "w%d" % (2 + i)); WIN.append(("W_in", 2 + i))
        dma("pool", W_in[:, :, 2560:2816], w_in[:, 512:768].rearrange("(k p) n -> p k n", p=128), [], [("W_in", 6)], "w6"); WIN.append(("W_in", 6))
        dma("pool", W_out[:], w_out.rearrange("(k p) n -> p k n", p=128), [], ["W_out"], "w7")
        dma("pool", W_mq[:], w_mq.rearrange("(k p) n -> p k n", p=128), [], ["W_mq"], "w8")
        dma("pool", W_mo[:], w_mo.rearrange("(k p) n -> p k n", p=128), [], ["W_mo"], "w9")

        def rstd_chain(S, src_tok_list, st_ap, rh_ap, n, scale, rd, wr):
            P.add("pool", lambda e: e.tensor_scalar(out=rh_ap, in0=st_ap, scalar1=scale, scalar2=EPS, op0=ALU.mult, op1=ALU.add), rd, wr, cost=0.25)
            P.add("pool", lambda e: e.tensor_tensor(out=rh_ap, in0=rh_ap, in1=nh[:, 0:n], op=ALU.pow), wr + ["nh"], wr, cost=0.3)

        def norm_T(S, src, src_tok, gi, dstT, dst_tok, defer=False):
            P.add("act", lambda e: e.activation(out=S.xn[:], in_=src, func=AF.Square, accum_out=S.ss[:]), [src_tok], [S.T("xn"), S.T("ss")], cost=1.1)
            rstd_chain(S, None, S.ss[:], S.rs[:], 1, 1.0 / 1024, [S.T("ss")], [S.T("rs")])
            if defer:
                P.add("act", lambda e: e.activation(out=S.xn[:], in_=src, func=AF.Copy), [src_tok], [S.T("xn")], cost=1.1)
            else:
                P.add("act", lambda e: e.activation(out=S.xn[:], in_=src, func=AF.Copy, scale=S.rs[:]), [src_tok, S.T("rs")], [S.T("xn")], cost=1.1)

            ba, bd = BA.get(), BA.get()

            def tr(e):
                for k in range(8):
                    bb = ba if k < 4 else bd
                    ins = e.transpose(out=bankbf(bb)[:, (k % 4) * 128:(k % 4 + 1) * 128], in_=S.xn[:, k * 128:(k + 1) * 128], identity=ident[:])
                return ins
            P.add("pe", tr, [S.T("xn"), "ident"], [BK(ba), BK(bd)], cost=0.6)

            def ev_a(e):
                for k in range(0, 4):
                    ins = e.activation(out=dstT[:, k, :], in_=bankbf(ba)[:, k * 128:(k + 1) * 128], func=AF.Copy, scale=gcols[:, gi, k:k + 1])
                return ins

            def ev_d(e):
                for k in range(4, 8):
                    ins = e.tensor_scalar(out=dstT[:, k, :], in0=bankbf(bd)[:, (k - 4) * 128:(k - 3) * 128], scalar1=gcols[:, gi, k:k + 1], scalar2=None, op0=ALU.mult)
                return ins
            P.add("act", ev_a, [BK(ba), ("gcols", gi)], [dst_tok + (0,)], cost=0.9)
            P.add("dve", ev_d, [BK(bd), ("gcols", gi)], [dst_tok + (1,)], cost=0.7)
            BA.put(ba, bd)

        def head_rstd(S, src3, H, D, rd, scratch, scratch_tok):
            sq = scratch[:, 0:H * D].rearrange("p (h d) -> p h d", d=D)
            P.ew(lambda e: e.tensor_tensor(out=sq, in0=src3, in1=src3, op=ALU.mult), rd, [scratch_tok], cost=H * D / 512.0)
            P.add("dve", lambda e: e.tensor_reduce(out=S.st8[:, 0:H], in_=sq, axis=AX.X, op=ALU.add), [scratch_tok], [S.T("st8")])
            rstd_chain(S, None, S.st8[:, 0:H], S.rh8[:, 0:H], H, 1.0 / D, [S.T("st8")], [S.T("rh8")])

        def bc_h(ap2, H):
            return ap2.unsqueeze(1).to_broadcast([128, H, ap2.shape[1]])

        def bc_d(ap2, D):
            return ap2.unsqueeze(2).to_broadcast([128, ap2.shape[1], D])

        def rot_half(S, src3, H, CT, ST, rd):
            A = S.tmpA[:, 0:H * 64].rearrange("p (h d) -> p h d", d=64)
            B = S.tmpB[:, 0:H * 64].rearrange("p (h d) -> p h d", d=64)
            c = H * 64 / 512.0
            P.ew(lambda e: e.tensor_tensor(out=A, in0=src3, in1=bc_h(CT, H), op=ALU.mult), rd, [S.T("tmpA")], cost=c)
            P.ew(lambda e: e.tensor_tensor(out=B[:, :, 0:32], in0=src3[:, :, 32:64], in1=bc_h(ST[:, 0:32], H), op=ALU.mult), rd, [S.T("tmpB", 0)], cost=c / 2)
            P.ew(lambda e: e.tensor_tensor(out=B[:, :, 32:64], in0=src3[:, :, 0:32], in1=bc_h(ST[:, 32:64], H), op=ALU.mult), rd, [S.T("tmpB", 1)], cost=c / 2)
            P.ew(lambda e: e.tensor_tensor(out=A, in0=A, in1=B, op=ALU.add), [S.T("tmpA"), S.T("tmpB", 0), S.T("tmpB", 1)], [S.T("tmpA")], cost=c)
            return A

        def rot_pair(S, src3, CF, SF, rd):
            A = S.tmpA[:].rearrange("p (h d) -> p h d", d=64)
            B4 = S.tmpB[:].rearrange("p (h i two) -> p h i two", h=8, two=2)
            s4 = src3.rearrange("p h (i two) -> p h i two", two=2)
            SF3 = SF.rearrange("p (i two) -> p i two", two=2)
            P.ew(lambda e: e.tensor_tensor(out=A, in0=src3, in1=bc_h(CF, 8), op=ALU.mult), rd, [S.T("tmpA")], cost=1)
            P.ew(lambda e: e.tensor_tensor(out=B4[:, :, :, 0], in0=s4[:, :, :, 1], in1=bc_h(SF3[:, :, 0], 8), op=ALU.mult), rd, [S.T("tmpB", 0)], cost=0.5)
            P.ew(lambda e: e.tensor_tensor(out=B4[:, :, :, 1], in0=s4[:, :, :, 0], in1=bc_h(SF3[:, :, 1], 8), op=ALU.mult), rd, [S.T("tmpB", 1)], cost=0.5)
            P.ew(lambda e: e.tensor_tensor(out=S.tmpA[:], in0=S.tmpA[:], in1=S.tmpB[:], op=ALU.add), [S.T("tmpA"), S.T("tmpB", 0), S.T("tmpB", 1)], [S.T("tmpA")], cost=1)
            return A

        def transposes(pairs, rd, dst, dst_tok, n):
            b0 = BA.get()

            def tr(e):
                for i, src in enumerate(pairs):
                    ins = e.transpose(out=bankbf(b0)[:, i * 128:(i + 1) * 128], in_=src, identity=ident[:])
                return ins
            P.add("pe", tr, list(rd) + ["ident"], [BK(b0)], cost=0.08 * n)
            wr = dst_tok if isinstance(dst_tok, list) else [dst_tok]
            cnt["tr"] = cnt.get("tr", 0) + 1
            if cnt["tr"] % 2:
                P.add("act", lambda e: e.activation(out=dst, in_=bankbf(b0)[:, 0:n * 128], func=AF.Copy), [BK(b0)], wr, cost=0.25 + n * 0.1)
            else:
                P.add("dve", lambda e: e.tensor_copy(out=dst, in_=bankbf(b0)[:, 0:n * 128]), [BK(b0)], wr, cost=0.12 + n * 0.07)
            BA.put(b0)

        def memkv():
            S = SETS[0]
            memT = [xn3T[:, :, i * 128:(i + 1) * 128] for i in range(2)]
            for mt in range(2):
                dma("sp", stg[mt][:], mem_p[mt * 128:(mt + 1) * 128, :], [], [("stg", mt)], "sg%d" % mt)
                norm_T(S, stg[mt][:], ("stg", mt), 3, memT[mt], ("xn3T", mt))
            ckpt(1.1)
            for c in range(4):
                s = c % 3
                dma("pool", wgu[s][:], w_mkv[:, c * 256:(c + 1) * 256].rearrange("(k p) n -> p k n", p=128), [], [("wgu", s)], "pg%d" % s)
                for mt in range(2):
                    bk = BA.get()

                    def mmk(e, s=s, mt=mt, bk=bk):
                        for k in range(8):
                            ins = e.matmul(bank(bk, 256), lhsT=memT[mt][:, k, :], rhs=wgu[s][:, k, :], start=(k == 0), stop=(k == 7))
                        return ins
                    P.add("pe", mmk, [("xn3T", mt, 0), ("xn3T", mt, 1), ("wgu", s)], [BK(bk)])
                    P.add("act", lambda e, mt=mt, c=c, bk=bk: e.activation(out=proj[:, mt * 1024 + c * 256: mt * 1024 + (c + 1) * 256], in_=bank(bk, 256), func=AF.Copy),
                          [BK(bk)], [("proj", mt * 2 + c // 2)])
                    BA.put(bk)
            ckpt(1.2)
            for mt in range(2):
                kv = proj[:, mt * 1024:(mt + 1) * 1024]
                k3 = kv[:, 0:512].rearrange("p (h d) -> p h d", d=128)
                rdk = [("proj", mt * 2)]
                head_rstd(S, k3, 4, 128, rdk, S.tmpA, S.T("tmpA"))
                A3 = S.tmpA[:].rearrange("p (h d) -> p h d", d=128)
                P.ew(lambda e, k3=k3, A3=A3: e.tensor_tensor(out=A3, in0=k3, in1=bc_d(S.rh8[:, 0:4], 128), op=ALU.mult), rdk + [S.T("rh8"), S.T("tmpA")], [S.T("tmpA")])
                C3 = S.tmpC[:].rearrange("p (h d) -> p h d", d=128)
                P.ew(lambda e, A3=A3, C3=C3: e.tensor_tensor(out=C3, in0=A3, in1=bc_h(gkm[:], 4), op=ALU.mult), [S.T("tmpA"), "gkm"], [S.T("tmpC")])
                ckpt(1.3)
                dma("sp", o_mk_p[mt * 128:(mt + 1) * 128, :], S.tmpC[:], [S.T("tmpC")], [], "omk", is_out=True)
                dma("sp", o_mv_p[mt * 128:(mt + 1) * 128, :], kv[:, 512:1024], [("proj", mt * 2 + 1)], [], "omv", is_out=True)
                P.ew(lambda e: e.tensor_copy(out=stg_bf[:, 0:512], in_=S.tmpC[:]), [S.T("tmpC")], ["stg_bf"])
                P.ew(lambda e, mt=mt, kv=kv: e.tensor_copy(out=Vm[:, mt, :, 0:128], in_=kv[:, 512:1024].rearrange("p (h d) -> p h d", d=128)),
                     [("proj", mt * 2 + 1), "Vm"], ["Vm"])

                ckpt(1.4)

                b0 = BA.get()

                def trk(e, b0=b0):
                    for h in range(4):
                        ins = e.transpose(out=bankbf(b0)[:, h * 128:(h + 1) * 128], in_=stg_bf[:, h * 128:(h + 1) * 128], identity=ident[:])
                    return ins
                P.add("pe", trk, ["stg_bf", "ident"], [BK(b0)])
                P.add("act", lambda e, mt=mt, b0=b0: e.activation(out=kmT[:, :, mt * 128:(mt + 1) * 128], in_=bankbf(b0)[:, 0:512].rearrange("p (h m) -> p h m", h=4), func=AF.Copy),
                      [BK(b0), "kmT"], ["kmT"])
                BA.put(b0)

        if stage > 1:
            try:
                memkv()
            except _Stop:
                pass
        if stage >= 3:
            for j in range(NJ):
                s = j % 3
                for two in range(2):
                    src = w_gu[:, two * 2816 + j * 128: two * 2816 + (j + 1) * 128].rearrange("(k p) n -> p k n", p=128)
                    dma("pool", wgu[s][:, :, two * 128:(two + 1) * 128], src, [], [("wgu", s)], "pg%d" % s)
                dma("sp", wgu_d[j], wgu[s][:].rearrange("p k c -> p (k c)"), [("wgu", s)], [("wgud", j)], "cs%d" % s)
                s_d = j % 2
                dma("pool", wdb[s_d][:], w_down[j * 128:(j + 1) * 128, :], [], [WDT[s_d]], "pd%d" % s_d)
                dma("sp", wd_d[j], wdb[s_d][:], [WDT[s_d]], [("wdd", j)], "ct%d" % s_d)

        def ffn_load(j):
            s = j % NWG
            dma("sp", WGV[s].rearrange("p k c -> p (k c)"), wgu_d[j], [("wgud", j)], [WGT[s]], "fg%d" % s)

        def ffn_load_d(j):
            s = j % 3
            dma("sp", wdb[s][:], wd_d[j], [("wdd", j)], [WDT[s]], "fd%d" % s)

        mult, add = ALU.mult, ALU.add

        def tt(out, in0, in1, op):
            return lambda e: e.tensor_tensor(out=out, in0=in0, in1=in1, op=op)

        def cpy(out, in_):
            return lambda e: e.tensor_copy(out=out, in_=in_)

        def v3(ap, d):
            return ap.rearrange("p (h d) -> p h d", d=d)

        def sample_preload():
            S1_ = SETS[1]
            fence_reads = [S1_.T("mix", 0), S1_.T("mix", 1), S1_.T("mix", 2), S1_.T("pmT", 0), S1_.T("pmT", 1), S1_.T("pT", 0), S1_.T("pT", 1),
                           S1_.T("qm_bf"), S1_.T("qmT"), S1_.T("om_bf", 0), S1_.T("om_bf", 1), S1_.T("omT")]
            P.add("pool", lambda e: e.memset(Vc[:], 1.0), fence_reads, ["Vc", "kcT", "kmTb"])
            for hf in range(2):
                b0 = hf * 8
                dma("sp", v3(stg[0][:], 128), c_k[b0:b0 + 8].rearrange("b r f -> r b f"), [], [("stg", 0)], "sg0")
                dma("sp", v3(stg[1][:], 128), c_v[b0:b0 + 8].rearrange("b r f -> r b f"), [], [("stg", 1)], "sg1")
                P.ew(cpy(stg_bf[:], stg[0][:]), [("stg", 0)], ["stg_bf"], cost=2)
                P.ew(cpy(Vc[:, b0:b0 + 8, :, 0:64], stg[1][:].rearrange("p (b h d) -> p b h d", b=8, h=2)), [("stg", 1), "Vc"], ["Vc"], cost=2)

                bq = BA.get()

                def trc(e, bq=bq):
                    for b in range(8):
                        ins = e.transpose(out=bankbf(bq)[:, b * 128:(b + 1) * 128], in_=stg_bf[:, b * 128:(b + 1) * 128], identity=ident[:])
                    return ins
                P.add("pe", trc, ["stg_bf", "ident"], [BK(bq)])
                P.add("act", lambda e, b0=b0, bq=bq: e.activation(out=kcT[:, b0:b0 + 8, :].rearrange("p b r -> p (b r)"), in_=bankbf(bq), func=AF.Copy), [BK(bq), "kcT"], ["kcT"])
                BA.put(bq)
            dma("sp", o_swk_s[:, 0:120, :], c_k[:, 8:128, :], [], [], "osk", is_out=True)
            dma("sp", o_swv_s[:, 0:120, :], c_v[:, 8:128, :], [], [], "osv", is_out=True)

        def tile_mixer(T, i, xi):
            samp = (T == NT - 1)
            S = SETS[i % 2]
            BA.cur = i % 2
            X = xres[xi]; XT = ("xres", xi)
            sl = T % 2
            tb = tab[T % 2]; TB = ("tab", T % 2)
            dv = 1 if samp else 0
            dma("sp", tb[:], c_tabs[T], [], [TB], "tb%d" % (T % 2))
            norm_T(S, X[:], XT, 0, S.xnT, S.T("xnT"), defer=True)
            groups = [(0, 512), (512, 1024), (1024, 1536), (1536, 2048), (2048, 2560), (2560, 2816)]
            if T == 0:
                S1 = SETS[1]
                xn32 = proj[:, 0:1024]
                P.add("act", lambda e: e.activation(out=xn32, in_=X[:], func=AF.Copy, scale=S.rs[:]), [XT, S.T("rs")], [("proj", 0), ("proj", 1)], cost=1.1)
                b32a, b32b = BA.get(), BA.get()

                def tr32(e):
                    for k in range(8):
                        bb = b32a if k < 4 else b32b
                        ins = e.transpose(out=bank(bb)[:, (k % 4) * 128:(k % 4 + 1) * 128], in_=xn32[:, k * 128:(k + 1) * 128], identity=ident32[:])
                    return ins
                P.add("pe", tr32, [("proj", 0), ("proj", 1), "ident32"], [BK(b32a), BK(b32b)], cost=2.0)

                def ev32a(e):
                    for k in range(4):
                        ins = e.activation(out=S1.tmpA[:, k * 128:(k + 1) * 128], in_=bank(b32a)[:, k * 128:(k + 1) * 128], func=AF.Copy, scale=gcols[:, 0, k:k + 1])
                    return ins

                def ev32b(e):
                    for k in range(4, 8):
                        ins = e.tensor_scalar(out=S1.tmpB[:, (k - 4) * 128:(k - 3) * 128], in0=bank(b32b)[:, (k - 4) * 128:(k - 3) * 128], scalar1=gcols[:, 0, k:k + 1], scalar2=None, op0=ALU.mult)
                    return ins
                P.add("act", ev32a, [BK(b32a), ("gcols", 0)], [S1.T("tmpA")], cost=0.9)
                P.add("dve", ev32b, [BK(b32b), ("gcols", 0)], [S1.T("tmpB", 0), S1.T("tmpB", 1)], cost=0.7)
                BA.put(b32a, b32b)
                for c in range(4):
                    c0 = 768 + c * 256
                    for kh in range(2):
                        dma("sp", stg[kh][:].rearrange("p (k n) -> p k n", k=4), w_in[kh * 512:(kh + 1) * 512, c0:c0 + 256].rearrange("(k p) n -> p k n", p=128),
                            [], [("stg", kh)], "sg%d" % kh)
                    bk = BA.get()

                    def mm32(e, bk=bk):
                        for k in range(8):
                            xt_ = S1.tmpA if k < 4 else S1.tmpB
                            ins = e.matmul(bank(bk, 256), lhsT=xt_[:, (k % 4) * 128:(k % 4 + 1) * 128], rhs=stg[k // 4][:, (k % 4) * 256:(k % 4 + 1) * 256],
                                           start=(k == 0), stop=(k == 7))
                        return ins
                    P.add("pe", mm32, [S1.T("tmpA"), S1.T("tmpB", 0), S1.T("tmpB", 1), ("stg", 0), ("stg", 1)], [BK(bk)], cost=3.6)
                    P.add("act", lambda e, c=c, bk=bk: e.activation(out=proj[:, 512 + c * 256:768 + c * 256], in_=bank(bk, 256), func=AF.Copy), [BK(bk)], [("proj", 1 + c // 2)], cost=0.45)
                    BA.put(bk)
            for gi, (c0, c1) in enumerate(groups):
                if T == 0 and gi in (1, 2):
                    continue
                bk = BA.get()

                def mmp(e, c0=c0, c1=c1, bk=bk):
                    for k in range(8):
                        ins = e.matmul(bank(bk, c1 - c0), lhsT=S.xnT[:, k, :], rhs=W_in[:, k, c0:c1], start=(k == 0), stop=(k == 7))
                    return ins
                P.add("pe", mmp, S.XN + WIN, [BK(bk)], cost=0.03 + 8 * (c1 - c0) / 2400.0)
                P.add("act", lambda e, c0=c0, c1=c1, bk=bk: e.activation(out=proj[:, c0:c1], in_=bank(bk, c1 - c0), func=AF.Copy, scale=S.rs[:]), [BK(bk), S.T("rs")], [("proj", gi)], cost=0.75)
                BA.put(bk)
            ckpt(4)
            P.ew(tt(tg[:, 0, :], tb[:, 0, :], gqa[:, 0, :], mult), [TB] + GQA, [("tg", 0)], cost=0.15)
            P.ew(tt(tg[:, 1, :], tb[:, 1, :], gqa[:, 1, :], mult), [TB] + GQA, [("tg", 1)], cost=0.15)
            P.ew(tt(tg[:, 2, :], tb[:, 0, :], gka[:, 0, :], mult), [TB] + GKA, [("tg", 2)], cost=0.15)
            P.ew(tt(tg[:, 3, :], tb[:, 1, :], gka[:, 1, :], mult), [TB] + GKA, [("tg", 3)], cost=0.15)
            q3 = v3(proj[:, 0:512], 64)
            head_rstd(S, q3, 8, 64, [("proj", 0)], S.tmpC, S.T("tmpC"))
            A = rot_half(S, q3, 8, tg[:, 0, :], tg[:, 1, :], [("proj", 0), ("tg", 0), ("tg", 1)])
            P.ew(tt(v3(qa_bf[:], 64), A, bc_d(S.rh8[:, 0:8], 64), mult), [S.T("tmpA"), S.T("rh8")], ["qa_bf"])
            k3 = v3(proj[:, 2560:2688], 64)
            head_rstd(S, k3, 2, 64, [("proj", 5)], S.tmpC, S.T("tmpC"))
            A = rot_half(S, k3, 2, tg[:, 2, :], tg[:, 3, :], [("proj", 5), ("tg", 2), ("tg", 3)])
            P.ew(tt(v3(ka_f[:], 64), A, bc_d(S.rh8[:, 0:2], 64), mult), [S.T("tmpA"), S.T("rh8")], ["ka_f"], cost=0.25)
            P.ew(cpy(ka_bf[:], ka_f[:]), ["ka_f"], ["ka_bf"], cost=0.25)
            P.ew(cpy(Vext[sl][:, :, 0:64], v3(proj[:, 2688:2816], 64)), [("proj", 5), ("Vext", sl)], [("Vext", sl)], cost=0.25)

            bq, bk_ = BA.get(), BA.get()

            def tra(e):
                for c in range(4):
                    ins = e.transpose(out=bankbf(bq)[:, c * 128:(c + 1) * 128], in_=qa_bf[:, c * 128:(c + 1) * 128], identity=ident[:])
                return ins
            P.add("pe", tra, ["qa_bf", "ident"], [BK(bq)], cost=0.35)
            P.add("pe", lambda e: e.transpose(out=bankbf(bk_)[:, 0:128], in_=ka_bf[:], identity=ident[:]), ["ka_bf", "ident"], [BK(bk_)], cost=0.1)
            P.add("act", lambda e: e.activation(out=qaT[:], in_=bankbf(bq)[:, 0:512], func=AF.Copy), [BK(bq)], ["qaT"], cost=0.65)
            P.add("dve", cpy(kaT[sl][:], bankbf(bk_)[:, 0:128]), [BK(bk_)], [("kaT", sl)], cost=0.25)
            BA.put(bq, bk_)
            if T == NT - 2:
                dma("sp", o_swk_p, ka_f[:], ["ka_f"], [], "okp", is_out=True)
                dma("sp", o_swv_p, proj[:, 2688:2816], [("proj", 5)], [], "ovp", is_out=True)
            if samp:
                for b in range(16):
                    dma("sp", o_swk_s[b, 120:128, :], ka_f[b * 8:(b + 1) * 8, :], ["ka_f"], [], "osk%d" % (b % 4), is_out=True)
                    dma("sp", o_swv_s[b, 120:128, :], proj[b * 8:(b + 1) * 8, 2688:2816], [("proj", 5)], [], "osv%d" % (b % 4), is_out=True)
            blocks = []
            if samp:
                for b in range(16):
                    blocks.append((kcT[:, b, :], "kcT", Vc[:, b, :, :], "Vc", (colmask[:, b, :], masks[:, 3, :])))
                blocks.append((kaT[sl][:], ("kaT", sl), Vext[sl][:], ("Vext", sl), masks[:, 2, :]))
            else:
                if T > 0:
                    blocks.append((kaT[1 - sl][:], ("kaT", 1 - sl), Vext[1 - sl][:], ("Vext", 1 - sl), masks[:, 1, :]))
                blocks.append((kaT[sl][:], ("kaT", sl), Vext[sl][:], ("Vext", sl), masks[:, 0, :]))
            n = 0
            for kvh in range(2):
                ob = BA.get()
                for bi, (kT, kTt, Vx, Vt, mk) in enumerate(blocks):
                    sbk = BA.get()
                    pi = n % 2
                    n += 1
                    P.add("pe", lambda e, kT=kT, kvh=kvh, sbk=sbk: e.matmul(bank(sbk), lhsT=kT[kvh * 64:(kvh + 1) * 64, :], rhs=qaT[kvh * 64:(kvh + 1) * 64, :], start=True, stop=True),
                          [kTt, "qaT"], [BK(sbk)])
                    P.add("act", lambda e, sbk=sbk, pi=pi: e.activation(out=S.pT[pi][:], in_=bank(sbk), func=AF.Exp, scale=0.125), [BK(sbk)], [S.T("pT", pi)], cost=0.65)
                    BA.put(sbk)
                    for mk1 in (mk if isinstance(mk, tuple) else (mk,)):
                        P.ew(tt(v3(S.pT[pi][:], 128), v3(S.pT[pi][:], 128), bc_h(mk1, 4), mult), [S.T("pT", pi), "masks", "colmask"], [S.T("pT", pi)])

                    def pv(e, pi=pi, Vx=Vx, kvh=kvh, ob=ob, bi=bi, nb=len(blocks)):
                        for g in range(4):
                            ins = e.matmul(bank(ob)[:, g * 65:(g + 1) * 65], lhsT=S.pT[pi][:, g * 128:(g + 1) * 128], rhs=Vx[:, kvh, :],
                                           start=(bi == 0 and g == 0), stop=(bi == nb - 1 and g == 3), skip_group_check=True)
                        return ins
                    P.add("pe", pv, [S.T("pT", pi), Vt], [BK(ob)])
                o3 = bank(ob)[:, 0:260].rearrange("p (g d) -> p g d", d=65)
                P.add("dve", tt(S.den8[:, 0:4], o3[:, :, 64], esink[:, kvh * 4:(kvh + 1) * 4], add), [BK(ob), "esink"], [S.T("den8")])
                P.add("dve", lambda e: e.reciprocal(out=S.den8[:, 0:4], in_=S.den8[:, 0:4]), [S.T("den8")], [S.T("den8")])
                P.add("dve", tt(v3(S.mix_bf[:, kvh * 256:(kvh + 1) * 256], 64), o3[:, :, 0:64], bc_d(S.den8[:, 0:4], 64), mult), [BK(ob), S.T("den8")], [S.T("mix", kvh)])
                BA.put(ob)
            ckpt(5)
            q_lo, k_lo = S.qm_bf, S.om_bf
            QLO, KLO = [S.T("qm_bf")], [S.T("om_bf", 0), S.T("om_bf", 1)]
            hilo = (T == 0)
            A = rot_pair(S, v3(proj[:, 512:1024], 64), tb[:, 2, :], tb[:, 3, :], [("proj", 1), TB])
            if hilo:
                P.ew(tt(A, A, bc_d(dec[:, dv, 0, :], 64), mult), [S.T("tmpA"), "dec"], [S.T("tmpA")])
                P.ew(cpy(qr_bf[:], S.tmpA[:]), [S.T("tmpA")], ["qr_bf"])
                P.ew(tt(q_lo[:], S.tmpA[:], qr_bf[:], ALU.subtract), [S.T("tmpA"), "qr_bf"], QLO)
            else:
                P.ew(tt(v3(qr_bf[:], 64), A, bc_d(dec[:, dv, 0, :], 64), mult), [S.T("tmpA"), "dec"], ["qr_bf"])
            A = rot_pair(S, v3(proj[:, 1024:1536], 64), tb[:, 2, :], tb[:, 3, :], [("proj", 2), TB])
            if hilo:
                P.ew(tt(A, A, bc_d(dec[:, dv, 1, :], 64), mult), [S.T("tmpA"), "dec"], [S.T("tmpA")])
                P.ew(cpy(kr_bf[:], S.tmpA[:]), [S.T("tmpA")], ["kr_bf"])
                P.ew(tt(k_lo[:], S.tmpA[:], kr_bf[:], ALU.subtract), [S.T("tmpA"), "kr_bf"], KLO)
            else:
                P.ew(tt(v3(kr_bf[:], 64), A, bc_d(dec[:, dv, 1, :], 64), mult), [S.T("tmpA"), "dec"], ["kr_bf"])
            P.ew(cpy(vr_bf[:], proj[:, 1536:2048]), [("proj", 3)], ["vr_bf"])
            transposes([qr_bf[:, c * 128:(c + 1) * 128] for c in range(4)] + [kr_bf[:, c * 128:(c + 1) * 128] for c in range(4)],
                       ["qr_bf", "kr_bf"], qkT[:].rearrange("p a t -> p (a t)"), "qkT", 8)
            if hilo:
                transposes([q_lo[:, c * 128:(c + 1) * 128] for c in range(4)] + [k_lo[:, c * 128:(c + 1) * 128] for c in range(4)],
                           QLO + KLO, innT[:].rearrange("p a t -> p (a t)"), [("innT", 0), ("innT", 1)], 8)

            bi0, bi1 = BA.get(), BA.get()
            bis = (bi0, bi1)

            def inn(e):
                for c in range(4):
                    for hl in range(2):
                        pr = slice(hl * 64, (hl + 1) * 64)
                        o_ = bank(bis[hl])[:, c * 128:(c + 1) * 128]
                        ins = e.matmul(o_, lhsT=qkT[pr, 4 + c, :], rhs=qkT[pr, c, :], start=(c == 0), stop=(c == 3 and not hilo), skip_group_check=True)
                        if hilo:
                            e.matmul(o_, lhsT=qkT[pr, 4 + c, :], rhs=innT[pr, c, :], start=False, stop=False, skip_group_check=True)
                            ins = e.matmul(o_, lhsT=innT[pr, 4 + c, :], rhs=qkT[pr, c, :], start=False, stop=(c == 3), skip_group_check=True)
                return ins
            P.add("pe", inn, ["qkT"] + ([("innT", 0), ("innT", 1)] if hilo else []), [BK(bi0), BK(bi1)], cost=1.6 if hilo else 0.6)
            mk = masks[:, 2 if samp else 0, :]
            for hl in range(2):
                P.add("dve", tt(innT[:, hl * 4:(hl + 1) * 4, :], v3(bank(bis[hl]), 128), bc_h(mk, 4), mult), [BK(bis[hl]), "masks"], [("innT", hl)], cost=0.65)
            BA.put(bi0, bi1)
            b7 = BA.get()

            def orm(e):
                for h in range(8):
                    c, hl = h // 2, h % 2
                    ins = e.matmul(bank(b7)[:, h * 64:(h + 1) * 64], lhsT=innT[:, hl * 4 + c, :], rhs=vr_bf[:, h * 64:(h + 1) * 64],
                                   start=(h == 0), stop=False, skip_group_check=True)
                return ins
            P.add("pe", orm, [("innT", 0), ("innT", 1), "vr_bf"], [BK(b7)], cost=0.5)
            if not samp:
                cur = T % 2

                def crs(e):
                    for c in range(4):
                        ins = e.matmul(bank(b7)[:, c * 128:(c + 1) * 128], lhsT=qkT[:, c, :], rhs=S_bd[cur][:, c, :], start=False, stop=(c == 3), skip_group_check=True)
                    return ins
                P.add("pe", crs, ["qkT", ("S_bd", cur)], [BK(b7)], cost=0.3)
                bkv = BA.get()
                def kvm(e):
                    for c in range(4):
                        ins = e.matmul(bank(bkv)[:, c * 128:(c + 1) * 128], lhsT=kr_bf[:, c * 128:(c + 1) * 128], rhs=vr_bf[:, c * 128:(c + 1) * 128],
                                       start=(c == 0), stop=(c == 3), skip_group_check=True)
                    return ins
                P.add("pe", kvm, ["kr_bf", "vr_bf"], [BK(bkv)], cost=0.3)
                kv4 = v3(bank(bkv), 128)
                P.add("dve", tt(S_f[0:64], kv4[0:64, :, 0:64], S_f[0:64], add), [BK(bkv), "S_f"], ["S_f"], cost=0.3)
                P.add("dve", tt(S_f[64:128], kv4[64:128, :, 64:128], S_f[64:128], add), [BK(bkv), "S_f"], ["S_f"], cost=0.3)
                BA.put(bkv)
                P.ew(tt(S_f[:], S_f[:], bc_d(cdec[:, 0, :], 64), mult), ["S_f", "cdec"], ["S_f"], cost=0.5)
                nx = 1 - cur
                P.ew(cpy(S_bd[nx][0:64, :, 0:64], S_f[0:64]), ["S_f", ("S_bd", nx)], [("S_bd", nx)], cost=0.25)
                P.ew(cpy(S_bd[nx][64:128, :, 64:128], S_f[64:128]), ["S_f", ("S_bd", nx)], [("S_bd", nx)], cost=0.25)
                if T == NT - 2:
                    dma("sp", o_ret_p.rearrange("(c hl) d e -> (hl d) c e", hl=2), S_f[:], ["S_f"], [], "orp", is_out=True)
            else:
                for b in range(16):
                    s2 = b % 2
                    S0 = stg[s2][:, 0:256].rearrange("p (c e) -> p c e", e=64)
                    dma("sp", S0, s_ret[b].rearrange("(c hl) d e -> (hl d) c e", hl=2), [], [("stg", s2)], "sg%d" % s2)
                    P.ew(cpy(S_bd[s2][0:64, :, 0:64], S0[0:64]), [("stg", s2), ("S_bd", s2)], [("S_bd", s2)], cost=0.25)
                    P.ew(cpy(S_bd[s2][64:128, :, 64:128], S0[64:128]), [("stg", s2), ("S_bd", s2)], [("S_bd", s2)], cost=0.25)
                    padb = (qa_bf, S.qm_bf)
                    padt = ("qa_bf", S.T("qm_bf"))[s2]
                    P.ew(tt(v3(padb[s2][:], 128), qkT[:, 0:4, :], bc_h(colmask[:, b, :], 4), mult), ["qkT", "colmask"], [padt])

                    def crs(e, b=b, s2=s2, padb=padb):
                        for c in range(4):
                            ins = e.matmul(bank(b7)[:, c * 128:(c + 1) * 128], lhsT=padb[s2][:, c * 128:(c + 1) * 128], rhs=S_bd[s2][:, c, :],
                                           start=False, stop=(b == 15 and c == 3), skip_group_check=True)
                        return ins
                    P.add("pe", crs, [padt, ("S_bd", s2)], [BK(b7)])
                    P.ew(lambda e, b=b, s2=s2: e.tensor_scalar(out=S.pT[s2][:], in0=kr_bf[:], scalar1=rowmask[:, b:b + 1], scalar2=None, op0=mult),
                         ["kr_bf", "rowmask"], [S.T("pT", s2)])
                    kb = BA.get()

                    def kvm(e, s2=s2, kb=kb):
                        for c in range(4):
                            ins = e.matmul(bank(kb)[:, c * 128:(c + 1) * 128], lhsT=S.pT[s2][:, c * 128:(c + 1) * 128], rhs=vr_bf[:, c * 128:(c + 1) * 128],
                                           start=(c == 0), stop=(c == 3), skip_group_check=True)
                        return ins
                    P.add("pe", kvm, [S.T("pT", s2), "vr_bf"], [BK(kb)])
                    kv4 = v3(bank(kb), 128)
                    So = S.tmpC[:, s2 * 256:(s2 + 1) * 256].rearrange("p (c e) -> p c e", e=64)
                    SoT = S.T("tmpC2", s2)
                    P.add("dve", tt(So[0:64], kv4[0:64, :, 0:64], S0[0:64], add), [BK(kb), ("stg", s2), S.T("tmpC")], [SoT])
                    P.add("dve", tt(So[64:128], kv4[64:128, :, 64:128], S0[64:128], add), [BK(kb), ("stg", s2), SoT], [SoT])
                    BA.put(kb)
                    P.ew(tt(So, So, bc_d(cdec[:, 1, :], 64), mult), [SoT, "cdec"], [SoT], cost=0.5)
                    dma("sp", o_ret_s[b].rearrange("(c hl) d e -> (hl d) c e", hl=2), So, [SoT], [], "ors%d" % s2, is_out=True)
            P.add("act", lambda e: e.activation(out=S.tmpA[:], in_=bank(b7), func=AF.Square), [BK(b7)], [S.T("tmpA")], cost=0.65)
            P.add("dve", lambda e: e.tensor_reduce(out=S.st8[:], in_=v3(S.tmpA[:], 64), axis=AX.X, op=add), [S.T("tmpA")], [S.T("st8")])
            rstd_chain(S, None, S.st8[:], S.rh8[:], 8, 1.0 / 64, [S.T("st8")], [S.T("rh8")])
            P.add("dve", tt(v3(S.tmpB[:], 64), v3(bank(b7), 64), bc_d(S.rh8[:], 64), mult), [BK(b7), S.T("rh8")], [S.T("tmpB", 0), S.T("tmpB", 1)], cost=0.65)
            BA.put(b7)
            P.add("act", lambda e: e.activation(out=S.tmpC[:], in_=proj[:, 2048:2560], func=AF.Silu), [("proj", 4)], [S.T("tmpC"), S.T("tmpC2", 0), S.T("tmpC2", 1)])
            P.ew(tt(S.mix_bf[:, 512:1024], S.tmpB[:], S.tmpC[:], mult), [S.T("tmpB", 0), S.T("tmpB", 1), S.T("tmpC")], [S.T("mix", 2)])
            ckpt(6)
            transposes([S.mix_bf[:, k * 128:(k + 1) * 128] for k in range(8)], [S.T("mix", 0), S.T("mix", 1), S.T("mix", 2)], S.xnT[:].rearrange("p k t -> p (k t)"), S.XN, 8)

            for hf in range(2):
                by = BA.get()

                def mmo(e, hf=hf, by=by):
                    for k in range(8):
                        ins = e.matmul(bank(by), lhsT=S.xnT[:, k, :], rhs=W_out[:, k, hf * 512:(hf + 1) * 512], start=(k == 0), stop=(k == 7))
                    return ins
                P.add("pe", mmo, S.XN + ["W_out"], [BK(by)], cost=1.75)
                P.add("dve", tt(X[:, hf * 512:(hf + 1) * 512], X[:, hf * 512:(hf + 1) * 512], bank(by), add), [XT, BK(by)], [XT], cost=0.65)
                BA.put(by)

        def tile_cross(T, i, xi):
            samp = (T == NT - 1)
            S = SETS[i % 2]
            X = xres[xi]; XT = ("xres", xi)
            norm_T(S, X[:], XT, 1, S.xnT, S.T("xnT"), defer=True)

            bq_ = BA.get()

            def mmq(e):
                for k in range(8):
                    ins = e.matmul(bank(bq_), lhsT=S.xnT[:, k, :], rhs=W_mq[:, k, :], start=(k == 0), stop=(k == 7))
                return ins
            P.add("pe", mmq, S.XN + ["W_mq"], [BK(bq_)], cost=1.75)
            P.add("act", lambda e: e.activation(out=S.tmpA[:], in_=bank(bq_), func=AF.Copy, scale=S.rs[:]), [BK(bq_), S.T("rs")], [S.T("tmpA")], cost=0.75)
            BA.put(bq_)
            A3 = v3(S.tmpA[:], 128)
            head_rstd(S, A3, 4, 128, [S.T("tmpA")], S.tmpC, S.T("tmpC"))
            P.ew(tt(A3, A3, bc_d(S.rh8[:, 0:4], 128), mult), [S.T("tmpA"), S.T("rh8")], [S.T("tmpA")])
            P.ew(tt(v3(S.qm_bf[:], 128), A3, bc_h(gqm[:], 4), mult), [S.T("tmpA"), "gqm"], [S.T("qm_bf")])
            transposes([S.qm_bf[:, h * 128:(h + 1) * 128] for h in range(4)], [S.T("qm_bf")], S.qmT[:], S.T("qmT"), 4)
            SC = 128.0 ** -0.5
            nb = 16 if samp else 1
            bo = (BA.get(), BA.get())
            for b in range(nb):
                if samp:
                    s2 = b % 2
                    kb = stg_bf[:] if s2 == 0 else stg[0][:].bitcast(BF16)[:, 0:1024]
                    kb_tok = "stg_bf" if s2 == 0 else ("stg", 0)
                    vb = Vmb[:] if s2 == 0 else stg[1][:].bitcast(BF16)[:, 0:1032].rearrange("p (mc h d) -> p mc h d", mc=2, h=4)
                    vb_tok = "Vmb" if s2 == 0 else ("stg", 1)
                    if b == 1:
                        P.add("pool", lambda e, vb=vb: e.memset(vb, 1.0), [], [vb_tok])
                    dma("pool", kb.rearrange("p (mc f) -> p mc f", mc=2), c_mk[b].rearrange("(mc m) f -> m mc f", m=128), [], [kb_tok], "pk%d" % s2)
                    for mc in range(2):
                        dma("pool", vb[:, mc, :, 0:128], c_mv[b, mc * 128:(mc + 1) * 128, :].rearrange("m (h d) -> m h d", h=4), [vb_tok], [vb_tok], "pv%d" % s2)
                    bt = BAS[1].get()

                    def trk(e, bt=bt, kb=kb):
                        for h in range(4):
                            for mc in range(2):
                                s8 = h * 2 + mc
                                ins = e.transpose(out=bankbf(bt)[:, s8 * 128:(s8 + 1) * 128], in_=kb[:, mc * 512 + h * 128: mc * 512 + (h + 1) * 128], identity=ident[:])
                        return ins
                    P.add("pe", trk, [kb_tok, "ident"], [BK(bt)], cost=0.7)
                    kmb = kmTb[:] if s2 == 0 else qkT[:].rearrange("p a t -> p (a t)")
                    kmb_tok = "kmTb" if s2 == 0 else "qkT"
                    P.add("act", lambda e, bt=bt, kmb=kmb: e.activation(out=kmb, in_=bankbf(bt), func=AF.Copy), [BK(bt)], [kmb_tok], cost=1.1)
                    BA.put(bt)
                    kt_tok, v_tok = kmb_tok, vb_tok
                    kt_of = lambda h, mc, kmb=kmb: kmb[:, (h * 2 + mc) * 128:(h * 2 + mc + 1) * 128]
                    v_of = lambda h, mc, vb=vb: vb[:, mc, h, :]
                else:
                    kt_tok, v_tok = "kmT", "Vm"
                    kt_of = lambda h, mc: kmT[:, h, mc * 128:(mc + 1) * 128]
                    v_of = lambda h, mc: Vm[:, mc, h, :]

                bs = (BAS[1].get(), BAS[1].get()) if samp else (BA.get(), BA.get())
                if samp and b % 2 == 1:
                    pm = (S.pT[0][:], S.pT[1][:]); pm_tok = [S.T("pT", 0), S.T("pT", 1)]
                else:
                    pm = (S.pmT[:, 0:512], S.pmT[:, 512:1024]); pm_tok = [S.T("pmT", 0), S.T("pmT", 1)]

                def scm(e, kt_of=kt_of, bs=bs):
                    for h in range(4):
                        for mc in range(2):
                            s8 = h * 2 + mc
                            ins = e.matmul(bank(bs[s8 // 4])[:, (s8 % 4) * 128:(s8 % 4 + 1) * 128], lhsT=kt_of(h, mc), rhs=S.qmT[:, h * 128:(h + 1) * 128],
                                           start=(s8 % 4 == 0), stop=(s8 % 4 == 3), skip_group_check=True)
                    return ins
                P.add("pe", scm, [kt_tok, S.T("qmT")], [BK(bs[0]), BK(bs[1])], cost=0.7)
                for hb in range(2):
                    P.add("act", lambda e, hb=hb, bs=bs, pm=pm: e.activation(out=pm[hb], in_=bank(bs[hb]), func=AF.Exp, scale=SC), [BK(bs[hb])], [pm_tok[hb]], cost=0.65)
                BA.put(*bs)
                if samp:
                    for hb in range(2):
                        P.ew(tt(v3(pm[hb], 128), v3(pm[hb], 128), bc_h(colmask[:, b, :], 4), mult), [pm_tok[hb], "colmask"], [pm_tok[hb]], cost=1)

                def pvm(e, b=b, v_of=v_of, pm=pm):
                    for h in range(4):
                        for mc in range(2):
                            s8 = h * 2 + mc
                            ins = e.matmul(bank(bo[h // 2])[:, (h % 2) * 256:(h % 2) * 256 + 129], lhsT=pm[s8 // 4][:, (s8 % 4) * 128:(s8 % 4 + 1) * 128], rhs=v_of(h, mc),
                                           start=(b == 0 and h % 2 == 0 and mc == 0), stop=(b == nb - 1 and h % 2 == 1 and mc == 1), skip_group_check=True)
                    return ins
                P.add("pe", pvm, pm_tok + [v_tok], [BK(bo[0]), BK(bo[1])], cost=0.8)
            for hh in range(2):
                om2 = bank(bo[hh]).rearrange("p (h d) -> p h d", d=256)
                P.add("dve", lambda e, hh=hh, om2=om2: e.reciprocal(out=S.den8[:, 4 + 2 * hh:6 + 2 * hh], in_=om2[:, :, 128]), [BK(bo[hh])], [S.T("den8b", hh)], cost=0.15)
                P.add("dve", tt(v3(S.om_bf[:, hh * 256:(hh + 1) * 256], 128), om2[:, :, 0:128], bc_d(S.den8[:, 4 + 2 * hh:6 + 2 * hh], 128), mult),
                      [BK(bo[hh]), S.T("den8b", hh)], [S.T("om_bf", hh)], cost=0.4)
            BA.put(*bo)
            transposes([S.om_bf[:, h * 128:(h + 1) * 128] for h in range(4)], [S.T("om_bf", 0), S.T("om_bf", 1)], S.omT[:], S.T("omT"), 4)
            for hf in range(2):
                by = BA.get()

                def mmo(e, hf=hf, by=by):
                    for c in range(4):
                        ins = e.matmul(bank(by), lhsT=S.omT[:, c * 128:(c + 1) * 128], rhs=W_mo[:, c, hf * 512:(hf + 1) * 512], start=(c == 0), stop=(c == 3))
                    return ins
                P.add("pe", mmo, [S.T("omT"), "W_mo"], [BK(by)], cost=0.9)
                P.add("dve", tt(X[:, hf * 512:(hf + 1) * 512], X[:, hf * 512:(hf + 1) * 512], bank(by), add), [XT, BK(by)], [XT], cost=0.65)
                BA.put(by)
            norm_T(S, X[:], XT, 2, xn3T[:, :, i * 128:(i + 1) * 128], ("xn3T", i))

        def ffn_group(g):
            tiles = GROUPS[g]
            nt_ = len(tiles)
            xo = (g % 2) * GRP
            XR = [("xn3T", i, h) for i in range(nt_) for h in range(2)]
            NTOK = nt_ * 128
            if (NT - 1) in tiles:
                ffn_load_d(2)
            if (NT - 1) in tiles or 0 in tiles:
                for j_ in range(3, NWG):
                    ffn_load(j_)
            yb = []
            for i in range(nt_):
                yb += [BAS[i % 2].get(), BAS[i % 2].get()]
            for j in range(NJ):
                s = j % NWG
                hs = j % 2
                bg, bu = BAS[0].get(), BAS[1].get()

                def gmm(e, s=s, bg=bg):
                    for k in range(8):
                        ins = e.matmul(bank(bg, NTOK), lhsT=WGV[s][:, k, 0:128], rhs=xn3T[:, k, 0:NTOK], start=(k == 0), stop=(k == 7))
                    return ins

                def umm(e, s=s, bu=bu):
                    for k in range(8):
                        ins = e.matmul(bank(bu, NTOK), lhsT=WGV[s][:, k, 128:256], rhs=xn3T[:, k, 0:NTOK], start=(k == 0), stop=(k == 7))
                    return ins
                P.add("pe", gmm, XR + [WGT[s]], [BK(bg)], cost=0.05 + 8 * max(NTOK, 128) / 2400.0)
                P.add("pe", umm, XR + [WGT[s]], [BK(bu)], cost=0.05 + 8 * max(NTOK, 128) / 2400.0)
                P.add("act", lambda e, hs=hs, bg=bg: e.activation(out=sgb[hs][:, 0:NTOK], in_=bank(bg, NTOK), func=AF.Silu), [BK(bg)], [("sgb", hs)], cost=0.45)
                P.add("dve", tt(hT[hs][:, 0:NTOK], sgb[hs][:, 0:NTOK], bank(bu, NTOK), mult), [("sgb", hs), BK(bu)], [("hT", hs)], cost=0.45)
                BA.put(bg, bu)

                def dn(e, s=s, hs=hs, j=j):
                    for i in range(nt_):
                        for hf in range(2):
                            ins = e.matmul(bank(yb[2 * i + hf]), lhsT=hT[hs][:, i * 128:(i + 1) * 128], rhs=wdb[j % 3][:, hf * 512:(hf + 1) * 512],
                                           start=(j == 0), stop=(j == NJ - 1))
                    return ins
                P.add("pe", dn, [("hT", hs), WDT[j % 3]], [BK(b_) for b_ in yb], cost=0.05 + 2 * nt_ * 0.22)
                if j + NWG < NJ:
                    ffn_load(j + NWG)
                if j + 3 < NJ:
                    ffn_load_d(j + 3)
            for i, T in enumerate(tiles):
                xi = xo + i
                for hf in range(2):
                    P.add("dve", tt(xres[xi][:, hf * 512:(hf + 1) * 512], xres[xi][:, hf * 512:(hf + 1) * 512], bank(yb[2 * i + hf]), add),
                          [("xres", xi), BK(yb[2 * i + hf])], [("xres", xi)], cost=0.65)
                dst = y_s if T == NT - 1 else y_p[T * 128:(T + 1) * 128, :]
                dma("sp", dst, xres[xi][:], [("xres", xi)], [], "yo%d" % xi, is_out=True)
            BA.put(*yb)

        def main_loop():
          ckpt(3)
          for g in range(NG):
              xo = (g % 2) * GRP
              for i, T in enumerate(GROUPS[g]):
                  src = x_s if T == NT - 1 else x_p[T * 128:(T + 1) * 128, :]
                  dma("sp", xres[xo + i][:], src, [], [("xres", xo + i)], "xl%d" % (xo + i))
              has_samp = (NT - 1) in GROUPS[g]
              uses_stg = has_samp or (0 in GROUPS[g])
              for j in range(3 if uses_stg else NWG):
                  ffn_load(j)
              for j in range(2 if has_samp else 3):
                  ffn_load_d(j)
              for i, T in enumerate(GROUPS[g]):
                  if T == NT - 1:
                      sample_preload()
                  tile_mixer(T, i, xo + i)
                  ckpt(7)
                  tile_cross(T, i, xo + i)
                  ckpt(8)
              ckpt(9)
              ffn_group(g)
              ckpt(10 + g)
        try:
            main_loop()
        except _Stop:
            pass
        P.emit(nc, st)
    return nc


_CACHE = {}


def kernel(x_prompt, x_sample, mem_prompt, cache_swa_k, cache_swa_v, state_ret, cache_mem_k, cache_mem_v,
           norm_mix, w_in, q_norm_a, k_norm_a, sinks, w_out, norm_cross, norm_mem, w_mq, w_mkv,
           q_norm_m, k_norm_m, w_mo, norm_ffn, w_gu, w_down):
    f = lambda a: np.ascontiguousarray(np.asarray(a, dtype=np.float32))
    if "nc" not in _CACHE:
        _CACHE["nc"] = build_program()
        _CACHE["consts"] = host_consts()
    nc = _CACHE["nc"]
    consts = _CACHE["consts"]
    shared = {
        "w_in": f(w_in)[0], "w_out": f(w_out)[0], "w_mq": f(w_mq)[0], "w_mkv": f(w_mkv)[0], "w_mo": f(w_mo)[0],
        "w_gu": f(w_gu)[0], "w_down": f(w_down)[0],
        "norm_mix": f(norm_mix), "norm_cross": f(norm_cross), "norm_mem": f(norm_mem), "norm_ffn": f(norm_ffn),
        "q_norm_a": f(q_norm_a), "k_norm_a": f(k_norm_a), "sinks": f(sinks), "q_norm_m": f(q_norm_m), "k_norm_m": f(k_norm_m),
    }
    shared.update(consts)
    xp, xs, mp = f(x_prompt), f(x_sample), f(mem_prompt)
    ck, cv, sr, cmk, cmv = f(cache_swa_k)[0], f(cache_swa_v)[0], f(state_ret)[0], f(cache_mem_k)[0], f(cache_mem_v)[0]
    in_maps = []
    for c in range(NCORES):
        sl = slice(16 * c, 16 * (c + 1))
        m = dict(shared)
        m["x_p"] = xp[c]
        m["x_s"] = np.ascontiguousarray(xs[sl].reshape(128, 1024))
        m["mem_p"] = mp[c]
        m["c_k"] = np.ascontiguousarray(ck[sl].reshape(16, 128, 128))
        m["c_v"] = np.ascontiguousarray(cv[sl].reshape(16, 128, 128))
        m["s_ret"] = np.ascontiguousarray(sr[sl])
        m["c_mk"] = np.ascontiguousarray(cmk[sl].reshape(16, 256, 512))
        m["c_mv"] = np.ascontiguousarray(cmv[sl].reshape(16, 256, 512))
        in_maps.append(m)
    res = run_bass_kernel_spmd(nc, in_maps, core_ids=list(range(NCORES)))
    R = res.results
    cat = lambda k: np.stack([np.asarray(r[k], dtype=np.float32) for r in R], 0)
    y_prompt = cat("y_p").reshape(8, 4096, 1024)
    y_sample = cat("y_s").reshape(128, 8, 1024)
    swk_p = cat("o_swk_p").reshape(1, 8, 128, 2, 64)
    swv_p = cat("o_swv_p").reshape(1, 8, 128, 2, 64)
    ret_p = cat("o_ret_p").reshape(1, 8, 8, 64, 64)
    mk_p = cat("o_mk_p").reshape(1, 8, 256, 4, 128)
    mv_p = cat("o_mv_p").reshape(1, 8, 256, 4, 128)
    swk_s = cat("o_swk_s").reshape(1, 128, 128, 2, 64)
    swv_s = cat("o_swv_s").reshape(1, 128, 128, 2, 64)
    ret_s = cat("o_ret_s").reshape(1, 128, 8, 64, 64)
    return (y_prompt, y_sample, swk_p, swv_p, ret_p, mk_p, mv_p, swk_s, swv_s, ret_s)
```
